# Optimizing a Trainium2 kernel written in Bass

```python
import math
import jax, jax.numpy as jnp
from jax import lax
import numpy as np

D_MODEL = 1024
BATCH = 2
SEQ = 8192
DEPTH = 1

N_MEM = 256
EPS = 1e-6
MLA_HEADS = 4
QK_NOPE = 128
QK_ROPE = 64
QK_HEAD = QK_NOPE + QK_ROPE
V_HEAD = 128
Q_LORA = 384
KV_LORA = 256
ROPE_THETA = 10000.0
Q_BLOCK = 128
HG_HEADS = 4
HG_DK = 128
HG_DV = 128
HG_CHUNK = 64
MEM_HEADS = 4
MEM_HEAD_DIM = 128
MLA_WIDTH = MLA_HEADS * V_HEAD
HG_KWIDTH = HG_HEADS * HG_DK
HG_WIDTH = HG_HEADS * HG_DV
MEM_WIDTH = MEM_HEADS * MEM_HEAD_DIM
MIX_WIDTH = MLA_WIDTH + HG_WIDTH + MEM_WIDTH
D_FF = -(-8 * D_MODEL // (3 * 256)) * 256
IN_SIZES = (Q_LORA, KV_LORA, QK_ROPE, HG_KWIDTH, HG_KWIDTH, HG_WIDTH, HG_WIDTH, MEM_WIDTH)
IN_WIDTH = Q_LORA + KV_LORA + QK_ROPE + 2 * HG_KWIDTH + 2 * HG_WIDTH + MEM_WIDTH

kernel_name = 'hymba_mla_hgrn2_memxattn_swiglu'


def rmsnorm(x, g):
    xf = x.astype(jnp.float32)
    y = xf * lax.rsqrt(jnp.mean(xf * xf, axis=-1, keepdims=True) + EPS) * g.astype(jnp.float32)
    return y.astype(x.dtype)


def apply_rope(x, pos):
    half = QK_ROPE // 2
    inv_freq = jnp.power(ROPE_THETA, -jnp.arange(half, dtype=jnp.float32) / half)
    ang = pos.astype(jnp.float32)[:, :, None, None] * inv_freq
    cos, sin = jnp.cos(ang), jnp.sin(ang)
    xf = x.astype(jnp.float32)
    x1, x2 = xf[..., :half], xf[..., half:]
    return jnp.concatenate([x1 * cos - x2 * sin, x2 * cos + x1 * sin], axis=-1).astype(x.dtype)


def causal_block_attention(q, k, v):
    B, S, H, Dqk = q.shape
    Dv = v.shape[-1]
    nq = S // Q_BLOCK
    scale = Dqk ** -0.5
    qb = q.astype(jnp.float32).reshape(B, nq, Q_BLOCK, H, Dqk).transpose(1, 0, 2, 3, 4)
    starts = jnp.arange(nq, dtype=jnp.int32) * Q_BLOCK
    kf = k.astype(jnp.float32)
    vf = v.astype(jnp.float32)
    kpos = jnp.arange(S, dtype=jnp.int32)

    def one_block(args):
        qblk, start = args
        s = jnp.einsum('bqhd,bkhd->bhqk', qblk, kf) * scale
        qpos = start + jnp.arange(Q_BLOCK, dtype=jnp.int32)
        mask = kpos[None, :] <= qpos[:, None]
        s = jnp.where(mask[None, None], s, -jnp.inf)
        p = jax.nn.softmax(s, axis=-1)
        return jnp.einsum('bhqk,bkhd->bqhd', p, vf)

    out = lax.map(one_block, (qb, starts))
    return out.transpose(1, 0, 2, 3, 4).reshape(B, S, H, Dv).astype(v.dtype)


def mla_group(c_q, c_kv, k_rope, pos, q_a_norm, w_uq, kv_a_norm, w_ukv, q_norm, k_norm):
    B, S, _ = c_q.shape
    q = (rmsnorm(c_q, q_a_norm) @ w_uq).reshape(B, S, MLA_HEADS, QK_HEAD)
    kv = (rmsnorm(c_kv, kv_a_norm) @ w_ukv).reshape(B, S, MLA_HEADS, QK_NOPE + V_HEAD)
    k_nope, v = kv[..., :QK_NOPE], kv[..., QK_NOPE:]
    k_pe = jnp.broadcast_to(k_rope[:, :, None, :], (B, S, MLA_HEADS, QK_ROPE))
    k = jnp.concatenate([k_nope, k_pe], axis=-1)
    q = rmsnorm(q, q_norm)
    k = rmsnorm(k, k_norm)
    q = jnp.concatenate([q[..., :QK_NOPE], apply_rope(q[..., QK_NOPE:], pos)], axis=-1)
    k = jnp.concatenate([k[..., :QK_NOPE], apply_rope(k[..., QK_NOPE:], pos)], axis=-1)
    o = causal_block_attention(q, k, v)
    return o.reshape(B, S, MLA_WIDTH)


def hgrn2_group(q_raw, f_raw, i_raw, g_raw, lb, out_gain):
    B, S, _ = q_raw.shape
    n = S // HG_CHUNK
    f32 = jnp.float32
    q = jax.nn.silu(q_raw.astype(f32)) * (HG_DK ** -0.5)
    lbf = lb.astype(f32)
    f = lbf + (1.0 - lbf) * jax.nn.sigmoid(f_raw.astype(f32))
    k = 1.0 - f
    logf = jnp.log(f)
    v = i_raw.astype(f32)

    def chunks(t, d):
        return t.reshape(B, n, HG_CHUNK, HG_HEADS, d).transpose(1, 0, 3, 2, 4)

    qc, kc, vc = chunks(q, HG_DK), chunks(k, HG_DK), chunks(v, HG_DV)
    bc = jnp.cumsum(chunks(logf, HG_DK), axis=3)
    causal = jnp.tril(jnp.ones((HG_CHUNK, HG_CHUNK), dtype=bool))

    def chunk_step(state, xs):
        qx, kx, vx, bx = xs
        diff = bx[:, :, :, None, :] - bx[:, :, None, :, :]
        decay = jnp.where(causal[None, None, :, :, None], jnp.exp(jnp.minimum(diff, 0.0)), 0.0)
        a = jnp.einsum('bhtd,bhsd,bhtsd->bhts', qx, kx, decay)
        o = (jnp.einsum('bhts,bhse->bhte', a, vx)
             + jnp.einsum('bhtd,bhde->bhte', qx * jnp.exp(bx), state))
        b_last = bx[:, :, -1:, :]
        new_state = (jnp.exp(b_last[:, :, 0, :])[..., None] * state
                     + jnp.einsum('bhsd,bhse->bhde', kx * jnp.exp(b_last - bx), vx))
        return new_state, o

    s0 = jnp.zeros((B, HG_HEADS, HG_DK, HG_DV), f32)
    _, o = lax.scan(chunk_step, s0, (qc, kc, vc, bc))
    o = o.transpose(1, 0, 3, 2, 4).reshape(B, S, HG_HEADS, HG_DV)
    o = rmsnorm(o, out_gain.reshape(HG_HEADS, HG_DV)).reshape(B, S, HG_WIDTH)
    return (o * jax.nn.silu(g_raw.astype(f32))).astype(q_raw.dtype)


def mem_group(q_raw, mem_h, w_mem_kv, q_norm, k_norm):
    B, S, _ = q_raw.shape
    M = mem_h.shape[1]
    q = rmsnorm(q_raw.reshape(B, S, MEM_HEADS, MEM_HEAD_DIM), q_norm)
    kv = (mem_h @ w_mem_kv).reshape(B, M, 2, MEM_HEADS, MEM_HEAD_DIM)
    k = rmsnorm(kv[:, :, 0], k_norm)
    v = kv[:, :, 1]
    s = jnp.einsum('bqhd,bkhd->bhqk', q.astype(jnp.float32), k.astype(jnp.float32)) * (MEM_HEAD_DIM ** -0.5)
    p = jax.nn.softmax(s, axis=-1)
    o = jnp.einsum('bhqk,bkhd->bqhd', p, v.astype(jnp.float32))
    return o.reshape(B, S, MEM_WIDTH).astype(q_raw.dtype)


def setup_inputs(seed: int = 0) -> dict:
    key = jax.random.key(seed)
    ks = jax.random.split(key, 32)
    f32 = jnp.float32

    def dense(k, shape, fan_in):
        return jax.random.normal(k, shape, f32) * (fan_in ** -0.5)

    def gain(k, shape):
        return 1.0 + 0.05 * jax.random.normal(k, shape, f32)

    L = DEPTH
    x = jax.random.normal(ks[0], (BATCH, SEQ, D_MODEL), f32)
    mem = jax.random.normal(ks[1], (BATCH, N_MEM, D_MODEL), f32)
    offset = jax.random.randint(ks[2], (BATCH, 1), 0, 4096, dtype=jnp.int32)
    positions = offset + jnp.arange(SEQ, dtype=jnp.int32)[None, :]
    return {
        'x': x,
        'mem': mem,
        'positions': positions,
        'norm_mix': gain(ks[3], (L, D_MODEL)),
        'norm_mem': gain(ks[4], (L, D_MODEL)),
        'w_in': dense(ks[5], (L, D_MODEL, IN_WIDTH), D_MODEL),
        'q_a_norm': gain(ks[6], (L, Q_LORA)),
        'w_uq': dense(ks[7], (L, Q_LORA, MLA_HEADS * QK_HEAD), Q_LORA),
        'kv_a_norm': gain(ks[8], (L, KV_LORA)),
        'w_ukv': dense(ks[9], (L, KV_LORA, MLA_HEADS * (QK_NOPE + V_HEAD)), KV_LORA),
        'mla_q_norm': gain(ks[10], (L, QK_HEAD)),
        'mla_k_norm': gain(ks[11], (L, QK_HEAD)),
        'hg_lb_logits': 0.1 * jax.random.normal(ks[12], (L + 1, HG_KWIDTH), f32),
        'hg_out_norm': gain(ks[13], (L, HG_WIDTH)),
        'w_mem_kv': dense(ks[14], (L, D_MODEL, 2 * MEM_WIDTH), D_MODEL),
        'mem_q_norm': gain(ks[15], (L, MEM_HEAD_DIM)),
        'mem_k_norm': gain(ks[16], (L, MEM_HEAD_DIM)),
        'mla_out_norm': gain(ks[17], (L, MLA_WIDTH)),
        'mem_out_norm': gain(ks[18], (L, MEM_WIDTH)),
        'w_out': dense(ks[19], (L, MIX_WIDTH, D_MODEL), MIX_WIDTH),
        'norm_ffn': gain(ks[20], (L, D_MODEL)),
        'w_gate': dense(ks[21], (L, D_MODEL, D_FF), D_MODEL),
        'w_up': dense(ks[22], (L, D_MODEL, D_FF), D_MODEL),
        'w_down': dense(ks[23], (L, D_FF, D_MODEL), D_FF),
    }


def reference(x, mem, positions, norm_mix, norm_mem, w_in, q_a_norm, w_uq, kv_a_norm, w_ukv,
              mla_q_norm, mla_k_norm, hg_lb_logits, hg_out_norm, w_mem_kv, mem_q_norm, mem_k_norm,
              mla_out_norm, mem_out_norm, w_out, norm_ffn, w_gate, w_up, w_down):
    lb_all = jnp.cumsum(jax.nn.softmax(hg_lb_logits.astype(jnp.float32), axis=0), axis=0)
    offsets = [0]
    for sz in IN_SIZES:
        offsets.append(offsets[-1] + sz)
    for l in range(DEPTH):
        h = rmsnorm(x, norm_mix[l])
        mem_h = rmsnorm(mem, norm_mem[l])
        proj = h @ w_in[l]
        c_q, c_kv, k_rope, hq, hf, hi, hg, mq = [proj[..., offsets[j]:offsets[j + 1]] for j in range(len(IN_SIZES))]
        y_mla = mla_group(c_q, c_kv, k_rope, positions, q_a_norm[l], w_uq[l], kv_a_norm[l], w_ukv[l],
                          mla_q_norm[l], mla_k_norm[l])
        y_hg = hgrn2_group(hq, hf, hi, hg, lb_all[l], hg_out_norm[l])
        y_mem = mem_group(mq, mem_h, w_mem_kv[l], mem_q_norm[l], mem_k_norm[l])
        mix = jnp.concatenate([rmsnorm(y_mla, mla_out_norm[l]), y_hg, rmsnorm(y_mem, mem_out_norm[l])], axis=-1)
        x = x + (mix @ w_out[l]).astype(x.dtype)
        h2 = rmsnorm(x, norm_ffn[l])
        x = x + ((jax.nn.silu(h2 @ w_gate[l]) * (h2 @ w_up[l])) @ w_down[l]).astype(x.dtype)
    return x
```

```python
import contextlib
import math
import numpy as np
import ml_dtypes
import concourse.bass as bass
import concourse.mybir as mybir
from concourse.bass_utils import run_bass_kernel_spmd

F32 = mybir.dt.float32
BF16 = mybir.dt.bfloat16
I32 = mybir.dt.int32
AF = mybir.ActivationFunctionType
ALU = mybir.AluOpType
AX = mybir.AxisListType

EPS = 1e-6
NS = 8192
NT = NS // 128
NG = NS // 512
OWN = 2048
OWN0 = NS - OWN
OT0 = OWN0 // 128
OG0 = OWN0 // 512
DFF = 2816
NF = DFF // 128

C_KV, C_KR, C_HF, C_HI, C_CQ, C_HQ, C_HG, C_MQ = 0, 256, 320, 832, 1344, 1728, 2240, 2752

V_GMIX, V_GMEM, V_GFFN, V_GQA, V_GKVA, V_GOUT, V_GHGO, V_LBL, V_VFLAG = 0, 8, 16, 24, 27, 29, 41, 45, 53
V_GQ, V_GK, V_GMQ, V_GMK, V_RMASK, V_INVF = 57, 249, 441, 569, 697, 1209
NV = 1241


class Op:
    __slots__ = ("eng", "fn", "waits", "signal", "ticket", "chan", "cidx")


class Prog:
    ENG = ["pe", "act", "dve", "pool", "sp"]

    def __init__(self):
        self.ops = {e: [] for e in self.ENG}
        self.buf = {}
        self.waited = {e: {} for e in self.ENG}
        self.chan_ops = {}

    def add(self, eng, fn, reads=(), writes=(), dma=None):
        op = Op()
        op.eng = eng
        op.fn = fn
        op.signal = dma is not None
        op.ticket = None
        op.chan = dma if dma is not None else eng
        lst = self.chan_ops.setdefault(op.chan, [])
        op.cidx = len(lst)
        lst.append(op)
        deps = {}
        for k in reads:
            b = self.buf.setdefault(k, [None, []])
            d = b[0]
            if d is not None and (d.chan not in deps or deps[d.chan].cidx < d.cidx):
                deps[d.chan] = d
            if k.startswith("ps"):
                for d in b[1]:
                    if d.chan != op.chan and (d.chan not in deps or deps[d.chan].cidx < d.cidx):
                        deps[d.chan] = d
        for k in writes:
            b = self.buf.setdefault(k, [None, []])
            for d in ([b[0]] if b[0] is not None else []) + b[1]:
                if d.chan not in deps or deps[d.chan].cidx < d.cidx:
                    deps[d.chan] = d
        op.waits = []
        w = self.waited[eng]
        for chan, d in deps.items():
            if chan == "pe" and eng == "pe":
                continue
            if w.get(chan, -1) >= d.cidx:
                continue
            w[chan] = d.cidx
            d.signal = True
            op.waits.append(d)
        for k in reads:
            self.buf[k][1].append(op)
        for k in writes:
            self.buf[k][0] = op
            self.buf[k][1] = []
        self.ops[eng].append(op)
        return op

    def barrier(self):
        chans = {c: l[-1] for c, l in self.chan_ops.items() if l}
        for e in self.ENG:
            w = self.waited[e]
            for chan, d in chans.items():
                if chan == e or w.get(chan, -1) >= d.cidx:
                    continue
                w[chan] = d.cidx
                d.signal = True
                op = Op()
                op.eng, op.fn, op.signal, op.ticket, op.chan, op.cidx = e, None, False, None, None, -1
                op.waits = [d]
                self.ops[e].append(op)

    def run(self, nc, stack):
        sems = {}
        for chan, lst in self.chan_ops.items():
            sems[chan] = stack.enter_context(nc.semaphore("s_" + chan))
            isdma = chan not in self.ENG
            cnt = 0
            for op in lst:
                if isdma:
                    cnt += 16
                    op.ticket = cnt
                elif op.signal:
                    cnt += 1
                    op.ticket = cnt
        block = stack.enter_context(nc.Block())

        def replay(eng, e):
            for op in self.ops[eng]:
                for d in op.waits:
                    e.wait_ge(sems[d.chan], d.ticket)
                if op.fn is None:
                    continue
                ins = op.fn(e)
                if op.signal:
                    ins.then_inc(sems[op.chan], 16 if op.chan not in self.ENG else 1)

        @block.tensor
        def _(e):
            replay("pe", e)

        @block.scalar
        def _(e):
            replay("act", e)

        @block.vector
        def _(e):
            replay("dve", e)

        @block.gpsimd
        def _(e):
            replay("pool", e)

        @block.sync
        def _(e):
            replay("sp", e)


class Arena:
    def __init__(self, t, ncols):
        self.t = t
        self.n = ncols
        self.off = 0

    def alloc(self, cols, dtype=BF16):
        mult = 1 if dtype == BF16 else 2
        self.off = (self.off + 1) // 2 * 2
        a = self.t[:, self.off:self.off + cols * mult]
        self.off += cols * mult
        assert self.off <= self.n, (self.off, self.n)
        return a if dtype == BF16 else a.bitcast(dtype)


class _Stop(Exception):
    pass


def build(debug=False, stop=99):
    nc = bass.Bass("TRN2", target_bir_lowering=False)

    def din(name, shape, dt=F32):
        return nc.dram_tensor(name, list(shape), dt, kind="ExternalInput").ap()

    xs = din("xs", [NS, 1024])
    posd = din("pos", [128, NT], I32)
    memd = din("mem", [256, 1024])
    vecd = din("vec", [128, NV])
    cbfd = din("cbf", [128, 512], BF16)
    w_in = din("w_in", [1024, 3264])
    w_uq = din("w_uq", [384, 768])
    w_ukv = din("w_ukv", [256, 1024])
    w_mkv = din("w_mkv", [1024, 1024])
    w_out = din("w_out", [1536, 1024])
    w_gate = din("w_gate", [1024, DFF])
    w_up = din("w_up", [1024, DFF])
    w_down = din("w_down", [DFF, 1024])
    yd = nc.dram_tensor("y", [OWN, 1024], F32, kind="ExternalOutput").ap()
    dbg = nc.dram_tensor("dbg", [128, 12 * OWN], BF16, kind="ExternalOutput").ap() if debug else None
    wgu_scr = nc.dram_tensor("wgu_scr", [NF, 128, 2048], BF16, kind="Internal").ap()
    mix_scr = nc.dram_tensor("mix_scr", [128, 8 * OWN], BF16, kind="Internal").ap()

    P = Prog()
    st = contextlib.ExitStack()
    with st:
        TOT = 106400
        arena_t = st.enter_context(nc.sbuf_tensor("arena", [128, TOT], BF16))
        A = Arena(arena_t, TOT)
        ps = st.enter_context(nc.psum_tensor("ps", [128, 4096], F32))

        def bank(i, n=512, off=0):
            return ps[:, i * 512 + off:i * 512 + off + n]

        def bankb(i, n=1024, off=0):
            return ps[:, i * 512:(i + 1) * 512].bitcast(BF16)[:, off:off + n]

        vec = A.alloc(NV, F32)
        cbf = A.alloc(512)
        ident = cbf[:, 0:128]
        onesb = cbf[:, 128:256]
        tri = cbf[:, 256:384]
        tri2 = cbf[:, 384:512]
        posi = A.alloc(NT, I32)
        posf = A.alloc(NT, F32)
        lb = A.alloc(4, F32)
        oml = A.alloc(4, F32)
        noml = A.alloc(4, F32)
        vones = A.alloc(3 * 128)
        stat = A.alloc(8 * NT, F32)
        rs1 = stat[:, 0:NT]
        rskv = stat[:, NT:2 * NT]
        sskr = stat[:, 2 * NT:3 * NT]
        rskv2 = stat[:, 3 * NT:4 * NT]
        ssmla = stat[:, 4 * NT:4 * NT + 16]
        ssmem = stat[:, 4 * NT + 16:4 * NT + 32]
        rsmla = stat[:, 4 * NT + 32:4 * NT + 48]
        rsmem = stat[:, 4 * NT + 48:4 * NT + 64]
        scr = A.alloc(64, F32)
        S32 = A.alloc(512, F32)
        Sb = A.alloc(512)
        dec = A.alloc(32, F32).rearrange("p (h c) -> p h c", h=4)
        mixT = A.alloc(12 * OWN).rearrange("p (k n) -> p k n", k=12)
        base_off = A.off
        ckvT = A.alloc(2 * NS).rearrange("p (k n) -> p k n", k=2)
        kr = A.alloc(NT * 64).rearrange("p (t d) -> p t d", d=64)
        wukv = A.alloc(2 * 1024).rearrange("p (k n) -> p k n", k=2)
        p1_off = A.off

        _dq = [0]

        def dma(out, in_, r=(), w=(), grp=None, eng=None):
            if eng is None:
                eng = "sp"
            if grp is None:
                _dq[0] += 1
                grp = "dq%d" % (_dq[0] % 12)
            P.add(eng, lambda e: e.dma_start(out=out, in_=in_), reads=r, writes=w, dma=grp)

        def act(out, in_, func, r, w, scale=1.0, bias=0.0, accum=None):
            if accum is None:
                P.add("act", lambda e: e.activation(out=out, in_=in_, func=func, scale=scale, bias=bias), reads=r, writes=w)
            else:
                P.add("act", lambda e: e.activation(out=out, in_=in_, func=func, scale=scale, bias=bias, accum_out=accum), reads=r, writes=w)

        def rsqrt_act(out, in_, n, r, w, bias=EPS):
            act(out, in_, AF.Ln, r, w, scale=1.0 / n, bias=bias)
            act(out, out, AF.Exp, w, w, scale=-0.5)

        def tt(eng, out, in0, in1, op, r, w):
            P.add(eng, lambda e: e.tensor_tensor(out=out, in0=in0, in1=in1, op=op), reads=r, writes=w)

        def ts(eng, out, in0, s1, s2, op0, op1, r, w):
            if s2 is None:
                P.add(eng, lambda e: e.tensor_scalar(out=out, in0=in0, scalar1=s1, scalar2=None, op0=op0), reads=r, writes=w)
            else:
                P.add(eng, lambda e: e.tensor_scalar(out=out, in0=in0, scalar1=s1, scalar2=s2, op0=op0, op1=op1), reads=r, writes=w)

        def stt(eng, out, in0, s, in1, op0, op1, r, w):
            P.add(eng, lambda e: e.scalar_tensor_tensor(out=out, in0=in0, scalar=s, in1=in1, op0=op0, op1=op1), reads=r, writes=w)

        def cp(eng, out, in_, r, w):
            if eng == "act":
                act(out, in_, AF.Copy, r, w)
            else:
                P.add(eng, lambda e: e.tensor_copy(out=out, in_=in_), reads=r, writes=w)

        def mm(out, lhsT, rhs, start, stop, r, w):
            P.add("pe", lambda e: e.matmul(out, lhsT=lhsT, rhs=rhs, start=start, stop=stop), reads=r, writes=w)

        def tp(out, in_, r, w, idn=None):
            idn = ident if idn is None else idn
            P.add("pe", lambda e: e.transpose(out=out, in_=in_, identity=idn), reads=list(r) + ["cbf"], writes=w)

        def sigmoid_(buf, src, r, key):
            act(buf, src, AF.Exp, r, [key], scale=-1.0)
            act(buf, buf, AF.Ln, [key], [key], bias=1.0)
            act(buf, buf, AF.Exp, [key], [key], scale=-1.0)

        def ck(level):
            if stop == level:
                raise _Stop()

        try:
            dma(vec, vecd, w=["vec"])
            dma(cbf, cbfd, w=["cbf"])
            dma(posi, posd, w=["posi"])
            cp("dve", posf, posi, ["posi"], ["posf"])
            lbl = vec[:, V_LBL:V_LBL + 8].rearrange("p (r h) -> p r h", r=2)
            tt("dve", lb, lbl[:, 0, :], lbl[:, 1, :], ALU.subtract, ["vec"], ["lb"])
            sigmoid_(lb, lb, ["lb"], "lb")
            ts("dve", oml, lb, -1.0, 1.0, ALU.mult, ALU.add, ["lb"], ["oml"])
            ts("dve", noml, oml, -1.0, None, ALU.mult, None, ["oml"], ["oml"])
            for r_ in range(3):
                ts("dve", vones[:, r_ * 128:(r_ + 1) * 128], onesb, vec[:, V_VFLAG + r_:V_VFLAG + r_ + 1], None, ALU.mult, None,
                   ["cbf", "vec"], ["vones"])
            P.add("dve", lambda e: e.memset(S32, 0.0), writes=["S32"])
            P.add("dve", lambda e: e.memset(Sb, 0.0), writes=["Sb"])
            P.add("dve", lambda e: e.memset(stat, 0.0), writes=["stat"])

            ck(1)
            NA = 2368
            win = A.alloc(8 * NA).rearrange("p (k n) -> p k n", k=8)
            wl_off = A.off
            stg = [A.alloc(1024, F32) for _ in range(6)]
            _wl = [0]

            def load_w(dst, src, ncols, scale, dkey):
                ns = len(stg)
                cw = stg[0].shape[1]
                for c0 in range(0, ncols, cw):
                    n = min(cw, ncols - c0)
                    i = _wl[0] % ns
                    _wl[0] += 1
                    dma(stg[i][:, 0:n], src[:, c0:c0 + n], w=["stg%d" % i], grp="dstg%d" % i, eng=("sp", "pool", "act")[i % 3])
                    if i % 2 == 0:
                        if scale is None:
                            cp("dve", dst[:, c0:c0 + n], stg[i][:, 0:n], ["stg%d" % i], [dkey])
                        else:
                            ts("dve", dst[:, c0:c0 + n], stg[i][:, 0:n], scale, None, ALU.mult, None, ["stg%d" % i, "vec"], [dkey])
                    else:
                        act(dst[:, c0:c0 + n], stg[i][:, 0:n], AF.Copy, ["stg%d" % i, "vec"], [dkey], scale=(1.0 if scale is None else scale))

            w_in_v = w_in.rearrange("(k p) n -> p k n", p=128)
            for k in range(8):
                load_w(win[:, k, 0:1344], w_in_v[:, k, 0:1344], 1344, vec[:, V_GMIX + k:V_GMIX + k + 1], "win")
                load_w(win[:, k, 1344:NA], w_in_v[:, k, C_HQ:C_HQ + 1024], 1024, vec[:, V_GMIX + k:V_GMIX + k + 1], "win")
            w_ukv_v = w_ukv.rearrange("(k p) n -> p k n", p=128)
            for k in range(2):
                load_w(wukv[:, k, :], w_ukv_v[:, k, :], 1024, vec[:, V_GKVA + k:V_GKVA + k + 1], "wukv")
            A_HQ, A_HG = 1344, 1344 + 512
            A.off = wl_off
            W1b = A.alloc(512, F32)
            W2b = A.alloc(512, F32)

            xt = [A.alloc(1024, F32) for _ in range(2)]
            junk = A.alloc(1024)
            hb = [A.alloc(1024)]
            hT = A.alloc(8 * 512).rearrange("p (k n) -> p k n", k=8)
            sqt = A.alloc(2 * 512).rearrange("p (k n) -> p k n", k=2)
            W1 = A.alloc(512, F32)
            W2 = A.alloc(512, F32)
            W3 = A.alloc(512, F32)
            W4 = A.alloc(512, F32)
            W3b = A.alloc(512, F32)
            W4b = A.alloc(512, F32)
            kT = A.alloc(4 * 512).rearrange("p (h n) -> p h n", h=4)
            qT = A.alloc(4 * 512).rearrange("p (h n) -> p h n", h=4)
            sgT = A.alloc(4 * 512).rearrange("p (h n) -> p h n", h=4)
            ktok = A.alloc(4 * 4 * 128).rearrange("p (t h d) -> p t h d", t=4, h=4)
            vtok = A.alloc(4 * 512).rearrange("p (t n) -> p t n", t=4)
            Amb = A.alloc(4 * 128).rearrange("p (h n) -> p h n", h=4)
            krg = A.alloc(4 * 64, F32).rearrange("p (t d) -> p t d", d=64)
            rp = [A.alloc(4 * 32, F32).rearrange("p (t d) -> p t d", d=32) for _ in range(4)]
            cst = A.alloc(4 * 32, F32).rearrange("p (t d) -> p t d", d=32)
            snt = A.alloc(4 * 32, F32).rearrange("p (t d) -> p t d", d=32)
            tA = A.alloc(128, F32)
            tB = A.alloc(128, F32)
            tI = A.alloc(128, I32)
            pm = [A.alloc(512)]
            rmask = vec[:, V_RMASK:V_RMASK + 512]
            invf = vec[:, V_INVF:V_INVF + 32]

            def cs_tables(g4):
                tt("dve", tA.rearrange("p (t d) -> p t d", d=32), posf[:, g4].unsqueeze(2).to_broadcast([128, 4, 32]),
                   invf.unsqueeze(1).to_broadcast([128, 4, 32]), ALU.mult, ["posf", "vec"], ["tA"])
                for (shift, dst, dkey) in ((0.0, snt, "snt"), (0.25, cst, "cst")):
                    dflat = dst.rearrange("p t d -> p (t d)")
                    ts("dve", tB, tA, shift, None, ALU.add, None, ["tA"], ["tB"])
                    cp("dve", tI, tB, ["tB"], ["tI"])
                    cp("dve", dflat, tI, ["tI"], [dkey])
                    tt("dve", tB, tB, dflat, ALU.subtract, ["tB", dkey], ["tB"])
                    ts("dve", dflat, tB, 0.5, None, ALU.is_gt, None, ["tB"], [dkey])
                    tt("dve", tB, tB, dflat, ALU.subtract, ["tB", dkey], ["tB"])
                    ts("dve", dflat, tB, -0.5, None, ALU.is_lt, None, ["tB"], [dkey])
                    tt("dve", tB, tB, dflat, ALU.add, ["tB", dkey], ["tB"])
                    act(dflat, tB, AF.Sin, ["tB"], [dkey], scale=2.0 * math.pi)

            def rope(dst_lo, dst_hi, src_lo, src_hi, cs, sn, rkeys, wkeys):
                tt("dve", rp[0], src_lo, cs, ALU.mult, rkeys, ["rp0"])
                tt("dve", rp[1], src_hi, sn, ALU.mult, rkeys, ["rp1"])
                tt("dve", dst_lo, rp[0], rp[1], ALU.subtract, ["rp0", "rp1"], wkeys)
                tt("dve", rp[2], src_hi, cs, ALU.mult, rkeys, ["rp2"])
                tt("dve", rp[3], src_lo, sn, ALU.mult, rkeys, ["rp3"])
                tt("dve", dst_hi, rp[2], rp[3], ALU.add, ["rp2", "rp3"], wkeys)

            def rms_load_transpose(src_rows, xbuf, xkey, hbuf, hkey, rcol, rkey, dstT, dkey, tsl, eng_q):
                dma(xbuf, src_rows, w=[xkey], grp="d" + xkey, eng=eng_q)
                act(junk, xbuf, AF.Square, [xkey], ["junk", rkey], accum=rcol)
                rsqrt_act(rcol, rcol, 1024, [rkey], [rkey])
                ts("dve", hbuf, xbuf, rcol, None, ALU.mult, None, [xkey, rkey], [hkey])
                pb = bankb(0)
                for k in range(8):
                    tp(pb[:, k * 128:(k + 1) * 128], hbuf[:, k * 128:(k + 1) * 128], [hkey], ["ps0"])
                cp("act", dstT[:, :, tsl], pb.rearrange("p (k n) -> p k n", k=8), ["ps0"], [dkey])

            def fproj(col0, bnk):
                for k in range(8):
                    mm(bank(bnk), win[:, k, col0:col0 + 128], hT[:, k, :], k == 0, k == 7, ["hT", "win"], ["ps%d" % bnk])

            P.barrier()
            ck(2)
            mA = [mixT[:, r_, :] for r_ in range(4)]
            mB = [mixT[:, 8 + r_, :] for r_ in range(4)]
            kT_ = [kT, mA[0].rearrange("p (h n) -> p h n", h=4)]
            qT_ = [qT, mA[1].rearrange("p (h n) -> p h n", h=4)]
            sgT_ = [sgT, mA[2].rearrange("p (h n) -> p h n", h=4)]
            ktok_ = [ktok, mA[3].rearrange("p (t h d) -> p t h d", t=4, h=4)]
            vtok_ = [vtok, mB[0].rearrange("p (t n) -> p t n", t=4)]
            hT_ = [hT, mixT[:, 9:11, :].rearrange("p a n -> p (a n)").rearrange("p (k n) -> p k n", k=8)]
            RW3 = mB[3][:, 0:1024].bitcast(F32)
            RW1 = mB[3][:, 1024:2048].bitcast(F32)
            dec_ = [dec, A.alloc(32, F32).rearrange("p (h c) -> p h c", h=4)]
            Wsets = [(W1, W2, W3, W4, "a"), (W1b, W2b, W3b, W4b, "b")]
            FM = [1, 2, 3]
            _fm = [0]

            def fm_next():
                bk = FM[_fm[0] % 3]
                _fm[0] += 1
                return bk

            def fprojp(col0, p_):
                bk = fm_next()
                for k in range(8):
                    mm(bank(bk), win[:, k, col0:col0 + 128], hT_[p_][:, k, :], k == 0, k == 7, ["hT%d" % p_, "win"], ["ps%d" % bk])
                return bk

            def front(g):
                p_ = g % 2
                for t in range(4):
                    gt = 4 * g + t
                    xb, xk = xt[gt % 2], "xt%d" % (gt % 2)
                    rcol = rs1[:, gt:gt + 1]
                    dma(xb, xs[gt * 128:(gt + 1) * 128, :], w=[xk], grp="d" + xk, eng="sp" if gt % 2 == 0 else "pool")
                    act(junk, xb, AF.Square, [xk], ["junk", "rs1"], accum=rcol)
                    rsqrt_act(rcol, rcol, 1024, ["rs1"], ["rs1"])
                    ts("dve", hb[0], xb, rcol, None, ALU.mult, None, [xk, "rs1"], ["hb0"])
                    yield
                    pb = bankb(0)
                    for k in range(8):
                        tp(pb[:, k * 128:(k + 1) * 128], hb[0][:, k * 128:(k + 1) * 128], ["hb0"], ["ps0"])
                    cp("act", hT_[p_][:, :, t * 128:(t + 1) * 128], pb.rearrange("p (k n) -> p k n", k=8), ["ps0"], ["hT%d" % p_])
                    yield

            def f_chain(h, bk, p_):
                Wa, Wb, Wc, Wd, sfx = Wsets[h % 2]
                k1, k2, k3, k4 = "W1" + sfx, "W2" + sfx, "W3" + sfx, "W4" + sfx
                pk = "ps%d" % bk
                sigmoid_(Wa, bank(bk), [pk], k1)
                act(Wb, Wa, AF.Ln, [k1, "lb", "oml"], [k2], scale=oml[:, h:h + 1], bias=lb[:, h:h + 1])
                P.add("dve", lambda e, Wc=Wc, Wb=Wb, rmask=rmask: e.tensor_tensor_scan(out=Wc, data0=rmask, data1=Wb, initial=0.0,
                                                                                       op0=ALU.mult, op1=ALU.add),
                      reads=[k2, "vec"], writes=[k3])
                ts("dve", Wa, Wa, noml[:, h:h + 1], oml[:, h:h + 1], ALU.mult, ALU.add, [k1, "oml"], [k1])
                act(Wb, Wc, AF.Exp, [k3], [k2], scale=-1.0)
                act(Wd, Wc, AF.Exp, [k3], [k4])
                cp("dve", dec_[p_][:, h, :], Wd.rearrange("p (c t) -> p c t", t=64)[:, :, 63], [k4], ["dec%d" % p_])
                tt("dve", kT_[p_][:, h, :], Wa, Wb, ALU.mult, [k1, k2], ["kT%d" % p_])

            def kT_transpose(h, p_):
                pb = bankb(0)
                for t in range(4):
                    tp(pb[:, t * 128:(t + 1) * 128], kT_[p_][:, h, t * 128:(t + 1) * 128], ["kT%d" % p_], ["ps0"])
                cp("act", ktok_[p_][:, :, h, :], pb[:, 0:512].rearrange("p (t d) -> p t d", t=4), ["ps0"], ["ktok%d" % p_])

            def qg_chain(h, bq, bg_, p_):
                Wa, Wb, Wc, Wd, sfx = Wsets[h % 2]
                k1, k4 = "W1" + sfx, "W4" + sfx
                sigmoid_(Wa, bank(bq), ["ps%d" % bq], k1)
                tt("dve", Wa, bank(bq), Wa, ALU.mult, ["ps%d" % bq, k1], [k1])
                stt("dve", qT_[p_][:, h, :], Wa, 128 ** -0.5, Wd, ALU.mult, ALU.mult, [k1, k4], ["qT%d" % p_])
                sigmoid_(Wa, bank(bg_), ["ps%d" % bg_], k1)
                tt("dve", sgT_[p_][:, h, :], bank(bg_), Wa, ALU.mult, ["ps%d" % bg_, k1], ["sgT%d" % p_])

            def proj(g, own):
                p_ = g % 2
                hk = "hT%d" % p_
                gsl = slice(g * 512, (g + 1) * 512)
                g4 = slice(4 * g, 4 * g + 4)
                if not own:
                    bf = [fprojp(C_HF + h * 128, p_) for h in range(2)]
                    f_chain(0, bf[0], p_)
                    yield
                    bf.append(fprojp(C_HF + 2 * 128, p_))
                    f_chain(1, bf[1], p_)
                    yield
                    kT_transpose(0, p_)
                    bf.append(fprojp(C_HF + 3 * 128, p_))
                    f_chain(2, bf[2], p_)
                    yield
                    kT_transpose(1, p_)
                    f_chain(3, bf[3], p_)
                    yield
                    kT_transpose(2, p_)
                    yield
                    kT_transpose(3, p_)
                    yield
                else:
                    for h in range(4):
                        bf_ = fprojp(C_HF + h * 128, p_)
                        bq = fprojp(A_HQ + h * 128, p_)
                        f_chain(h, bf_, p_)
                        yield
                        bg_ = fprojp(A_HG + h * 128, p_)
                        qg_chain(h, bq, bg_, p_)
                        yield
                        kT_transpose(h, p_)
                        yield
                for t in range(4):
                    bk = fm_next()
                    for k in range(8):
                        mm(bank(bk), hT_[p_][:, k, t * 128:(t + 1) * 128], win[:, k, C_HI:C_HI + 512], k == 0, k == 7, [hk, "win"], ["ps%d" % bk])
                    cp("dve", vtok_[p_][:, t, :], bank(bk), ["ps%d" % bk], ["vtok%d" % p_])
                    yield
                bc = [fprojp(C_KV + m * 128, p_) for m in range(2)]
                for m in range(2):
                    bk = bc[m]
                    cp("act", ckvT[:, m, gsl], bank(bk), ["ps%d" % bk], ["ckvT"])
                    act(sqt[:, m, :], bank(bk), AF.Square, ["ps%d" % bk], ["sqt"])
                yield
                for t in range(4):
                    for m in range(2):
                        mm(bank(4, 1, 256 + t), sqt[:, m, t * 128:(t + 1) * 128], onesb[:, 0:1], m == 0, m == 1, ["sqt", "cbf"], ["ps4"])
                rsqrt_act(rskv[:, g4], bank(4, 4, 256), 256, ["ps4"], ["rskv"])
                tt("dve", rskv2[:, g4], rskv[:, g4], rskv[:, g4], ALU.mult, ["rskv"], ["rskv2"])
                yield
                for t in range(4):
                    for k in range(8):
                        mm(bank(4, 64, t * 64), hT_[p_][:, k, t * 128:(t + 1) * 128], win[:, k, C_KR:C_KR + 64], k == 0, k == 7,
                           [hk, "win"], ["ps4"])
                for t in range(4):
                    act(junk[:, 0:64], bank(4, 64, t * 64), AF.Square, ["ps4"], ["junk", "sskr"], accum=sskr[:, 4 * g + t:4 * g + t + 1])
                yield
                for t in range(4):
                    tt("dve", krg[:, t, :], bank(4, 64, t * 64), vec[:, V_GK + 128:V_GK + 192], ALU.mult, ["ps4", "vec"], ["krg"])
                yield
                cs_tables(g4)
                yield
                rope(kr[:, g4, 0:32], kr[:, g4, 32:64], krg[:, :, 0:32], krg[:, :, 32:64], cst, snt, ["krg", "cst", "snt"], ["kr"])
                yield

            def rec(g, own):
                p_ = g % 2
                kTp, qTp, sgTp, ktokp, vtokp, decp = kT_[p_], qT_[p_], sgT_[p_], ktok_[p_], vtok_[p_], dec_[p_]
                kk, qk, sgk, ktk, vtk, dk = ["%s%d" % (n_, p_) for n_ in ("kT", "qT", "sgT", "ktok", "vtok", "dec")]
                osl0 = (g - OG0) * 512
                for t in range(4):
                    tsl = slice(t * 128, (t + 1) * 128)
                    if own:
                        for h in range(4):
                            mm(bank(6, 128, h * 128), kTp[:, h, tsl], qTp[:, h, tsl], True, True, [kk, qk], ["ps6"])
                        tt("dve", Amb, bank(6).rearrange("p (h n) -> p h n", h=4), tri2.unsqueeze(1).to_broadcast([128, 4, 128]),
                           ALU.mult, ["ps6", "cbf"], ["Amb"])
                    for c2 in range(2):
                        rows = slice(c2 * 64, (c2 + 1) * 64)
                        csl = slice(t * 128 + c2 * 64, t * 128 + (c2 + 1) * 64)
                        ci = t * 2 + c2
                        if own:
                            for h in range(4):
                                mm(bank(7, 64, h * 128 + c2 * 64), vtokp[rows, t, h * 128:(h + 1) * 128], Amb[rows, h, rows], True, False,
                                   [vtk, "Amb"], ["ps7"])
                                mm(bank(7, 64, h * 128 + c2 * 64), Sb[:, h * 128:(h + 1) * 128], qTp[:, h, csl], False, True,
                                   ["Sb", qk], ["ps7"])
                        for h in range(4):
                            mm(bank(5, 128, h * 128), ktokp[rows, t, h, :], vtokp[rows, t, h * 128:(h + 1) * 128], True, True,
                               [ktk, vtk], ["ps5"])
                        tt("dve", RW3, S32, bank(5), ALU.add, ["S32", "ps5"], ["RW3"])
                        tt("dve", S32.rearrange("p (h n) -> p h n", h=4), RW3.rearrange("p (h n) -> p h n", h=4),
                           decp[:, :, ci:ci + 1].to_broadcast([128, 4, 128]), ALU.mult, ["RW3", dk], ["S32"])
                        if own or (g == OG0 - 1 and ci == 7):
                            cp("act", Sb, S32, ["S32"], ["Sb"])
                        yield
                    if own:
                        act(pm[0], bank(7), AF.Square, ["ps7"], ["pm0"])
                        mm(bank(6), onesb, pm[0], True, True, ["cbf", "pm0"], ["ps6"])
                        rsqrt_act(RW1, bank(6), 128, ["ps6"], ["RW1"])
                        tt("dve", RW1, RW1, bank(7), ALU.mult, ["RW1", "ps7"], ["RW1"])
                        for h in range(4):
                            stt("dve", mixT[:, 4 + h, osl0 + t * 128:osl0 + (t + 1) * 128], RW1[:, h * 128:(h + 1) * 128],
                                vec[:, V_GHGO + h:V_GHGO + h + 1], sgTp[:, h, tsl], ALU.mult, ALU.mult, ["RW1", "vec", sgk], ["mixT"])

            def interleave(gens):
                gens = [g_ for g_ in gens if g_ is not None]
                while gens:
                    for g_ in list(gens):
                        if g_ not in gens:
                            continue
                        try:
                            next(g_)
                        except StopIteration:
                            while g_ in gens:
                                gens.remove(g_)

            interleave([front(0)])
            interleave([proj(0, False), front(1)])
            for g in range(NG):
                if g == 1:
                    ck(3)
                if g == OG0 + 1:
                    ck(35)
                pj = proj(g + 1, g + 1 >= OG0) if g + 1 < NG else None
                interleave([rec(g, g >= OG0), pj, front(g + 2) if g + 2 < NG else None, pj])

            P.barrier()
            ck(4)
            A.off = p1_off
            qTn = A.alloc(4 * OWN).rearrange("p (h n) -> p h n", h=4)
            qTr = A.alloc(2 * OWN).rearrange("p (h n) -> p h n", h=2)
            p2_off = A.off
            NB = 896
            win = A.alloc(8 * NB).rearrange("p (k n) -> p k n", k=8)
            wuq = A.alloc(3 * 768).rearrange("p (k n) -> p k n", k=3)
            kmemT = A.alloc(4 * 256).rearrange("p (h n) -> p h n", h=4)
            vmem = A.alloc(2 * 512).rearrange("p (t n) -> p t n", t=2)
            xt = [A.alloc(1024, F32) for _ in range(2)]
            junk = A.alloc(1024)
            hb = [A.alloc(1024) for _ in range(2)]
            W1 = A.alloc(512, F32)
            W2 = A.alloc(512, F32)
            p1b_off = A.off
            stg = [A.alloc(1024, F32) for _ in range(3)]
            wmkv = A.alloc(8 * 1024).rearrange("p (k n) -> p k n", k=8)
            memT = A.alloc(8 * 256).rearrange("p (k n) -> p k n", k=8)
            kmtok = A.alloc(512).rearrange("p (h d) -> p h d", h=4)
            for k in range(8):
                load_w(win[:, k, 0:384], w_in_v[:, k, C_CQ:C_CQ + 384], 384, vec[:, V_GMIX + k:V_GMIX + k + 1], "win")
                load_w(win[:, k, 384:NB], w_in_v[:, k, C_MQ:C_MQ + 512], 512, vec[:, V_GMIX + k:V_GMIX + k + 1], "win")
            w_uq_v = w_uq.rearrange("(k p) n -> p k n", p=128)
            for k in range(3):
                load_w(wuq[:, k, :], w_uq_v[:, k, :], 768, vec[:, V_GQA + k:V_GQA + k + 1], "wuq")
            w_mkv_v = w_mkv.rearrange("(k p) n -> p k n", p=128)
            for k in range(8):
                load_w(wmkv[:, k, :], w_mkv_v[:, k, :], 1024, vec[:, V_GMEM + k:V_GMEM + k + 1], "wmkv")
            for mt in range(2):
                rms_load_transpose(memd[mt * 128:(mt + 1) * 128, :], xt[mt], "xt%d" % mt, hb[mt], "hb%d" % mt,
                                   scr[:, mt:mt + 1], "scr%d" % mt, memT, "memT", slice(mt * 128, (mt + 1) * 128), "sp" if mt == 0 else "pool")
            for mt in range(2):
                for half in range(2):
                    for k in range(8):
                        mm(bank(1 + half), memT[:, k, mt * 128:(mt + 1) * 128], wmkv[:, k, half * 512:(half + 1) * 512],
                           k == 0, k == 7, ["memT", "wmkv"], ["ps%d" % (1 + half)])
                kps = bank(1).rearrange("p (h d) -> p h d", h=4)
                act(W1, bank(1), AF.Square, ["ps1"], ["W1"])
                P.add("dve", lambda e: e.tensor_reduce(out=scr[:, 8:12], in_=W1.rearrange("p (h d) -> p h d", h=4), axis=AX.X, op=ALU.add),
                      reads=["W1"], writes=["scr8"])
                rsqrt_act(scr[:, 8:12], scr[:, 8:12], 128, ["scr8"], ["scr8"])
                W1v = W1.rearrange("p (h d) -> p h d", h=4)
                tt("dve", W1v, kps, scr[:, 8:12].unsqueeze(2).to_broadcast([128, 4, 128]), ALU.mult, ["ps1", "scr8"], ["W1"])
                tt("dve", kmtok, W1v, vec[:, V_GMK:V_GMK + 128].unsqueeze(1).to_broadcast([128, 4, 128]), ALU.mult,
                   ["W1", "vec"], ["kmtok"])
                pb = bankb(0)
                for h in range(4):
                    tp(pb[:, h * 128:(h + 1) * 128], kmtok[:, h, :], ["kmtok"], ["ps0"])
                cp("act", kmemT[:, :, mt * 128:(mt + 1) * 128], pb[:, 0:512].rearrange("p (h n) -> p h n", h=4), ["ps0"], ["kmemT"])
                cp("act", vmem[:, mt, :], bank(2), ["ps2"], ["vmem"])
            P.barrier()
            ck(5)
            A.off = p1b_off
            hT = A.alloc(8 * 512).rearrange("p (k n) -> p k n", k=8)
            sqt = A.alloc(3 * 512).rearrange("p (k n) -> p k n", k=3)
            cqT = A.alloc(3 * 512).rearrange("p (k n) -> p k n", k=3)
            rp = [A.alloc(4 * 32, F32).rearrange("p (t d) -> p t d", d=32) for _ in range(4)]
            cst = A.alloc(4 * 32, F32).rearrange("p (t d) -> p t d", d=32)
            snt = A.alloc(4 * 32, F32).rearrange("p (t d) -> p t d", d=32)
            tA = A.alloc(128, F32)
            tB = A.alloc(128, F32)
            tI = A.alloc(128, I32)
            qn = A.alloc(768, F32).rearrange("p (h d) -> p h d", h=4)
            qbn = A.alloc(4 * 128).rearrange("p (h d) -> p h d", h=4)
            qbr = A.alloc(4 * 64).rearrange("p (h d) -> p h d", h=4)
            mqT = A.alloc(4 * 512).rearrange("p (h n) -> p h n", h=4)
            pm = [A.alloc(512) for _ in range(2)]
            sqy = A.alloc(4 * 512).rearrange("p (h n) -> p h n", h=4)
            qn_ = [qn, A.alloc(768, F32).rearrange("p (h d) -> p h d", h=4)]
            qbn_ = [qbn, A.alloc(4 * 128).rearrange("p (h d) -> p h d", h=4)]
            qbr_ = [qbr, A.alloc(4 * 64).rearrange("p (h d) -> p h d", h=4)]
            mqb = A.alloc(4 * 128).rearrange("p (h d) -> p h d", h=4)

            def fb(g):
                for t in range(4):
                    gt = 4 * g + t
                    rms_load_transpose(xs[gt * 128:(gt + 1) * 128, :], xt[gt % 2], "xt%d" % (gt % 2), hb[gt % 2], "hb%d" % (gt % 2),
                                       scr[:, 2 + t:3 + t], "scr%d" % (2 + t), hT, "hT", slice(t * 128, (t + 1) * 128), "sp" if gt % 2 == 0 else "pool")
                    yield

            def qstream(g):
                g4 = slice(4 * g, 4 * g + 4)
                cs_tables(g4)
                yield
                for m in range(3):
                    b_ = 1 + (m % 2)
                    fproj(m * 128, b_)
                    cp("act", cqT[:, m, :], bank(b_), ["ps%d" % b_], ["cqT"])
                    act(sqt[:, m, :], bank(b_), AF.Square, ["ps%d" % b_], ["sqt"])
                yield
                for t in range(4):
                    for m in range(3):
                        mm(bank(4, 1, t), sqt[:, m, t * 128:(t + 1) * 128], onesb[:, 0:1], m == 0, m == 2, ["sqt", "cbf"], ["ps4"])
                ts("dve", scr[:, 16:20], bank(4, 4), EPS / 384.0, EPS * EPS, ALU.mult, ALU.add, ["ps4"], ["scr16"])
                yield
                for t in range(4):
                    q_ = t % 2
                    gt = 4 * g + t
                    tsl = slice(t * 128, (t + 1) * 128)
                    qb0 = 6 if q_ == 0 else 2
                    pk = ["ps%d" % qb0, "ps%d" % (qb0 + 1)]
                    qnq, qbnq, qbrq = qn_[q_], qbn_[q_], qbr_[q_]
                    nk, bnk, brk = "qn%d" % q_, "qbn%d" % q_, "qbr%d" % q_
                    sc0 = 20 + 4 * q_
                    sck = "scr%d" % sc0
                    scq = scr[:, sc0:sc0 + 4]
                    for (c0, n, off) in ((0, 512, 0), (512, 256, 512)):
                        for m in range(3):
                            mm(ps[:, qb0 * 512 + off:qb0 * 512 + off + n], cqT[:, m, tsl], wuq[:, m, c0:c0 + n], m == 0, m == 2,
                               ["cqT", "wuq"], [pk[off // 512]])
                    qps = ps[:, qb0 * 512:qb0 * 512 + 768]
                    qps3 = qps.rearrange("p (h d) -> p h d", h=4)
                    act(qnq.rearrange("p h d -> p (h d)"), qps, AF.Square, pk, [nk])
                    P.add("dve", lambda e, scq=scq, qnq=qnq: e.tensor_reduce(out=scq, in_=qnq, axis=AX.X, op=ALU.add), reads=[nk], writes=[sck])
                    ts("dve", scq, scq, 1.0 / 192, scr[:, 16 + t:17 + t], ALU.mult, ALU.add, [sck, "scr16"], [sck])
                    yield
                    act(scq, scq, AF.Ln, [sck], [sck])
                    act(scq, scq, AF.Exp, [sck], [sck], scale=-0.5)
                    tt("dve", qnq, qps3, scq.unsqueeze(2).to_broadcast([128, 4, 192]), ALU.mult, pk + [sck], [nk])
                    tt("dve", qnq, qnq, vec[:, V_GQ:V_GQ + 192].unsqueeze(1).to_broadcast([128, 4, 192]), ALU.mult, [nk, "vec"], [nk])
                    yield
                    cp("act", qbnq, qnq[:, :, 0:128], [nk], [bnk])
                    cs = cst[:, t, :].unsqueeze(1).to_broadcast([128, 4, 32])
                    sn = snt[:, t, :].unsqueeze(1).to_broadcast([128, 4, 32])
                    rope(qbrq[:, :, 0:32], qbrq[:, :, 32:64], qnq[:, :, 128:160], qnq[:, :, 160:192], cs, sn, [nk, "cst", "snt"], [brk])
                    yield
                    pb = bankb(0)
                    for h in range(4):
                        tp(pb[:, h * 128:(h + 1) * 128], qbnq[:, h, :], [bnk], ["ps0"])
                    qbrf = qbrq.rearrange("p h d -> p (h d)")
                    for pr in range(2):
                        tp(pb[:, 512 + pr * 128:512 + (pr + 1) * 128], qbrf[:, pr * 128:(pr + 1) * 128], [brk], ["ps0"])
                    osl_t = slice((g - OG0) * 512 + t * 128, (g - OG0) * 512 + (t + 1) * 128)
                    cp("act", qTn[:, :, osl_t], pb[:, 0:512].rearrange("p (h n) -> p h n", h=4), ["ps0"], ["qTn"])
                    cp("act", qTr[:, :, osl_t], pb[:, 512:768].rearrange("p (h n) -> p h n", h=2), ["ps0"], ["qTr"])
                    yield

            def mstream(g):
                for t in range(4):
                    tsl = slice(t * 128, (t + 1) * 128)
                    for k in range(8):
                        mm(bank(5), hT[:, k, tsl], win[:, k, 384:NB], k == 0, k == 7, ["hT", "win"], ["ps5"])
                    act(W1, bank(5), AF.Square, ["ps5"], ["W1"])
                    P.add("dve", lambda e: e.tensor_reduce(out=scr[:, 28:32], in_=W1.rearrange("p (h d) -> p h d", h=4), axis=AX.X, op=ALU.add),
                          reads=["W1"], writes=["scr28"])
                    rsqrt_act(scr[:, 28:32], scr[:, 28:32], 128, ["scr28"], ["scr28"])
                    yield
                    W1v = W1.rearrange("p (h d) -> p h d", h=4)
                    tt("dve", W1v, bank(5).rearrange("p (h d) -> p h d", h=4), scr[:, 28:32].unsqueeze(2).to_broadcast([128, 4, 128]),
                       ALU.mult, ["ps5", "scr28"], ["W1"])
                    tt("dve", mqb, W1v, vec[:, V_GMQ:V_GMQ + 128].unsqueeze(1).to_broadcast([128, 4, 128]), ALU.mult, ["W1", "vec"], ["mqb"])
                    yield
                    pb = bankb(5)
                    for h in range(4):
                        tp(pb[:, h * 128:(h + 1) * 128], mqb[:, h, :], ["mqb"], ["ps5"])
                    cp("act", mqT[:, :, tsl], pb[:, 0:512].rearrange("p (h n) -> p h n", h=4), ["ps5"], ["mqT"])
                    yield

            def ystream(g):
                osl = slice((g - OG0) * 512, (g - OG0 + 1) * 512)
                for h in range(4):
                    for mt in range(2):
                        mm(bank(1 + mt), kmemT[:, h, mt * 128:(mt + 1) * 128], mqT[:, h, :], True, True, ["kmemT", "mqT"], ["ps%d" % (1 + mt)])
                        act(pm[mt], bank(1 + mt), AF.Exp, ["ps%d" % (1 + mt)], ["pm%d" % mt], scale=128 ** -0.5)
                    yield
                    for mt in range(2):
                        mm(bank(6), vmem[:, mt, h * 128:(h + 1) * 128], pm[mt], mt == 0, mt == 1, ["vmem", "pm%d" % mt], ["ps6"])
                    for mt in range(2):
                        mm(bank(7), onesb, pm[mt], mt == 0, mt == 1, ["cbf", "pm%d" % mt], ["ps7"])
                    yield
                    P.add("dve", lambda e: e.reciprocal(out=W2, in_=bank(7)), reads=["ps7"], writes=["W2"])
                    tt("dve", W2, bank(6), W2, ALU.mult, ["ps6", "W2"], ["W2"])
                    cp("act", mixT[:, 8 + h, osl], W2, ["W2"], ["mixT"])
                    act(sqy[:, h, :], W2, AF.Square, ["W2"], ["sqy"])
                    yield
                for t in range(4):
                    for h in range(4):
                        mm(bank(4, 1, t), sqy[:, h, t * 128:(t + 1) * 128], onesb[:, 0:1], h == 0, h == 3, ["sqy", "cbf"], ["ps4"])
                o4 = slice(4 * (g - OG0), 4 * (g - OG0) + 4)
                cp("dve", ssmem[:, o4], bank(4, 4), ["ps4"], ["ssmem"])
                yield

            interleave([fb(OG0)])
            for g in range(OG0, NG):
                interleave([qstream(g), mstream(g)])
                interleave([ystream(g), fb(g + 1) if g + 1 < NG else None])

            P.barrier()
            ck(6)
            A.off = p2_off
            junk = A.alloc(1024)
            KTn = A.alloc(NS)
            KTr = A.alloc(NS)
            vt = A.alloc(NT * 128).rearrange("p (t d) -> p t d", d=128)
            kbn2 = [A.alloc(4 * 128).rearrange("p (t d) -> p t d", t=4) for _ in range(2)]
            kbr2 = [A.alloc(4 * 128).rearrange("p (t d) -> p t d", t=4) for _ in range(2)]
            pt = [A.alloc(512) for _ in range(4)]
            R1 = A.alloc(512, F32)
            Y1 = A.alloc(512, F32)
            sq2 = A.alloc(512)
            for p_ in range(2):
                P.add("dve", lambda e, p_=p_: e.memset(kbr2[p_].rearrange("p t d -> p (t d)"), 0.0), writes=["kbr%d" % p_])
            bgS = [A.alloc(1024, F32) for _ in range(2)]
            bgC = [A.alloc(1024) for _ in range(2)]

            def bg_convert():
                cnt = 0
                for a_, wsrc in enumerate((w_gate, w_up)):
                    wv = wsrc.rearrange("(k p) n -> p k n", p=128)
                    for k in range(8):
                        for c0 in range(0, DFF, 1024):
                            n = min(1024, DFF - c0)
                            i = cnt % 2
                            cnt += 1
                            dma(bgS[i][:, 0:n], wv[:, k, c0:c0 + n], w=["bgS%d" % i], grp="dbgs%d" % i, eng="sp")
                            ts("dve", bgC[i][:, 0:n], bgS[i][:, 0:n], vec[:, V_GFFN + k:V_GFFN + k + 1], None, ALU.mult, None,
                               ["bgS%d" % i, "vec"], ["bgC%d" % i])
                            f0, f1 = c0 // 128, (c0 + n) // 128
                            cc = a_ * 1024 + k * 128
                            dma(wgu_scr[f0:f1, :, cc:cc + 128].rearrange("f p n -> p f n"),
                                bgC[i][:, 0:n].rearrange("p (f n) -> p f n", n=128), r=["bgC%d" % i], w=["wgu_scr"],
                                grp="dbgo%d" % i, eng="pool")
                            yield

            bg = bg_convert()
            SC = 192 ** -0.5
            mixhm = mixT[:, 4:12, :].rearrange("p a n -> p (a n)")
            dma(mix_scr, mixhm, r=["mixT"], w=["mix_scr"], grp="dmsp")
            P.barrier()
            KTn_ = [KTn, mixT[:, 4:8, :].rearrange("p a n -> p (a n)")]
            vt_ = [vt, mixT[:, 8:12, :].rearrange("p a n -> p (a n)").rearrange("p (t d) -> p t d", d=128)]
            ksc = A.alloc(512, F32)
            P.add("dve", lambda e: e.memset(scr[:, 48:52], -0.5), writes=["scr48"])

            def kbuild(h, alone=False):
                ro = (h % 2) * 64
                s_ = h % 2
                KTs, vts = KTn_[s_], vt_[s_]
                kn_k, vt_k, kr_k = "KTn%d" % s_, "vt%d" % s_, "KTr%d" % s_
                for g in range(NG):
                    p_ = g % 2
                    b0, bT = 1, 0
                    kscb, kkey = ksc, "ksc"
                    if alone and p_ == 1:
                        b0, bT = 3, 5
                        kscb, kkey = Y1, "Y1"
                    pk = ["ps%d" % b0, "ps%d" % (b0 + 1)]
                    g4 = slice(4 * g, 4 * g + 4)
                    for ti in range(4):
                        tl = g * 4 + ti
                        for m in range(2):
                            mm(ps[:, b0 * 512 + ti * 256:b0 * 512 + (ti + 1) * 256], ckvT[:, m, tl * 128:(tl + 1) * 128],
                               wukv[:, m, h * 256:(h + 1) * 256], m == 0, m == 1, ["ckvT", "wukv"], [pk[ti // 2]])
                    kvv = ps[:, b0 * 512:b0 * 512 + 1024].rearrange("p (t c) -> p t c", c=256)
                    ksc3 = kscb.rearrange("p (t d) -> p t d", d=128)
                    c_ = 32 + p_ * 8
                    ssq_, c1_ = scr[:, c_:c_ + 4], scr[:, c_ + 4:c_ + 8]
                    sk = "scr%d" % c_
                    tt("dve", vts[:, g4, :], kvv[:, :, 128:256], rskv[:, g4].unsqueeze(2).to_broadcast([128, 4, 128]), ALU.mult,
                       pk + ["rskv"], [vt_k])
                    if alone:
                        act(ksc3, kvv[:, :, 0:128], AF.Square, pk, [kkey])
                        P.add("dve", lambda e, ssq_=ssq_, ksc3=ksc3: e.tensor_reduce(out=ssq_, in_=ksc3, axis=AX.X, op=ALU.add),
                              reads=[kkey], writes=[sk])
                        tt("dve", ssq_, ssq_, rskv2[:, g4], ALU.mult, [sk, "rskv2"], [sk])
                        tt("dve", ssq_, ssq_, sskr[:, g4], ALU.add, [sk, "sskr"], [sk])
                        yield
                        rsqrt_act(ssq_, ssq_, 192, [sk], [sk])
                        tt("dve", c1_, ssq_, rskv[:, g4], ALU.mult, [sk, "rskv"], [sk])
                        tt("dve", ksc3, kvv[:, :, 0:128], c1_.unsqueeze(2).to_broadcast([128, 4, 128]), ALU.mult, pk + [sk], [kkey])
                    else:
                        sqb = junk[:, 0:512].rearrange("p (t d) -> p t d", d=128)
                        cp("dve", ksc3, kvv[:, :, 0:128], pk, [kkey])
                        tt("dve", sqb, ksc3, ksc3, ALU.mult, [kkey], ["junk"])
                        P.add("dve", lambda e, ssq_=ssq_, sqb=sqb: e.tensor_reduce(out=ssq_, in_=sqb, axis=AX.X, op=ALU.add),
                              reads=["junk"], writes=[sk])
                        tt("dve", ssq_, ssq_, rskv2[:, g4], ALU.mult, [sk, "rskv2"], [sk])
                        tt("dve", ssq_, ssq_, sskr[:, g4], ALU.add, [sk, "sskr"], [sk])
                        ts("dve", ssq_, ssq_, 1.0 / 192, EPS, ALU.mult, ALU.add, [sk], [sk])
                        yield
                        tt("pool", ssq_, ssq_, scr[:, 48:52], ALU.pow, [sk, "scr48"], [sk])
                        tt("dve", c1_, ssq_, rskv[:, g4], ALU.mult, [sk, "rskv"], [sk])
                        tt("dve", ksc3, ksc3, c1_.unsqueeze(2).to_broadcast([128, 4, 128]), ALU.mult, [kkey, sk], [kkey])
                    tt("dve", kbn2[p_], ksc3, vec[:, V_GK:V_GK + 128].unsqueeze(1).to_broadcast([128, 4, 128]), ALU.mult,
                       [kkey, "vec"], ["kbn%d" % p_])
                    tt("dve", kbr2[p_][:, :, ro:ro + 64], kr[:, g4, :], ssq_.unsqueeze(2).to_broadcast([128, 4, 64]), ALU.mult,
                       ["kr", sk], ["kbr%d" % p_])
                    yield
                    pb = bankb(bT)
                    for ti in range(4):
                        tp(pb[:, ti * 128:(ti + 1) * 128], kbn2[p_][:, ti, :], ["kbn%d" % p_], ["ps%d" % bT])
                        tp(pb[:, 512 + ti * 128:512 + (ti + 1) * 128], kbr2[p_][:, ti, :], ["kbr%d" % p_], ["ps%d" % bT])
                    ev = "act" if alone else "dve"
                    cp(ev, KTs[:, g * 512:(g + 1) * 512], pb[:, 0:512], ["ps%d" % bT], [kn_k])
                    cp(ev, KTr[ro:ro + 64, g * 512:(g + 1) * 512], pb[ro:ro + 64, 512:1024], ["ps%d" % bT], [kr_k])
                    if g % 4 == 3:
                        next(bg, None)
                    yield

            for _ in kbuild(0, alone=True):
                pass
            for h in range(4):
                ro = (h % 2) * 64
                KTn, vt = KTn_[h % 2], vt_[h % 2]
                kn_k, vt_k, kr_k = "KTn%d" % (h % 2), "vt%d" % (h % 2), "KTr%d" % (h % 2)
                kb = kbuild(h + 1) if h < 3 else iter(())
                tile_ctr = [0]
                for Q in range(4):
                    qsl0 = Q * 512
                    nkt = OT0 + 4 * Q + 4
                    def tile_geo(kt):
                        j = kt - (OT0 + 4 * Q)
                        q0 = 0 if j < 0 else 128 * j
                        return j, q0, 512 - q0

                    def emit_S(kt):
                        j, q0, n = tile_geo(kt)
                        sb_ = 3 + (kt % 2)
                        ksl = slice(kt * 128, (kt + 1) * 128)
                        qs = slice(qsl0 + q0, qsl0 + 512)
                        mm(bank(sb_, n, q0), KTn[:, ksl], qTn[:, h, qs], True, False, [kn_k, "qTn"], ["ps%d" % sb_])
                        mm(bank(sb_, n, q0), KTr[ro:ro + 64, ksl], qTr[ro:ro + 64, h // 2, qs], False, True, [kr_k, "qTr"], ["ps%d" % sb_])

                    emit_S(0)
                    for kt in range(nkt):
                        if kt % 16 == 8:
                            next(bg, None)
                        tile_ctr[0] += 1
                        if tile_ctr[0] % 4 == 2:
                            next(kb, None)
                        if kt + 1 < nkt:
                            emit_S(kt + 1)
                        j, q0, n = tile_geo(kt)
                        sb_ = 3 + (kt % 2)
                        pti = kt % 4
                        act(pt[pti][:, q0:512], bank(sb_, n, q0), AF.Exp, ["ps%d" % sb_], ["pt%d" % pti], scale=SC)
                        if j >= 0:
                            tt("dve", pt[pti][:, q0:q0 + 128], pt[pti][:, q0:q0 + 128], tri, ALU.mult, ["pt%d" % pti, "cbf"], ["pt%d" % pti])
                        reg = kt // 16
                        von = onesb if reg >= 3 else vones[:, reg * 128:(reg + 1) * 128]
                        mm(bank(5, n, q0), vt[:, kt, :], pt[pti][:, q0:512], kt == 0, kt == nkt - 1, [vt_k, "pt%d" % pti], ["ps5"])
                        mm(bank(6, n, q0), von, pt[pti][:, q0:512], kt == 0, kt == nkt - 1, ["vones", "cbf", "pt%d" % pti], ["ps6"])
                    P.add("dve", lambda e: e.reciprocal(out=R1, in_=bank(6)), reads=["ps6"], writes=["R1"])
                    tt("dve", Y1, bank(5), R1, ALU.mult, ["ps5", "R1"], ["Y1"])
                    cp("act", mixT[:, h, qsl0:qsl0 + 512], Y1, ["Y1"], ["mixT"])
                    act(sq2, Y1, AF.Square, ["Y1"], ["sq2"])
                    for t in range(4):
                        mm(bank(7, 1, t), sq2[:, t * 128:(t + 1) * 128], onesb[:, 0:1], True, True, ["sq2", "cbf"], ["ps7"])
                    o4 = slice(4 * Q, 4 * Q + 4)
                    tt("dve", ssmla[:, o4], ssmla[:, o4], bank(7, 4), ALU.add, ["ssmla", "ps7"], ["ssmla"])
                for _ in kb:
                    pass

            for _ in bg:
                pass
            P.barrier()
            ck(7)
            dma(mixhm, mix_scr, r=["mix_scr"], w=["mixT"], grp="dmsp")
            A.off = base_off
            wout = A.alloc(12 * 1024).rearrange("p (k n) -> p k n", k=12)
            wdn = A.alloc(NF * 1024).rearrange("p (k n) -> p k n", k=NF)
            p3_off = A.off
            stg4 = [A.alloc(1024, F32) for _ in range(6)]
            rsqrt_act(rsmla, ssmla, 512, ["ssmla"], ["rsmla"])
            rsqrt_act(rsmem, ssmem, 512, ["ssmem"], ["rsmem"])
            w_out_v = w_out.rearrange("(k p) n -> p k n", p=128)
            w_dn_v = w_down.rearrange("(k p) n -> p k n", p=128)
            jobs = [(wout[:, k, :], w_out_v[:, k, :], None if 4 <= k < 8 else vec[:, V_GOUT + k:V_GOUT + k + 1], "wout") for k in range(12)]
            jobs += [(wdn[:, k, :], w_dn_v[:, k, :], None, "wdn") for k in range(NF)]
            for ji, (dst, src, sc, dkey) in enumerate(jobs):
                i = ji % 6
                dma(stg4[i], src, w=["stg4_%d" % i], grp="dstg4_%d" % i, eng=("sp", "pool", "act")[ji % 3])
                if ji % 2 == 0:
                    if sc is None:
                        cp("dve", dst, stg4[i], ["stg4_%d" % i], [dkey])
                    else:
                        ts("dve", dst, stg4[i], sc, None, ALU.mult, None, ["stg4_%d" % i, "vec"], [dkey])
                else:
                    act(dst, stg4[i], AF.Copy, ["stg4_%d" % i, "vec"], [dkey], scale=(1.0 if sc is None else sc))
            P.barrier()
            A.off = p3_off
            xo = [A.alloc(1024, F32) for _ in range(4)]
            h2b = [A.alloc(1024)] * 2
            h2T = A.alloc(8 * 512).rearrange("p (k n) -> p k n", k=8)
            actT = A.alloc(NF * 512).rearrange("p (k n) -> p k n", k=NF)
            wgu = [A.alloc(2048).rearrange("p (a k n) -> p a k n", a=2, k=8) for _ in range(3)]
            G1 = A.alloc(512, F32)
            G2 = A.alloc(512, F32)
            yo = [A.alloc(512, F32) for _ in range(2)]
            junk3 = A.alloc(1024)
            ck(8)
            for Gq in range(4):
                for t in range(4):
                    ot = Gq * 4 + t
                    tok = slice(ot * 128, (ot + 1) * 128)
                    dma(xo[t], xs[OWN0 + ot * 128:OWN0 + (ot + 1) * 128, :], w=["xo%d" % t], grp="dxo%d" % t, eng="sp" if t % 2 == 0 else "pool")
                    pa = ps[:, 1 * 512:3 * 512]
                    pbk = ps[:, 3 * 512:5 * 512]
                    pc = ps[:, 5 * 512:7 * 512]
                    for (k0, pk, keys) in ((0, 1, ["ps1", "ps2"]), (4, 3, ["ps3", "ps4"]), (8, 5, ["ps5", "ps6"])):
                        for nh in range(2):
                            for k in range(4):
                                mm(bank(pk + nh), mixT[:, k0 + k, tok], wout[:, k0 + k, nh * 512:(nh + 1) * 512], k == 0, k == 3,
                                   ["mixT", "wout"], [keys[nh]])
                    stt("dve", xo[t], pa, rsmla[:, ot:ot + 1], xo[t], ALU.mult, ALU.add, ["ps1", "ps2", "rsmla", "xo%d" % t], ["xo%d" % t])
                    tt("dve", xo[t], xo[t], pbk, ALU.add, ["xo%d" % t, "ps3", "ps4"], ["xo%d" % t])
                    stt("dve", xo[t], pc, rsmem[:, ot:ot + 1], xo[t], ALU.mult, ALU.add, ["ps5", "ps6", "rsmem", "xo%d" % t], ["xo%d" % t])
                    c_ = 40 + t
                    act(junk3, xo[t], AF.Square, ["xo%d" % t], ["junk3", "scr%d" % c_], accum=scr[:, c_:c_ + 1])
                    rsqrt_act(scr[:, c_:c_ + 1], scr[:, c_:c_ + 1], 1024, ["scr%d" % c_], ["scr%d" % c_])
                    ts("dve", h2b[t % 2], xo[t], scr[:, c_:c_ + 1], None, ALU.mult, None, ["xo%d" % t, "scr%d" % c_], ["h2b"])
                    pb = bankb(0)
                    for k in range(8):
                        tp(pb[:, k * 128:(k + 1) * 128], h2b[t % 2][:, k * 128:(k + 1) * 128], ["h2b"], ["ps0"])
                    cp("act", h2T[:, :, t * 128:(t + 1) * 128], pb.rearrange("p (k n) -> p k n", k=8), ["ps0"], ["h2T"])
                for f in range(NF):
                    wi = f % 3
                    dma(wgu[wi].rearrange("p a k n -> p (a k n)"), wgu_scr[f, :, :], r=["wgu_scr"], w=["wgu%d" % wi], grp="dwgu%d" % wi,
                        eng="pool" if wi == 1 else "sp")
                    bg = 1 + 2 * (f % 2)
                    for a_ in range(2):
                        for k in range(8):
                            mm(bank(bg + a_), wgu[wi][:, a_, k, :], h2T[:, k, :], k == 0, k == 7, ["wgu%d" % wi, "h2T"], ["ps%d" % (bg + a_)])
                    Gb = G1 if f % 2 == 0 else G2
                    gk = "G%d" % (f % 2)
                    act(Gb, bank(bg), AF.Silu, ["ps%d" % bg], [gk])
                    tt("dve", actT[:, f, :], bank(bg + 1), Gb, ALU.mult, ["ps%d" % (bg + 1), gk], ["actT"])
                for t in range(4):
                    ot = Gq * 4 + t
                    for nh in range(2):
                        bi = 5 + nh
                        for f in range(NF):
                            mm(bank(bi), actT[:, f, t * 128:(t + 1) * 128], wdn[:, f, nh * 512:(nh + 1) * 512], f == 0, f == NF - 1,
                               ["actT", "wdn"], ["ps%d" % bi])
                        tt("dve", yo[nh], bank(bi), xo[t][:, nh * 512:(nh + 1) * 512], ALU.add, ["ps%d" % bi, "xo%d" % t], ["yo%d" % nh])
                        dma(yd[ot * 128:(ot + 1) * 128, nh * 512:(nh + 1) * 512], yo[nh], r=["yo%d" % nh], grp="dyo%d" % nh)
        except _Stop:
            pass
        if debug:
            P.barrier()
            dma(dbg, mixT.rearrange("p k n -> p (k n)"), r=["mixT"], grp="ddbg")
        P.barrier()
        P.run(nc, st)
    return nc


def _prep(inputs):
    x = np.asarray(inputs["x"], np.float32)
    mem = np.asarray(inputs["mem"], np.float32)
    pos = np.asarray(inputs["positions"], np.int32)
    g = lambda k: np.asarray(inputs[k], np.float32)[0]
    perm = np.concatenate([np.arange(384, 640), np.arange(640, 704), np.arange(1216, 1728), np.arange(1728, 2240),
                           np.arange(0, 384), np.arange(704, 1216), np.arange(2240, 2752), np.arange(2752, 3264)])
    w_in = np.ascontiguousarray(g("w_in")[:, perm])
    colmaj = lambda v, k: np.ascontiguousarray(v.reshape(k, 128).T)
    rep = lambda v: np.broadcast_to(v[None, :], (128, v.shape[0]))
    vec = np.zeros((128, NV), np.float32)
    vec[:, V_GMIX:V_GMIX + 8] = colmaj(g("norm_mix"), 8)
    vec[:, V_GMEM:V_GMEM + 8] = colmaj(g("norm_mem"), 8)
    vec[:, V_GFFN:V_GFFN + 8] = colmaj(g("norm_ffn"), 8)
    vec[:, V_GQA:V_GQA + 3] = colmaj(g("q_a_norm"), 3)
    vec[:, V_GKVA:V_GKVA + 2] = colmaj(g("kv_a_norm"), 2)
    vec[:, V_GOUT:V_GOUT + 4] = colmaj(g("mla_out_norm"), 4)
    vec[:, V_GOUT + 8:V_GOUT + 12] = colmaj(g("mem_out_norm"), 4)
    vec[:, V_GHGO:V_GHGO + 4] = colmaj(g("hg_out_norm"), 4)
    lbl = np.asarray(inputs["hg_lb_logits"], np.float32)
    vec[:, V_LBL:V_LBL + 4] = colmaj(lbl[0], 4)
    vec[:, V_LBL + 4:V_LBL + 8] = colmaj(lbl[1], 4)
    vec[:, V_GQ:V_GQ + 192] = rep(g("mla_q_norm"))
    vec[:, V_GK:V_GK + 192] = rep(g("mla_k_norm"))
    vec[:, V_GMQ:V_GMQ + 128] = rep(g("mem_q_norm"))
    vec[:, V_GMK:V_GMK + 128] = rep(g("mem_k_norm"))
    rm = np.ones(512, np.float32)
    rm[::64] = 0.0
    vec[:, V_RMASK:V_RMASK + 512] = rm[None, :]
    half = 32
    invf = (10000.0 ** (-np.arange(half, dtype=np.float64) / half)) / (2.0 * np.pi)
    vec[:, V_INVF:V_INVF + 32] = invf.astype(np.float32)[None, :]
    cb = np.zeros((128, 512), np.float32)
    cb[:, 0:128] = np.eye(128)
    cb[:, 128:256] = 1.0
    kk = np.arange(128)
    cb[:, 256:384] = (kk[None, :] >= kk[:, None])
    cb[:, 384:512] = (kk[None, :] >= kk[:, None]) & ((kk[None, :] // 64) == (kk[:, None] // 64))
    cbf = cb.astype(ml_dtypes.bfloat16)
    shared = dict(w_in=w_in, w_uq=g("w_uq"), w_ukv=g("w_ukv"), w_mkv=g("w_mem_kv"), w_out=g("w_out"),
                  w_gate=g("w_gate"), w_up=g("w_up"), w_down=g("w_down"), cbf=cbf)
    maps = []
    for c in range(8):
        b, j = c // 4, c % 4
        n = OWN * (j + 1)
        xsl = np.zeros((NS, 1024), np.float32)
        xsl[NS - n:] = x[b, :n]
        ps_ = np.zeros((NS,), np.int32)
        ps_[NS - n:] = pos[b, :n]
        v = vec.copy()
        for r_ in range(3):
            v[:, V_VFLAG + r_] = 1.0 if (r_ + 1) * OWN > NS - n else 0.0
        m = dict(shared)
        m.update(xs=xsl, pos=np.ascontiguousarray(ps_.reshape(NT, 128).T), mem=np.ascontiguousarray(mem[b]), vec=v)
        maps.append(m)
    return maps


_NC = {}


def kernel(**inputs):
    maps = _prep(inputs)
    if "nc" not in _NC:
        _NC["nc"] = build(False)
    res = run_bass_kernel_spmd(_NC["nc"], maps, core_ids=list(range(8)))
    out = np.zeros((2, 8192, 1024), np.float32)
    for c in range(8):
        b, j = c // 4, c % 4
        out[b, j * OWN:(j + 1) * OWN] = res.results[c]["y"]
    return out
```

```python
import contextlib
import math
import numpy as np
import ml_dtypes
import concourse.bass as bass
import concourse.mybir as mybir
from concourse.bass_utils import run_bass_kernel_spmd

F32 = mybir.dt.float32
BF16 = mybir.dt.bfloat16
I32 = mybir.dt.int32
AF = mybir.ActivationFunctionType
ALU = mybir.AluOpType
AX = mybir.AxisListType

EPS = 1e-6
NS = 8192
NT = NS // 128
NG = NS // 512
OWN = 2048
OWN0 = NS - OWN
OT0 = OWN0 // 128
OG0 = OWN0 // 512
DFF = 2816
NF = DFF // 128

C_KV, C_KR, C_HF, C_HI, C_CQ, C_HQ, C_HG, C_MQ = 0, 256, 320, 832, 1344, 1728, 2240, 2752

V_GMIX, V_GMEM, V_GFFN, V_GQA, V_GKVA, V_GOUT, V_GHGO, V_LBL, V_VFLAG = 0, 8, 16, 24, 27, 29, 41, 45, 53
V_GQ, V_GK, V_GMQ, V_GMK, V_RMASK, V_INVF = 57, 249, 441, 569, 697, 1209
NV = 1241


class Op:
    __slots__ = ("eng", "fn", "waits", "signal", "ticket", "chan", "cidx")


class Prog:
    ENG = ["pe", "act", "dve", "pool", "sp"]

    def __init__(self):
        self.ops = {e: [] for e in self.ENG}
        self.buf = {}
        self.waited = {e: {} for e in self.ENG}
        self.chan_ops = {}

    def add(self, eng, fn, reads=(), writes=(), dma=None):
        op = Op()
        op.eng = eng
        op.fn = fn
        op.signal = dma is not None
        op.ticket = None
        op.chan = dma if dma is not None else eng
        lst = self.chan_ops.setdefault(op.chan, [])
        op.cidx = len(lst)
        lst.append(op)
        deps = {}
        for k in reads:
            b = self.buf.setdefault(k, [None, []])
            d = b[0]
            if d is not None and (d.chan not in deps or deps[d.chan].cidx < d.cidx):
                deps[d.chan] = d
            if k.startswith("ps"):
                for d in b[1]:
                    if d.chan != op.chan and (d.chan not in deps or deps[d.chan].cidx < d.cidx):
                        deps[d.chan] = d
        for k in writes:
            b = self.buf.setdefault(k, [None, []])
            for d in ([b[0]] if b[0] is not None else []) + b[1]:
                if d.chan == op.chan and dma is None:
                    continue
                if d.chan not in deps or deps[d.chan].cidx < d.cidx:
                    deps[d.chan] = d
        op.waits = []
        w = self.waited[eng]
        for chan, d in deps.items():
            if chan == "pe" and eng == "pe":
                continue
            if w.get(chan, -1) >= d.cidx:
                continue
            w[chan] = d.cidx
            d.signal = True
            op.waits.append(d)
        for k in reads:
            self.buf[k][1].append(op)
        for k in writes:
            self.buf[k][0] = op
            self.buf[k][1] = []
        self.ops[eng].append(op)
        return op

    def barrier(self):
        chans = {c: l[-1] for c, l in self.chan_ops.items() if l}
        for e in self.ENG:
            w = self.waited[e]
            for chan, d in chans.items():
                if chan == e or w.get(chan, -1) >= d.cidx:
                    continue
                w[chan] = d.cidx
                d.signal = True
                op = Op()
                op.eng, op.fn, op.signal, op.ticket, op.chan, op.cidx = e, None, False, None, None, -1
                op.waits = [d]
                self.ops[e].append(op)

    def run(self, nc, stack):
        sems = {}
        for chan, lst in self.chan_ops.items():
            sems[chan] = stack.enter_context(nc.semaphore("s_" + chan))
            isdma = chan not in self.ENG
            cnt = 0
            for op in lst:
                if isdma:
                    cnt += 16
                    op.ticket = cnt
                elif op.signal:
                    cnt += 1
                    op.ticket = cnt
        block = stack.enter_context(nc.Block())

        def replay(eng, e):
            for op in self.ops[eng]:
                for d in op.waits:
                    e.wait_ge(sems[d.chan], d.ticket)
                if op.fn is None:
                    continue
                ins = op.fn(e)
                if op.signal:
                    ins.then_inc(sems[op.chan], 16 if op.chan not in self.ENG else 1)

        @block.tensor
        def _(e):
            replay("pe", e)

        @block.scalar
        def _(e):
            replay("act", e)

        @block.vector
        def _(e):
            replay("dve", e)

        @block.gpsimd
        def _(e):
            replay("pool", e)

        @block.sync
        def _(e):
            replay("sp", e)


class Arena:
    def __init__(self, t, ncols):
        self.t = t
        self.n = ncols
        self.off = 0

    def alloc(self, cols, dtype=BF16):
        mult = 1 if dtype == BF16 else 2
        self.off = (self.off + 1) // 2 * 2
        a = self.t[:, self.off:self.off + cols * mult]
        self.off += cols * mult
        assert self.off <= self.n, (self.off, self.n)
        return a if dtype == BF16 else a.bitcast(dtype)


class _Stop(Exception):
    pass


def build(debug=False, stop=99):
    nc = bass.Bass("TRN2", target_bir_lowering=False)

    def din(name, shape, dt=F32):
        return nc.dram_tensor(name, list(shape), dt, kind="ExternalInput").ap()

    xs = din("xs", [NS, 1024])
    posd = din("pos", [128, NT], I32)
    memd = din("mem", [256, 1024])
    vecd = din("vec", [128, NV])
    cbfd = din("cbf", [128, 512], BF16)
    w_in = din("w_in", [1024, 3264])
    w_uq = din("w_uq", [384, 768])
    w_ukv = din("w_ukv", [256, 1024])
    w_mkv = din("w_mkv", [1024, 1024])
    w_out = din("w_out", [1536, 1024])
    w_gate = din("w_gate", [1024, DFF])
    w_up = din("w_up", [1024, DFF])
    w_down = din("w_down", [DFF, 1024])
    yd = nc.dram_tensor("y", [OWN, 1024], F32, kind="ExternalOutput").ap()
    dbg = nc.dram_tensor("dbg", [128, 12 * OWN], BF16, kind="ExternalOutput").ap() if debug else None
    wgu_scr = nc.dram_tensor("wgu_scr", [NF, 128, 2048], BF16, kind="Internal").ap()
    mix_scr = nc.dram_tensor("mix_scr", [128, 8 * OWN], BF16, kind="Internal").ap()

    P = Prog()
    st = contextlib.ExitStack()
    with st:
        TOT = 106400
        arena_t = st.enter_context(nc.sbuf_tensor("arena", [128, TOT], BF16))
        A = Arena(arena_t, TOT)
        ps = st.enter_context(nc.psum_tensor("ps", [128, 4096], F32))

        def bank(i, n=512, off=0):
            return ps[:, i * 512 + off:i * 512 + off + n]

        def bankb(i, n=1024, off=0):
            return ps[:, i * 512:(i + 1) * 512].bitcast(BF16)[:, off:off + n]

        vec = A.alloc(NV, F32)
        cbf = A.alloc(512)
        ident = cbf[:, 0:128]
        onesb = cbf[:, 128:256]
        tri = cbf[:, 256:384]
        tri2 = cbf[:, 384:512]
        posi = A.alloc(NT, I32)
        posf = A.alloc(NT, F32)
        lb = A.alloc(4, F32)
        oml = A.alloc(4, F32)
        noml = A.alloc(4, F32)
        vones = A.alloc(3 * 128)
        stat = A.alloc(8 * NT, F32)
        rs1 = stat[:, 0:NT]
        rskv = stat[:, NT:2 * NT]
        sskr = stat[:, 2 * NT:3 * NT]
        rskv2 = stat[:, 3 * NT:4 * NT]
        ssmla = stat[:, 4 * NT:4 * NT + 16]
        ssmem = stat[:, 4 * NT + 16:4 * NT + 32]
        rsmla = stat[:, 4 * NT + 32:4 * NT + 48]
        rsmem = stat[:, 4 * NT + 48:4 * NT + 64]
        scr = A.alloc(64, F32)
        S32 = A.alloc(512, F32)
        Sb = A.alloc(512)
        dec = A.alloc(32, F32).rearrange("p (h c) -> p h c", h=4)
        mixT = A.alloc(12 * OWN).rearrange("p (k n) -> p k n", k=12)
        base_off = A.off
        ckvT = A.alloc(2 * NS).rearrange("p (k n) -> p k n", k=2)
        kr = A.alloc(NT * 64).rearrange("p (t d) -> p t d", d=64)
        wukv = A.alloc(2 * 1024).rearrange("p (k n) -> p k n", k=2)
        p1_off = A.off

        _dq = [0]

        def dma(out, in_, r=(), w=(), grp=None, eng=None):
            if eng is None:
                eng = "sp"
            if grp is None:
                _dq[0] += 1
                grp = "dq%d" % (_dq[0] % 12)
            P.add(eng, lambda e: e.dma_start(out=out, in_=in_), reads=r, writes=w, dma=grp)

        def act(out, in_, func, r, w, scale=1.0, bias=0.0, accum=None):
            if accum is None:
                P.add("act", lambda e: e.activation(out=out, in_=in_, func=func, scale=scale, bias=bias), reads=r, writes=w)
            else:
                P.add("act", lambda e: e.activation(out=out, in_=in_, func=func, scale=scale, bias=bias, accum_out=accum), reads=r, writes=w)

        def rsqrt_act(out, in_, n, r, w, bias=EPS):
            act(out, in_, AF.Ln, r, w, scale=1.0 / n, bias=bias)
            act(out, out, AF.Exp, w, w, scale=-0.5)

        def tt(eng, out, in0, in1, op, r, w):
            P.add(eng, lambda e: e.tensor_tensor(out=out, in0=in0, in1=in1, op=op), reads=r, writes=w)

        def ts(eng, out, in0, s1, s2, op0, op1, r, w):
            if s2 is None:
                P.add(eng, lambda e: e.tensor_scalar(out=out, in0=in0, scalar1=s1, scalar2=None, op0=op0), reads=r, writes=w)
            else:
                P.add(eng, lambda e: e.tensor_scalar(out=out, in0=in0, scalar1=s1, scalar2=s2, op0=op0, op1=op1), reads=r, writes=w)

        def stt(eng, out, in0, s, in1, op0, op1, r, w):
            P.add(eng, lambda e: e.scalar_tensor_tensor(out=out, in0=in0, scalar=s, in1=in1, op0=op0, op1=op1), reads=r, writes=w)

        def cp(eng, out, in_, r, w):
            if eng == "act":
                act(out, in_, AF.Copy, r, w)
            else:
                P.add(eng, lambda e: e.tensor_copy(out=out, in_=in_), reads=r, writes=w)

        def mm(out, lhsT, rhs, start, stop, r, w):
            P.add("pe", lambda e: e.matmul(out, lhsT=lhsT, rhs=rhs, start=start, stop=stop), reads=r, writes=w)

        def tp(out, in_, r, w, idn=None):
            idn = ident if idn is None else idn
            P.add("pe", lambda e: e.transpose(out=out, in_=in_, identity=idn), reads=list(r) + ["cbf"], writes=w)

        def sigmoid_(buf, src, r, key):
            act(buf, src, AF.Exp, r, [key], scale=-1.0)
            act(buf, buf, AF.Ln, [key], [key], bias=1.0)
            act(buf, buf, AF.Exp, [key], [key], scale=-1.0)

        def ck(level):
            if stop == level:
                raise _Stop()

        try:
            dma(vec, vecd, w=["vec"])
            dma(cbf, cbfd, w=["cbf"])
            dma(posi, posd, w=["posi"])
            cp("dve", posf, posi, ["posi"], ["posf"])
            lbl = vec[:, V_LBL:V_LBL + 8].rearrange("p (r h) -> p r h", r=2)
            tt("dve", lb, lbl[:, 0, :], lbl[:, 1, :], ALU.subtract, ["vec"], ["lb"])
            sigmoid_(lb, lb, ["lb"], "lb")
            ts("dve", oml, lb, -1.0, 1.0, ALU.mult, ALU.add, ["lb"], ["oml"])
            ts("dve", noml, oml, -1.0, None, ALU.mult, None, ["oml"], ["oml"])
            for r_ in range(3):
                ts("dve", vones[:, r_ * 128:(r_ + 1) * 128], onesb, vec[:, V_VFLAG + r_:V_VFLAG + r_ + 1], None, ALU.mult, None,
                   ["cbf", "vec"], ["vones"])
            P.add("dve", lambda e: e.memset(S32, 0.0), writes=["S32"])
            P.add("dve", lambda e: e.memset(Sb, 0.0), writes=["Sb"])
            P.add("dve", lambda e: e.memset(stat, 0.0), writes=["stat"])

            ck(1)
            NA = 2368
            win = A.alloc(8 * NA).rearrange("p (k n) -> p k n", k=8)
            wl_off = A.off
            stg = [A.alloc(1024, F32) for _ in range(6)]
            _wl = [0]

            def load_w(dst, src, ncols, scale, dkey):
                ns = len(stg)
                cw = stg[0].shape[1]
                for c0 in range(0, ncols, cw):
                    n = min(cw, ncols - c0)
                    i = _wl[0] % ns
                    _wl[0] += 1
                    dma(stg[i][:, 0:n], src[:, c0:c0 + n], w=["stg%d" % i], grp="dstg%d" % i, eng=("sp", "pool", "act")[i % 3])
                    if i % 2 == 0:
                        if scale is None:
                            cp("dve", dst[:, c0:c0 + n], stg[i][:, 0:n], ["stg%d" % i], [dkey])
                        else:
                            ts("dve", dst[:, c0:c0 + n], stg[i][:, 0:n], scale, None, ALU.mult, None, ["stg%d" % i, "vec"], [dkey])
                    else:
                        act(dst[:, c0:c0 + n], stg[i][:, 0:n], AF.Copy, ["stg%d" % i, "vec"], [dkey], scale=(1.0 if scale is None else scale))

            w_in_v = w_in.rearrange("(k p) n -> p k n", p=128)
            for k in range(8):
                load_w(win[:, k, 0:1344], w_in_v[:, k, 0:1344], 1344, vec[:, V_GMIX + k:V_GMIX + k + 1], "win")
                load_w(win[:, k, 1344:NA], w_in_v[:, k, C_HQ:C_HQ + 1024], 1024, vec[:, V_GMIX + k:V_GMIX + k + 1], "win")
            w_ukv_v = w_ukv.rearrange("(k p) n -> p k n", p=128)
            for k in range(2):
                load_w(wukv[:, k, :], w_ukv_v[:, k, :], 1024, vec[:, V_GKVA + k:V_GKVA + k + 1], "wukv")
            A_HQ, A_HG = 1344, 1344 + 512
            A.off = wl_off
            W1b = A.alloc(512, F32)
            W2b = A.alloc(512, F32)

            xt = [A.alloc(1024, F32) for _ in range(2)]
            junk = A.alloc(1024)
            hb = [A.alloc(1024)]
            hT = A.alloc(8 * 512).rearrange("p (k n) -> p k n", k=8)
            sqt = A.alloc(2 * 512).rearrange("p (k n) -> p k n", k=2)
            W1 = A.alloc(512, F32)
            W2 = A.alloc(512, F32)
            W3 = A.alloc(512, F32)
            W4 = A.alloc(512, F32)
            W3b = A.alloc(512, F32)
            W4b = A.alloc(512, F32)
            kT = A.alloc(4 * 512).rearrange("p (h n) -> p h n", h=4)
            qT = A.alloc(4 * 512).rearrange("p (h n) -> p h n", h=4)
            sgT = A.alloc(4 * 512).rearrange("p (h n) -> p h n", h=4)
            ktok = A.alloc(4 * 4 * 128).rearrange("p (t h d) -> p t h d", t=4, h=4)
            vtok = A.alloc(4 * 512).rearrange("p (t n) -> p t n", t=4)
            Amb = A.alloc(4 * 128).rearrange("p (h n) -> p h n", h=4)
            krg = A.alloc(4 * 64, F32).rearrange("p (t d) -> p t d", d=64)
            rp = [A.alloc(4 * 32, F32).rearrange("p (t d) -> p t d", d=32) for _ in range(4)]
            cst = A.alloc(4 * 32, F32).rearrange("p (t d) -> p t d", d=32)
            snt = A.alloc(4 * 32, F32).rearrange("p (t d) -> p t d", d=32)
            tA = A.alloc(128, F32)
            tB = A.alloc(128, F32)
            tI = A.alloc(128, I32)
            pm = [A.alloc(512)]
            rmask = vec[:, V_RMASK:V_RMASK + 512]
            invf = vec[:, V_INVF:V_INVF + 32]

            def cs_tables(g4):
                tt("dve", tA.rearrange("p (t d) -> p t d", d=32), posf[:, g4].unsqueeze(2).to_broadcast([128, 4, 32]),
                   invf.unsqueeze(1).to_broadcast([128, 4, 32]), ALU.mult, ["posf", "vec"], ["tA"])
                for (shift, dst, dkey) in ((0.0, snt, "snt"), (0.25, cst, "cst")):
                    dflat = dst.rearrange("p t d -> p (t d)")
                    ts("dve", tB, tA, shift, None, ALU.add, None, ["tA"], ["tB"])
                    cp("dve", tI, tB, ["tB"], ["tI"])
                    cp("dve", dflat, tI, ["tI"], [dkey])
                    tt("dve", tB, tB, dflat, ALU.subtract, ["tB", dkey], ["tB"])
                    ts("dve", dflat, tB, 0.5, None, ALU.is_gt, None, ["tB"], [dkey])
                    tt("dve", tB, tB, dflat, ALU.subtract, ["tB", dkey], ["tB"])
                    ts("dve", dflat, tB, -0.5, None, ALU.is_lt, None, ["tB"], [dkey])
                    tt("dve", tB, tB, dflat, ALU.add, ["tB", dkey], ["tB"])
                    act(dflat, tB, AF.Sin, ["tB"], [dkey], scale=2.0 * math.pi)

            def rope(dst_lo, dst_hi, src_lo, src_hi, cs, sn, rkeys, wkeys):
                tt("dve", rp[0], src_lo, cs, ALU.mult, rkeys, ["rp0"])
                tt("dve", rp[1], src_hi, sn, ALU.mult, rkeys, ["rp1"])
                tt("dve", dst_lo, rp[0], rp[1], ALU.subtract, ["rp0", "rp1"], wkeys)
                tt("dve", rp[2], src_hi, cs, ALU.mult, rkeys, ["rp2"])
                tt("dve", rp[3], src_lo, sn, ALU.mult, rkeys, ["rp3"])
                tt("dve", dst_hi, rp[2], rp[3], ALU.add, ["rp2", "rp3"], wkeys)

            def rms_load_transpose(src_rows, xbuf, xkey, hbuf, hkey, rcol, rkey, dstT, dkey, tsl, eng_q):
                dma(xbuf, src_rows, w=[xkey], grp="d" + xkey, eng=eng_q)
                act(junk, xbuf, AF.Square, [xkey], ["junk", rkey], accum=rcol)
                rsqrt_act(rcol, rcol, 1024, [rkey], [rkey])
                ts("dve", hbuf, xbuf, rcol, None, ALU.mult, None, [xkey, rkey], [hkey])
                pb = bankb(0)
                for k in range(8):
                    tp(pb[:, k * 128:(k + 1) * 128], hbuf[:, k * 128:(k + 1) * 128], [hkey], ["ps0"])
                cp("act", dstT[:, :, tsl], pb.rearrange("p (k n) -> p k n", k=8), ["ps0"], [dkey])

            def fproj(col0, bnk):
                for k in range(8):
                    mm(bank(bnk), win[:, k, col0:col0 + 128], hT[:, k, :], k == 0, k == 7, ["hT", "win"], ["ps%d" % bnk])

            P.barrier()
            ck(2)
            mA = [mixT[:, r_, :] for r_ in range(4)]
            mB = [mixT[:, 8 + r_, :] for r_ in range(4)]
            kT_ = [kT, mA[0].rearrange("p (h n) -> p h n", h=4)]
            qT_ = [qT, mA[1].rearrange("p (h n) -> p h n", h=4)]
            sgT_ = [sgT, mA[2].rearrange("p (h n) -> p h n", h=4)]
            ktok_ = [ktok, mA[3].rearrange("p (t h d) -> p t h d", t=4, h=4)]
            vtok_ = [vtok, mB[0].rearrange("p (t n) -> p t n", t=4)]
            hT_ = [hT, mixT[:, 9:11, :].rearrange("p a n -> p (a n)").rearrange("p (k n) -> p k n", k=8)]
            RW3 = mB[3][:, 0:1024].bitcast(F32)
            RW1 = mB[3][:, 1024:2048].bitcast(F32)
            dec_ = [dec, A.alloc(32, F32).rearrange("p (h c) -> p h c", h=4)]
            Wsets = [(W1, W2, W3, W4, "a"), (W1b, W2b, W3b, W4b, "b")]
            FM = [1, 2, 3]
            _fm = [0]

            def fm_next():
                bk = FM[_fm[0] % 3]
                _fm[0] += 1
                return bk

            def fprojp(col0, p_):
                bk = fm_next()
                for k in range(8):
                    mm(bank(bk), win[:, k, col0:col0 + 128], hT_[p_][:, k, :], k == 0, k == 7, ["hT%d" % p_, "win"], ["ps%d" % bk])
                return bk

            def front(g):
                p_ = g % 2
                for t in range(4):
                    gt = 4 * g + t
                    xb, xk = xt[gt % 2], "xt%d" % (gt % 2)
                    rcol = rs1[:, gt:gt + 1]
                    dma(xb, xs[gt * 128:(gt + 1) * 128, :], w=[xk], grp="d" + xk, eng="sp" if gt % 2 == 0 else "pool")
                    act(junk, xb, AF.Square, [xk], ["junk", "rs1"], accum=rcol)
                    rsqrt_act(rcol, rcol, 1024, ["rs1"], ["rs1"])
                    ts("dve", hb[0], xb, rcol, None, ALU.mult, None, [xk, "rs1"], ["hb0"])
                    yield
                    pb = bankb(0)
                    for k in range(8):
                        tp(pb[:, k * 128:(k + 1) * 128], hb[0][:, k * 128:(k + 1) * 128], ["hb0"], ["ps0"])
                    cp("act", hT_[p_][:, :, t * 128:(t + 1) * 128], pb.rearrange("p (k n) -> p k n", k=8), ["ps0"], ["hT%d" % p_])
                    yield

            def f_chain(h, bk, p_):
                Wa, Wb, Wc, Wd, sfx = Wsets[h % 2]
                k1, k2, k3, k4 = "W1" + sfx, "W2" + sfx, "W3" + sfx, "W4" + sfx
                pk = "ps%d" % bk
                sigmoid_(Wa, bank(bk), [pk], k1)
                act(Wb, Wa, AF.Ln, [k1, "lb", "oml"], [k2], scale=oml[:, h:h + 1], bias=lb[:, h:h + 1])
                P.add("dve", lambda e, Wc=Wc, Wb=Wb, rmask=rmask: e.tensor_tensor_scan(out=Wc, data0=rmask, data1=Wb, initial=0.0,
                                                                                       op0=ALU.mult, op1=ALU.add),
                      reads=[k2, "vec"], writes=[k3])
                ts("dve", Wa, Wa, noml[:, h:h + 1], oml[:, h:h + 1], ALU.mult, ALU.add, [k1, "oml"], [k1])
                act(Wb, Wc, AF.Exp, [k3], [k2], scale=-1.0)
                act(Wd, Wc, AF.Exp, [k3], [k4])
                cp("dve", dec_[p_][:, h, :], Wd.rearrange("p (c t) -> p c t", t=64)[:, :, 63], [k4], ["dec%d" % p_])
                tt("dve", kT_[p_][:, h, :], Wa, Wb, ALU.mult, [k1, k2], ["kT%d" % p_])

            def kT_transpose(h, p_):
                pb = bankb(0)
                for t in range(4):
                    tp(pb[:, t * 128:(t + 1) * 128], kT_[p_][:, h, t * 128:(t + 1) * 128], ["kT%d" % p_], ["ps0"])
                cp("act", ktok_[p_][:, :, h, :], pb[:, 0:512].rearrange("p (t d) -> p t d", t=4), ["ps0"], ["ktok%d" % p_])

            def qg_chain(h, bq, bg_, p_):
                Wa, Wb, Wc, Wd, sfx = Wsets[h % 2]
                k1, k4 = "W1" + sfx, "W4" + sfx
                sigmoid_(Wa, bank(bq), ["ps%d" % bq], k1)
                tt("dve", Wa, bank(bq), Wa, ALU.mult, ["ps%d" % bq, k1], [k1])
                stt("dve", qT_[p_][:, h, :], Wa, 128 ** -0.5, Wd, ALU.mult, ALU.mult, [k1, k4], ["qT%d" % p_])
                sigmoid_(Wa, bank(bg_), ["ps%d" % bg_], k1)
                tt("dve", sgT_[p_][:, h, :], bank(bg_), Wa, ALU.mult, ["ps%d" % bg_, k1], ["sgT%d" % p_])

            def proj(g, own):
                p_ = g % 2
                hk = "hT%d" % p_
                gsl = slice(g * 512, (g + 1) * 512)
                g4 = slice(4 * g, 4 * g + 4)
                if not own:
                    bf = [fprojp(C_HF + h * 128, p_) for h in range(2)]
                    f_chain(0, bf[0], p_)
                    yield
                    bf.append(fprojp(C_HF + 2 * 128, p_))
                    f_chain(1, bf[1], p_)
                    yield
                    kT_transpose(0, p_)
                    bf.append(fprojp(C_HF + 3 * 128, p_))
                    f_chain(2, bf[2], p_)
                    yield
                    kT_transpose(1, p_)
                    f_chain(3, bf[3], p_)
                    yield
                    kT_transpose(2, p_)
                    yield
                    kT_transpose(3, p_)
                    yield
                else:
                    for h in range(4):
                        bf_ = fprojp(C_HF + h * 128, p_)
                        bq = fprojp(A_HQ + h * 128, p_)
                        f_chain(h, bf_, p_)
                        yield
                        bg_ = fprojp(A_HG + h * 128, p_)
                        qg_chain(h, bq, bg_, p_)
                        yield
                        kT_transpose(h, p_)
                        yield
                for t in range(4):
                    bk = fm_next()
                    for k in range(8):
                        mm(bank(bk), hT_[p_][:, k, t * 128:(t + 1) * 128], win[:, k, C_HI:C_HI + 512], k == 0, k == 7, [hk, "win"], ["ps%d" % bk])
                    cp("dve", vtok_[p_][:, t, :], bank(bk), ["ps%d" % bk], ["vtok%d" % p_])
                    yield
                bc = [fprojp(C_KV + m * 128, p_) for m in range(2)]
                for m in range(2):
                    bk = bc[m]
                    cp("act", ckvT[:, m, gsl], bank(bk), ["ps%d" % bk], ["ckvT"])
                    act(sqt[:, m, :], bank(bk), AF.Square, ["ps%d" % bk], ["sqt"])
                yield
                for t in range(4):
                    for m in range(2):
                        mm(bank(4, 1, 256 + t), sqt[:, m, t * 128:(t + 1) * 128], onesb[:, 0:1], m == 0, m == 1, ["sqt", "cbf"], ["ps4"])
                rsqrt_act(rskv[:, g4], bank(4, 4, 256), 256, ["ps4"], ["rskv"])
                tt("dve", rskv2[:, g4], rskv[:, g4], rskv[:, g4], ALU.mult, ["rskv"], ["rskv2"])
                yield
                for t in range(4):
                    for k in range(8):
                        mm(bank(4, 64, t * 64), hT_[p_][:, k, t * 128:(t + 1) * 128], win[:, k, C_KR:C_KR + 64], k == 0, k == 7,
                           [hk, "win"], ["ps4"])
                for t in range(4):
                    act(junk[:, 0:64], bank(4, 64, t * 64), AF.Square, ["ps4"], ["junk", "sskr"], accum=sskr[:, 4 * g + t:4 * g + t + 1])
                yield
                for t in range(4):
                    tt("dve", krg[:, t, :], bank(4, 64, t * 64), vec[:, V_GK + 128:V_GK + 192], ALU.mult, ["ps4", "vec"], ["krg"])
                yield
                cs_tables(g4)
                yield
                rope(kr[:, g4, 0:32], kr[:, g4, 32:64], krg[:, :, 0:32], krg[:, :, 32:64], cst, snt, ["krg", "cst", "snt"], ["kr"])
                yield

            def rec(g, own):
                p_ = g % 2
                kTp, qTp, sgTp, ktokp, vtokp, decp = kT_[p_], qT_[p_], sgT_[p_], ktok_[p_], vtok_[p_], dec_[p_]
                kk, qk, sgk, ktk, vtk, dk = ["%s%d" % (n_, p_) for n_ in ("kT", "qT", "sgT", "ktok", "vtok", "dec")]
                osl0 = (g - OG0) * 512
                for t in range(4):
                    tsl = slice(t * 128, (t + 1) * 128)
                    if own:
                        for h in range(4):
                            mm(bank(6, 128, h * 128), kTp[:, h, tsl], qTp[:, h, tsl], True, True, [kk, qk], ["ps6"])
                        tt("dve", Amb, bank(6).rearrange("p (h n) -> p h n", h=4), tri2.unsqueeze(1).to_broadcast([128, 4, 128]),
                           ALU.mult, ["ps6", "cbf"], ["Amb"])
                    for c2 in range(2):
                        rows = slice(c2 * 64, (c2 + 1) * 64)
                        csl = slice(t * 128 + c2 * 64, t * 128 + (c2 + 1) * 64)
                        ci = t * 2 + c2
                        if own:
                            for h in range(4):
                                mm(bank(7, 64, h * 128 + c2 * 64), vtokp[rows, t, h * 128:(h + 1) * 128], Amb[rows, h, rows], True, False,
                                   [vtk, "Amb"], ["ps7"])
                                mm(bank(7, 64, h * 128 + c2 * 64), Sb[:, h * 128:(h + 1) * 128], qTp[:, h, csl], False, True,
                                   ["Sb", qk], ["ps7"])
                        for h in range(4):
                            mm(bank(5, 128, h * 128), ktokp[rows, t, h, :], vtokp[rows, t, h * 128:(h + 1) * 128], True, True,
                               [ktk, vtk], ["ps5"])
                        tt("dve", RW3, S32, bank(5), ALU.add, ["S32", "ps5"], ["RW3"])
                        tt("dve", S32.rearrange("p (h n) -> p h n", h=4), RW3.rearrange("p (h n) -> p h n", h=4),
                           decp[:, :, ci:ci + 1].to_broadcast([128, 4, 128]), ALU.mult, ["RW3", dk], ["S32"])
                        if own or (g == OG0 - 1 and ci == 7):
                            cp("act", Sb, S32, ["S32"], ["Sb"])
                        yield
                    if own:
                        act(pm[0], bank(7), AF.Square, ["ps7"], ["pm0"])
                        mm(bank(6), onesb, pm[0], True, True, ["cbf", "pm0"], ["ps6"])
                        rsqrt_act(RW1, bank(6), 128, ["ps6"], ["RW1"])
                        tt("dve", RW1, RW1, bank(7), ALU.mult, ["RW1", "ps7"], ["RW1"])
                        for h in range(4):
                            stt("dve", mixT[:, 4 + h, osl0 + t * 128:osl0 + (t + 1) * 128], RW1[:, h * 128:(h + 1) * 128],
                                vec[:, V_GHGO + h:V_GHGO + h + 1], sgTp[:, h, tsl], ALU.mult, ALU.mult, ["RW1", "vec", sgk], ["mixT"])

            def interleave(gens):
                gens = [g_ for g_ in gens if g_ is not None]
                while gens:
                    for g_ in list(gens):
                        try:
                            next(g_)
                        except StopIteration:
                            gens.remove(g_)

            interleave([front(0)])
            interleave([proj(0, False), front(1)])
            for g in range(NG):
                if g == 1:
                    ck(3)
                if g == OG0 + 1:
                    ck(35)
                interleave([rec(g, g >= OG0),
                            proj(g + 1, g + 1 >= OG0) if g + 1 < NG else None,
                            front(g + 2) if g + 2 < NG else None])

            P.barrier()
            ck(4)
            A.off = p1_off
            qTn = A.alloc(4 * OWN).rearrange("p (h n) -> p h n", h=4)
            qTr = A.alloc(2 * OWN).rearrange("p (h n) -> p h n", h=2)
            p2_off = A.off
            NB = 896
            win = A.alloc(8 * NB).rearrange("p (k n) -> p k n", k=8)
            wuq = A.alloc(3 * 768).rearrange("p (k n) -> p k n", k=3)
            kmemT = A.alloc(4 * 256).rearrange("p (h n) -> p h n", h=4)
            vmem = A.alloc(2 * 512).rearrange("p (t n) -> p t n", t=2)
            xt = [A.alloc(1024, F32) for _ in range(2)]
            junk = A.alloc(1024)
            hb = [A.alloc(1024) for _ in range(2)]
            W1 = A.alloc(512, F32)
            W2 = A.alloc(512, F32)
            p1b_off = A.off
            stg = [A.alloc(1024, F32) for _ in range(3)]
            wmkv = A.alloc(8 * 1024).rearrange("p (k n) -> p k n", k=8)
            memT = A.alloc(8 * 256).rearrange("p (k n) -> p k n", k=8)
            kmtok = A.alloc(512).rearrange("p (h d) -> p h d", h=4)
            for k in range(8):
                load_w(win[:, k, 0:384], w_in_v[:, k, C_CQ:C_CQ + 384], 384, vec[:, V_GMIX + k:V_GMIX + k + 1], "win")
                load_w(win[:, k, 384:NB], w_in_v[:, k, C_MQ:C_MQ + 512], 512, vec[:, V_GMIX + k:V_GMIX + k + 1], "win")
            w_uq_v = w_uq.rearrange("(k p) n -> p k n", p=128)
            for k in range(3):
                load_w(wuq[:, k, :], w_uq_v[:, k, :], 768, vec[:, V_GQA + k:V_GQA + k + 1], "wuq")
            w_mkv_v = w_mkv.rearrange("(k p) n -> p k n", p=128)
            for k in range(8):
                load_w(wmkv[:, k, :], w_mkv_v[:, k, :], 1024, vec[:, V_GMEM + k:V_GMEM + k + 1], "wmkv")
            for mt in range(2):
                rms_load_transpose(memd[mt * 128:(mt + 1) * 128, :], xt[mt], "xt%d" % mt, hb[mt], "hb%d" % mt,
                                   scr[:, mt:mt + 1], "scr%d" % mt, memT, "memT", slice(mt * 128, (mt + 1) * 128), "sp" if mt == 0 else "pool")
            for mt in range(2):
                for half in range(2):
                    for k in range(8):
                        mm(bank(1 + half), memT[:, k, mt * 128:(mt + 1) * 128], wmkv[:, k, half * 512:(half + 1) * 512],
                           k == 0, k == 7, ["memT", "wmkv"], ["ps%d" % (1 + half)])
                kps = bank(1).rearrange("p (h d) -> p h d", h=4)
                act(W1, bank(1), AF.Square, ["ps1"], ["W1"])
                P.add("dve", lambda e: e.tensor_reduce(out=scr[:, 8:12], in_=W1.rearrange("p (h d) -> p h d", h=4), axis=AX.X, op=ALU.add),
                      reads=["W1"], writes=["scr8"])
                rsqrt_act(scr[:, 8:12], scr[:, 8:12], 128, ["scr8"], ["scr8"])
                W1v = W1.rearrange("p (h d) -> p h d", h=4)
                tt("dve", W1v, kps, scr[:, 8:12].unsqueeze(2).to_broadcast([128, 4, 128]), ALU.mult, ["ps1", "scr8"], ["W1"])
                tt("dve", kmtok, W1v, vec[:, V_GMK:V_GMK + 128].unsqueeze(1).to_broadcast([128, 4, 128]), ALU.mult,
                   ["W1", "vec"], ["kmtok"])
                pb = bankb(0)
                for h in range(4):
                    tp(pb[:, h * 128:(h + 1) * 128], kmtok[:, h, :], ["kmtok"], ["ps0"])
                cp("act", kmemT[:, :, mt * 128:(mt + 1) * 128], pb[:, 0:512].rearrange("p (h n) -> p h n", h=4), ["ps0"], ["kmemT"])
                cp("act", vmem[:, mt, :], bank(2), ["ps2"], ["vmem"])
            P.barrier()
            ck(5)
            A.off = p1b_off
            hT = A.alloc(8 * 512).rearrange("p (k n) -> p k n", k=8)
            sqt = A.alloc(3 * 512).rearrange("p (k n) -> p k n", k=3)
            cqT = A.alloc(3 * 512).rearrange("p (k n) -> p k n", k=3)
            rp = [A.alloc(4 * 32, F32).rearrange("p (t d) -> p t d", d=32) for _ in range(4)]
            cst = A.alloc(4 * 32, F32).rearrange("p (t d) -> p t d", d=32)
            snt = A.alloc(4 * 32, F32).rearrange("p (t d) -> p t d", d=32)
            tA = A.alloc(128, F32)
            tB = A.alloc(128, F32)
            tI = A.alloc(128, I32)
            qn = A.alloc(768, F32).rearrange("p (h d) -> p h d", h=4)
            qbn = A.alloc(4 * 128).rearrange("p (h d) -> p h d", h=4)
            qbr = A.alloc(4 * 64).rearrange("p (h d) -> p h d", h=4)
            mqT = A.alloc(4 * 512).rearrange("p (h n) -> p h n", h=4)
            pm = [A.alloc(512) for _ in range(2)]
            sqy = A.alloc(4 * 512).rearrange("p (h n) -> p h n", h=4)
            qn_ = [qn, A.alloc(768, F32).rearrange("p (h d) -> p h d", h=4)]
            qbn_ = [qbn, A.alloc(4 * 128).rearrange("p (h d) -> p h d", h=4)]
            qbr_ = [qbr, A.alloc(4 * 64).rearrange("p (h d) -> p h d", h=4)]
            mqb = A.alloc(4 * 128).rearrange("p (h d) -> p h d", h=4)

            def fb(g):
                for t in range(4):
                    gt = 4 * g + t
                    rms_load_transpose(xs[gt * 128:(gt + 1) * 128, :], xt[gt % 2], "xt%d" % (gt % 2), hb[gt % 2], "hb%d" % (gt % 2),
                                       scr[:, 2 + t:3 + t], "scr%d" % (2 + t), hT, "hT", slice(t * 128, (t + 1) * 128), "sp" if gt % 2 == 0 else "pool")
                    yield

            def qstream(g):
                g4 = slice(4 * g, 4 * g + 4)
                cs_tables(g4)
                yield
                for m in range(3):
                    b_ = 1 + (m % 2)
                    fproj(m * 128, b_)
                    cp("act", cqT[:, m, :], bank(b_), ["ps%d" % b_], ["cqT"])
                    act(sqt[:, m, :], bank(b_), AF.Square, ["ps%d" % b_], ["sqt"])
                yield
                for t in range(4):
                    for m in range(3):
                        mm(bank(4, 1, t), sqt[:, m, t * 128:(t + 1) * 128], onesb[:, 0:1], m == 0, m == 2, ["sqt", "cbf"], ["ps4"])
                ts("dve", scr[:, 16:20], bank(4, 4), EPS / 384.0, EPS * EPS, ALU.mult, ALU.add, ["ps4"], ["scr16"])
                yield
                for t in range(4):
                    q_ = t % 2
                    gt = 4 * g + t
                    tsl = slice(t * 128, (t + 1) * 128)
                    qb0 = 6 if q_ == 0 else 2
                    pk = ["ps%d" % qb0, "ps%d" % (qb0 + 1)]
                    qnq, qbnq, qbrq = qn_[q_], qbn_[q_], qbr_[q_]
                    nk, bnk, brk = "qn%d" % q_, "qbn%d" % q_, "qbr%d" % q_
                    sc0 = 20 + 4 * q_
                    sck = "scr%d" % sc0
                    scq = scr[:, sc0:sc0 + 4]
                    for (c0, n, off) in ((0, 512, 0), (512, 256, 512)):
                        for m in range(3):
                            mm(ps[:, qb0 * 512 + off:qb0 * 512 + off + n], cqT[:, m, tsl], wuq[:, m, c0:c0 + n], m == 0, m == 2,
                               ["cqT", "wuq"], [pk[off // 512]])
                    qps = ps[:, qb0 * 512:qb0 * 512 + 768]
                    qps3 = qps.rearrange("p (h d) -> p h d", h=4)
                    act(qnq.rearrange("p h d -> p (h d)"), qps, AF.Square, pk, [nk])
                    P.add("dve", lambda e, scq=scq, qnq=qnq: e.tensor_reduce(out=scq, in_=qnq, axis=AX.X, op=ALU.add), reads=[nk], writes=[sck])
                    ts("dve", scq, scq, 1.0 / 192, scr[:, 16 + t:17 + t], ALU.mult, ALU.add, [sck, "scr16"], [sck])
                    yield
                    act(scq, scq, AF.Ln, [sck], [sck])
                    act(scq, scq, AF.Exp, [sck], [sck], scale=-0.5)
                    tt("dve", qnq, qps3, scq.unsqueeze(2).to_broadcast([128, 4, 192]), ALU.mult, pk + [sck], [nk])
                    tt("dve", qnq, qnq, vec[:, V_GQ:V_GQ + 192].unsqueeze(1).to_broadcast([128, 4, 192]), ALU.mult, [nk, "vec"], [nk])
                    yield
                    cp("act", qbnq, qnq[:, :, 0:128], [nk], [bnk])
                    cs = cst[:, t, :].unsqueeze(1).to_broadcast([128, 4, 32])
                    sn = snt[:, t, :].unsqueeze(1).to_broadcast([128, 4, 32])
                    rope(qbrq[:, :, 0:32], qbrq[:, :, 32:64], qnq[:, :, 128:160], qnq[:, :, 160:192], cs, sn, [nk, "cst", "snt"], [brk])
                    yield
                    pb = bankb(0)
                    for h in range(4):
                        tp(pb[:, h * 128:(h + 1) * 128], qbnq[:, h, :], [bnk], ["ps0"])
                    qbrf = qbrq.rearrange("p h d -> p (h d)")
                    for pr in range(2):
                        tp(pb[:, 512 + pr * 128:512 + (pr + 1) * 128], qbrf[:, pr * 128:(pr + 1) * 128], [brk], ["ps0"])
                    osl_t = slice((g - OG0) * 512 + t * 128, (g - OG0) * 512 + (t + 1) * 128)
                    cp("act", qTn[:, :, osl_t], pb[:, 0:512].rearrange("p (h n) -> p h n", h=4), ["ps0"], ["qTn"])
                    cp("act", qTr[:, :, osl_t], pb[:, 512:768].rearrange("p (h n) -> p h n", h=2), ["ps0"], ["qTr"])
                    yield

            def mstream(g):
                for t in range(4):
                    tsl = slice(t * 128, (t + 1) * 128)
                    for k in range(8):
                        mm(bank(5), hT[:, k, tsl], win[:, k, 384:NB], k == 0, k == 7, ["hT", "win"], ["ps5"])
                    act(W1, bank(5), AF.Square, ["ps5"], ["W1"])
                    P.add("dve", lambda e: e.tensor_reduce(out=scr[:, 28:32], in_=W1.rearrange("p (h d) -> p h d", h=4), axis=AX.X, op=ALU.add),
                          reads=["W1"], writes=["scr28"])
                    rsqrt_act(scr[:, 28:32], scr[:, 28:32], 128, ["scr28"], ["scr28"])
                    yield
                    W1v = W1.rearrange("p (h d) -> p h d", h=4)
                    tt("dve", W1v, bank(5).rearrange("p (h d) -> p h d", h=4), scr[:, 28:32].unsqueeze(2).to_broadcast([128, 4, 128]),
                       ALU.mult, ["ps5", "scr28"], ["W1"])
                    tt("dve", mqb, W1v, vec[:, V_GMQ:V_GMQ + 128].unsqueeze(1).to_broadcast([128, 4, 128]), ALU.mult, ["W1", "vec"], ["mqb"])
                    yield
                    pb = bankb(5)
                    for h in range(4):
                        tp(pb[:, h * 128:(h + 1) * 128], mqb[:, h, :], ["mqb"], ["ps5"])
                    cp("act", mqT[:, :, tsl], pb[:, 0:512].rearrange("p (h n) -> p h n", h=4), ["ps5"], ["mqT"])
                    yield

            def ystream(g):
                osl = slice((g - OG0) * 512, (g - OG0 + 1) * 512)
                for h in range(4):
                    for mt in range(2):
                        mm(bank(1 + mt), kmemT[:, h, mt * 128:(mt + 1) * 128], mqT[:, h, :], True, True, ["kmemT", "mqT"], ["ps%d" % (1 + mt)])
                        act(pm[mt], bank(1 + mt), AF.Exp, ["ps%d" % (1 + mt)], ["pm%d" % mt], scale=128 ** -0.5)
                    yield
                    for mt in range(2):
                        mm(bank(6), vmem[:, mt, h * 128:(h + 1) * 128], pm[mt], mt == 0, mt == 1, ["vmem", "pm%d" % mt], ["ps6"])
                    for mt in range(2):
                        mm(bank(7), onesb, pm[mt], mt == 0, mt == 1, ["cbf", "pm%d" % mt], ["ps7"])
                    yield
                    P.add("dve", lambda e: e.reciprocal(out=W2, in_=bank(7)), reads=["ps7"], writes=["W2"])
                    tt("dve", W2, bank(6), W2, ALU.mult, ["ps6", "W2"], ["W2"])
                    cp("act", mixT[:, 8 + h, osl], W2, ["W2"], ["mixT"])
                    act(sqy[:, h, :], W2, AF.Square, ["W2"], ["sqy"])
                    yield
                for t in range(4):
                    for h in range(4):
                        mm(bank(4, 1, t), sqy[:, h, t * 128:(t + 1) * 128], onesb[:, 0:1], h == 0, h == 3, ["sqy", "cbf"], ["ps4"])
                o4 = slice(4 * (g - OG0), 4 * (g - OG0) + 4)
                cp("dve", ssmem[:, o4], bank(4, 4), ["ps4"], ["ssmem"])
                yield

            interleave([fb(OG0)])
            for g in range(OG0, NG):
                interleave([qstream(g), mstream(g)])
                interleave([ystream(g), fb(g + 1) if g + 1 < NG else None])

            P.barrier()
            ck(6)
            A.off = p2_off
            junk = A.alloc(1024)
            KTn = A.alloc(NS)
            KTr = A.alloc(NS)
            vt = A.alloc(NT * 128).rearrange("p (t d) -> p t d", d=128)
            kbn2 = [A.alloc(4 * 128).rearrange("p (t d) -> p t d", t=4) for _ in range(2)]
            kbr2 = [A.alloc(4 * 128).rearrange("p (t d) -> p t d", t=4) for _ in range(2)]
            pt = [A.alloc(512) for _ in range(4)]
            R1 = A.alloc(512, F32)
            Y1 = A.alloc(512, F32)
            sq2 = A.alloc(512)
            for p_ in range(2):
                P.add("dve", lambda e, p_=p_: e.memset(kbr2[p_].rearrange("p t d -> p (t d)"), 0.0), writes=["kbr%d" % p_])
            bgS = [A.alloc(1024, F32) for _ in range(2)]
            bgC = [A.alloc(1024) for _ in range(2)]

            def bg_convert():
                cnt = 0
                for a_, wsrc in enumerate((w_gate, w_up)):
                    wv = wsrc.rearrange("(k p) n -> p k n", p=128)
                    for k in range(8):
                        for c0 in range(0, DFF, 1024):
                            n = min(1024, DFF - c0)
                            i = cnt % 2
                            cnt += 1
                            dma(bgS[i][:, 0:n], wv[:, k, c0:c0 + n], w=["bgS%d" % i], grp="dbgs%d" % i, eng="sp")
                            ts("dve", bgC[i][:, 0:n], bgS[i][:, 0:n], vec[:, V_GFFN + k:V_GFFN + k + 1], None, ALU.mult, None,
                               ["bgS%d" % i, "vec"], ["bgC%d" % i])
                            f0, f1 = c0 // 128, (c0 + n) // 128
                            cc = a_ * 1024 + k * 128
                            dma(wgu_scr[f0:f1, :, cc:cc + 128].rearrange("f p n -> p f n"),
                                bgC[i][:, 0:n].rearrange("p (f n) -> p f n", n=128), r=["bgC%d" % i], w=["wgu_scr"],
                                grp="dbgo%d" % i, eng="pool")
                            yield

            bg = bg_convert()
            SC = 192 ** -0.5
            mixhm = mixT[:, 4:12, :].rearrange("p a n -> p (a n)")
            dma(mix_scr, mixhm, r=["mixT"], w=["mix_scr"], grp="dmsp")
            P.barrier()
            KTn_ = [KTn, mixT[:, 4:8, :].rearrange("p a n -> p (a n)")]
            vt_ = [vt, mixT[:, 8:12, :].rearrange("p a n -> p (a n)").rearrange("p (t d) -> p t d", d=128)]
            ksc = A.alloc(512, F32)
            P.add("dve", lambda e: e.memset(scr[:, 48:52], -0.5), writes=["scr48"])

            def kbuild(h, alone=False):
                ro = (h % 2) * 64
                s_ = h % 2
                KTs, vts = KTn_[s_], vt_[s_]
                kn_k, vt_k, kr_k = "KTn%d" % s_, "vt%d" % s_, "KTr%d" % s_
                for g in range(NG):
                    p_ = g % 2
                    b0, bT = 1, 0
                    kscb, kkey = ksc, "ksc"
                    if alone and p_ == 1:
                        b0, bT = 3, 5
                        kscb, kkey = Y1, "Y1"
                    pk = ["ps%d" % b0, "ps%d" % (b0 + 1)]
                    g4 = slice(4 * g, 4 * g + 4)
                    for ti in range(4):
                        tl = g * 4 + ti
                        for m in range(2):
                            mm(ps[:, b0 * 512 + ti * 256:b0 * 512 + (ti + 1) * 256], ckvT[:, m, tl * 128:(tl + 1) * 128],
                               wukv[:, m, h * 256:(h + 1) * 256], m == 0, m == 1, ["ckvT", "wukv"], [pk[ti // 2]])
                    kvv = ps[:, b0 * 512:b0 * 512 + 1024].rearrange("p (t c) -> p t c", c=256)
                    ksc3 = kscb.rearrange("p (t d) -> p t d", d=128)
                    c_ = 32 + p_ * 8
                    ssq_, c1_ = scr[:, c_:c_ + 4], scr[:, c_ + 4:c_ + 8]
                    sk = "scr%d" % c_
                    tt("dve", vts[:, g4, :], kvv[:, :, 128:256], rskv[:, g4].unsqueeze(2).to_broadcast([128, 4, 128]), ALU.mult,
                       pk + ["rskv"], [vt_k])
                    if alone:
                        act(ksc3, kvv[:, :, 0:128], AF.Square, pk, [kkey])
                        P.add("dve", lambda e, ssq_=ssq_, ksc3=ksc3: e.tensor_reduce(out=ssq_, in_=ksc3, axis=AX.X, op=ALU.add),
                              reads=[kkey], writes=[sk])
                        tt("dve", ssq_, ssq_, rskv2[:, g4], ALU.mult, [sk, "rskv2"], [sk])
                        tt("dve", ssq_, ssq_, sskr[:, g4], ALU.add, [sk, "sskr"], [sk])
                        yield
                        rsqrt_act(ssq_, ssq_, 192, [sk], [sk])
                        tt("dve", c1_, ssq_, rskv[:, g4], ALU.mult, [sk, "rskv"], [sk])
                        tt("dve", ksc3, kvv[:, :, 0:128], c1_.unsqueeze(2).to_broadcast([128, 4, 128]), ALU.mult, pk + [sk], [kkey])
                    else:
                        sqb = junk[:, 0:512].rearrange("p (t d) -> p t d", d=128)
                        cp("dve", ksc3, kvv[:, :, 0:128], pk, [kkey])
                        tt("dve", sqb, ksc3, ksc3, ALU.mult, [kkey], ["junk"])
                        P.add("dve", lambda e, ssq_=ssq_, sqb=sqb: e.tensor_reduce(out=ssq_, in_=sqb, axis=AX.X, op=ALU.add),
                              reads=["junk"], writes=[sk])
                        tt("dve", ssq_, ssq_, rskv2[:, g4], ALU.mult, [sk, "rskv2"], [sk])
                        tt("dve", ssq_, ssq_, sskr[:, g4], ALU.add, [sk, "sskr"], [sk])
                        ts("dve", ssq_, ssq_, 1.0 / 192, EPS, ALU.mult, ALU.add, [sk], [sk])
                        yield
                        tt("pool", ssq_, ssq_, scr[:, 48:52], ALU.pow, [sk, "scr48"], [sk])
                        tt("dve", c1_, ssq_, rskv[:, g4], ALU.mult, [sk, "rskv"], [sk])
                        tt("dve", ksc3, ksc3, c1_.unsqueeze(2).to_broadcast([128, 4, 128]), ALU.mult, [kkey, sk], [kkey])
                    tt("dve", kbn2[p_], ksc3, vec[:, V_GK:V_GK + 128].unsqueeze(1).to_broadcast([128, 4, 128]), ALU.mult,
                       [kkey, "vec"], ["kbn%d" % p_])
                    tt("dve", kbr2[p_][:, :, ro:ro + 64], kr[:, g4, :], ssq_.unsqueeze(2).to_broadcast([128, 4, 64]), ALU.mult,
                       ["kr", sk], ["kbr%d" % p_])
                    yield
                    pb = bankb(bT)
                    for ti in range(4):
                        tp(pb[:, ti * 128:(ti + 1) * 128], kbn2[p_][:, ti, :], ["kbn%d" % p_], ["ps%d" % bT])
                        tp(pb[:, 512 + ti * 128:512 + (ti + 1) * 128], kbr2[p_][:, ti, :], ["kbr%d" % p_], ["ps%d" % bT])
                    ev = "act" if alone else "dve"
                    cp(ev, KTs[:, g * 512:(g + 1) * 512], pb[:, 0:512], ["ps%d" % bT], [kn_k])
                    cp(ev, KTr[ro:ro + 64, g * 512:(g + 1) * 512], pb[ro:ro + 64, 512:1024], ["ps%d" % bT], [kr_k])
                    if g % 4 == 3:
                        next(bg, None)
                    yield

            for _ in kbuild(0, alone=True):
                pass
            for h in range(4):
                ro = (h % 2) * 64
                KTn, vt = KTn_[h % 2], vt_[h % 2]
                kn_k, vt_k, kr_k = "KTn%d" % (h % 2), "vt%d" % (h % 2), "KTr%d" % (h % 2)
                kb = kbuild(h + 1) if h < 3 else iter(())
                tile_ctr = [0]
                for Q in range(4):
                    qsl0 = Q * 512
                    nkt = OT0 + 4 * Q + 4
                    def tile_geo(kt):
                        j = kt - (OT0 + 4 * Q)
                        q0 = 0 if j < 0 else 128 * j
                        return j, q0, 512 - q0

                    def emit_S(kt):
                        j, q0, n = tile_geo(kt)
                        sb_ = 3 + (kt % 2)
                        ksl = slice(kt * 128, (kt + 1) * 128)
                        qs = slice(qsl0 + q0, qsl0 + 512)
                        mm(bank(sb_, n, q0), KTn[:, ksl], qTn[:, h, qs], True, False, [kn_k, "qTn"], ["ps%d" % sb_])
                        mm(bank(sb_, n, q0), KTr[ro:ro + 64, ksl], qTr[ro:ro + 64, h // 2, qs], False, True, [kr_k, "qTr"], ["ps%d" % sb_])

                    emit_S(0)
                    for kt in range(nkt):
                        if kt % 16 == 8:
                            next(bg, None)
                        tile_ctr[0] += 1
                        if tile_ctr[0] % 4 == 2:
                            next(kb, None)
                        if kt + 1 < nkt:
                            emit_S(kt + 1)
                        j, q0, n = tile_geo(kt)
                        sb_ = 3 + (kt % 2)
                        pti = kt % 4
                        act(pt[pti][:, q0:512], bank(sb_, n, q0), AF.Exp, ["ps%d" % sb_], ["pt%d" % pti], scale=SC)
                        if j >= 0:
                            tt("dve", pt[pti][:, q0:q0 + 128], pt[pti][:, q0:q0 + 128], tri, ALU.mult, ["pt%d" % pti, "cbf"], ["pt%d" % pti])
                        reg = kt // 16
                        von = onesb if reg >= 3 else vones[:, reg * 128:(reg + 1) * 128]
                        mm(bank(5, n, q0), vt[:, kt, :], pt[pti][:, q0:512], kt == 0, kt == nkt - 1, [vt_k, "pt%d" % pti], ["ps5"])
                        mm(bank(6, n, q0), von, pt[pti][:, q0:512], kt == 0, kt == nkt - 1, ["vones", "cbf", "pt%d" % pti], ["ps6"])
                    P.add("dve", lambda e: e.reciprocal(out=R1, in_=bank(6)), reads=["ps6"], writes=["R1"])
                    tt("dve", Y1, bank(5), R1, ALU.mult, ["ps5", "R1"], ["Y1"])
                    cp("act", mixT[:, h, qsl0:qsl0 + 512], Y1, ["Y1"], ["mixT"])
                    act(sq2, Y1, AF.Square, ["Y1"], ["sq2"])
                    for t in range(4):
                        mm(bank(7, 1, t), sq2[:, t * 128:(t + 1) * 128], onesb[:, 0:1], True, True, ["sq2", "cbf"], ["ps7"])
                    o4 = slice(4 * Q, 4 * Q + 4)
                    tt("dve", ssmla[:, o4], ssmla[:, o4], bank(7, 4), ALU.add, ["ssmla", "ps7"], ["ssmla"])
                for _ in kb:
                    pass

            for _ in bg:
                pass
            P.barrier()
            ck(7)
            dma(mixhm, mix_scr, r=["mix_scr"], w=["mixT"], grp="dmsp")
            A.off = base_off
            wout = A.alloc(12 * 1024).rearrange("p (k n) -> p k n", k=12)
            wdn = A.alloc(NF * 1024).rearrange("p (k n) -> p k n", k=NF)
            p3_off = A.off
            stg4 = [A.alloc(1024, F32) for _ in range(6)]
            rsqrt_act(rsmla, ssmla, 512, ["ssmla"], ["rsmla"])
            rsqrt_act(rsmem, ssmem, 512, ["ssmem"], ["rsmem"])
            w_out_v = w_out.rearrange("(k p) n -> p k n", p=128)
            w_dn_v = w_down.rearrange("(k p) n -> p k n", p=128)
            jobs = [(wout[:, k, :], w_out_v[:, k, :], None if 4 <= k < 8 else vec[:, V_GOUT + k:V_GOUT + k + 1], "wout") for k in range(12)]
            jobs += [(wdn[:, k, :], w_dn_v[:, k, :], None, "wdn") for k in range(NF)]
            for ji, (dst, src, sc, dkey) in enumerate(jobs):
                i = ji % 6
                dma(stg4[i], src, w=["stg4_%d" % i], grp="dstg4_%d" % i, eng=("sp", "pool", "act")[ji % 3])
                if ji % 2 == 0:
                    if sc is None:
                        cp("dve", dst, stg4[i], ["stg4_%d" % i], [dkey])
                    else:
                        ts("dve", dst, stg4[i], sc, None, ALU.mult, None, ["stg4_%d" % i, "vec"], [dkey])
                else:
                    act(dst, stg4[i], AF.Copy, ["stg4_%d" % i, "vec"], [dkey], scale=(1.0 if sc is None else sc))
            P.barrier()
            A.off = p3_off
            xo = [A.alloc(1024, F32) for _ in range(4)]
            h2b = [A.alloc(1024)] * 2
            h2T = A.alloc(8 * 512).rearrange("p (k n) -> p k n", k=8)
            actT = A.alloc(NF * 512).rearrange("p (k n) -> p k n", k=NF)
            wgu = [A.alloc(2048).rearrange("p (a k n) -> p a k n", a=2, k=8) for _ in range(3)]
            G1 = A.alloc(512, F32)
            G2 = A.alloc(512, F32)
            yo = [A.alloc(512, F32) for _ in range(2)]
            junk3 = A.alloc(1024)
            ck(8)
            for Gq in range(4):
                for t in range(4):
                    ot = Gq * 4 + t
                    tok = slice(ot * 128, (ot + 1) * 128)
                    dma(xo[t], xs[OWN0 + ot * 128:OWN0 + (ot + 1) * 128, :], w=["xo%d" % t], grp="dxo%d" % t, eng="sp" if t % 2 == 0 else "pool")
                    pa = ps[:, 1 * 512:3 * 512]
                    pbk = ps[:, 3 * 512:5 * 512]
                    pc = ps[:, 5 * 512:7 * 512]
                    for (k0, pk, keys) in ((0, 1, ["ps1", "ps2"]), (4, 3, ["ps3", "ps4"]), (8, 5, ["ps5", "ps6"])):
                        for nh in range(2):
                            for k in range(4):
                                mm(bank(pk + nh), mixT[:, k0 + k, tok], wout[:, k0 + k, nh * 512:(nh + 1) * 512], k == 0, k == 3,
                                   ["mixT", "wout"], [keys[nh]])
                    stt("dve", xo[t], pa, rsmla[:, ot:ot + 1], xo[t], ALU.mult, ALU.add, ["ps1", "ps2", "rsmla", "xo%d" % t], ["xo%d" % t])
                    tt("dve", xo[t], xo[t], pbk, ALU.add, ["xo%d" % t, "ps3", "ps4"], ["xo%d" % t])
                    stt("dve", xo[t], pc, rsmem[:, ot:ot + 1], xo[t], ALU.mult, ALU.add, ["ps5", "ps6", "rsmem", "xo%d" % t], ["xo%d" % t])
                    c_ = 40 + t
                    act(junk3, xo[t], AF.Square, ["xo%d" % t], ["junk3", "scr%d" % c_], accum=scr[:, c_:c_ + 1])
                    rsqrt_act(scr[:, c_:c_ + 1], scr[:, c_:c_ + 1], 1024, ["scr%d" % c_], ["scr%d" % c_])
                    ts("dve", h2b[t % 2], xo[t], scr[:, c_:c_ + 1], None, ALU.mult, None, ["xo%d" % t, "scr%d" % c_], ["h2b"])
                    pb = bankb(0)
                    for k in range(8):
                        tp(pb[:, k * 128:(k + 1) * 128], h2b[t % 2][:, k * 128:(k + 1) * 128], ["h2b"], ["ps0"])
                    cp("act", h2T[:, :, t * 128:(t + 1) * 128], pb.rearrange("p (k n) -> p k n", k=8), ["ps0"], ["h2T"])
                for f in range(NF):
                    wi = f % 3
                    dma(wgu[wi].rearrange("p a k n -> p (a k n)"), wgu_scr[f, :, :], r=["wgu_scr"], w=["wgu%d" % wi], grp="dwgu%d" % wi,
                        eng="pool" if wi == 1 else "sp")
                    bg = 1 + 2 * (f % 2)
                    for a_ in range(2):
                        for k in range(8):
                            mm(bank(bg + a_), wgu[wi][:, a_, k, :], h2T[:, k, :], k == 0, k == 7, ["wgu%d" % wi, "h2T"], ["ps%d" % (bg + a_)])
                    Gb = G1 if f % 2 == 0 else G2
                    gk = "G%d" % (f % 2)
                    act(Gb, bank(bg), AF.Silu, ["ps%d" % bg], [gk])
                    tt("dve", actT[:, f, :], bank(bg + 1), Gb, ALU.mult, ["ps%d" % (bg + 1), gk], ["actT"])
                for t in range(4):
                    ot = Gq * 4 + t
                    for nh in range(2):
                        bi = 5 + nh
                        for f in range(NF):
                            mm(bank(bi), actT[:, f, t * 128:(t + 1) * 128], wdn[:, f, nh * 512:(nh + 1) * 512], f == 0, f == NF - 1,
                               ["actT", "wdn"], ["ps%d" % bi])
                        tt("dve", yo[nh], bank(bi), xo[t][:, nh * 512:(nh + 1) * 512], ALU.add, ["ps%d" % bi, "xo%d" % t], ["yo%d" % nh])
                        dma(yd[ot * 128:(ot + 1) * 128, nh * 512:(nh + 1) * 512], yo[nh], r=["yo%d" % nh], grp="dyo%d" % nh)
        except _Stop:
            pass
        if debug:
            P.barrier()
            dma(dbg, mixT.rearrange("p k n -> p (k n)"), r=["mixT"], grp="ddbg")
        P.barrier()
        P.run(nc, st)
    return nc


def _prep(inputs):
    x = np.asarray(inputs["x"], np.float32)
    mem = np.asarray(inputs["mem"], np.float32)
    pos = np.asarray(inputs["positions"], np.int32)
    g = lambda k: np.asarray(inputs[k], np.float32)[0]
    perm = np.concatenate([np.arange(384, 640), np.arange(640, 704), np.arange(1216, 1728), np.arange(1728, 2240),
                           np.arange(0, 384), np.arange(704, 1216), np.arange(2240, 2752), np.arange(2752, 3264)])
    w_in = np.ascontiguousarray(g("w_in")[:, perm])
    colmaj = lambda v, k: np.ascontiguousarray(v.reshape(k, 128).T)
    rep = lambda v: np.broadcast_to(v[None, :], (128, v.shape[0]))
    vec = np.zeros((128, NV), np.float32)
    vec[:, V_GMIX:V_GMIX + 8] = colmaj(g("norm_mix"), 8)
    vec[:, V_GMEM:V_GMEM + 8] = colmaj(g("norm_mem"), 8)
    vec[:, V_GFFN:V_GFFN + 8] = colmaj(g("norm_ffn"), 8)
    vec[:, V_GQA:V_GQA + 3] = colmaj(g("q_a_norm"), 3)
    vec[:, V_GKVA:V_GKVA + 2] = colmaj(g("kv_a_norm"), 2)
    vec[:, V_GOUT:V_GOUT + 4] = colmaj(g("mla_out_norm"), 4)
    vec[:, V_GOUT + 8:V_GOUT + 12] = colmaj(g("mem_out_norm"), 4)
    vec[:, V_GHGO:V_GHGO + 4] = colmaj(g("hg_out_norm"), 4)
    lbl = np.asarray(inputs["hg_lb_logits"], np.float32)
    vec[:, V_LBL:V_LBL + 4] = colmaj(lbl[0], 4)
    vec[:, V_LBL + 4:V_LBL + 8] = colmaj(lbl[1], 4)
    vec[:, V_GQ:V_GQ + 192] = rep(g("mla_q_norm"))
    vec[:, V_GK:V_GK + 192] = rep(g("mla_k_norm"))
    vec[:, V_GMQ:V_GMQ + 128] = rep(g("mem_q_norm"))
    vec[:, V_GMK:V_GMK + 128] = rep(g("mem_k_norm"))
    rm = np.ones(512, np.float32)
    rm[::64] = 0.0
    vec[:, V_RMASK:V_RMASK + 512] = rm[None, :]
    half = 32
    invf = (10000.0 ** (-np.arange(half, dtype=np.float64) / half)) / (2.0 * np.pi)
    vec[:, V_INVF:V_INVF + 32] = invf.astype(np.float32)[None, :]
    cb = np.zeros((128, 512), np.float32)
    cb[:, 0:128] = np.eye(128)
    cb[:, 128:256] = 1.0
    kk = np.arange(128)
    cb[:, 256:384] = (kk[None, :] >= kk[:, None])
    cb[:, 384:512] = (kk[None, :] >= kk[:, None]) & ((kk[None, :] // 64) == (kk[:, None] // 64))
    cbf = cb.astype(ml_dtypes.bfloat16)
    shared = dict(w_in=w_in, w_uq=g("w_uq"), w_ukv=g("w_ukv"), w_mkv=g("w_mem_kv"), w_out=g("w_out"),
                  w_gate=g("w_gate"), w_up=g("w_up"), w_down=g("w_down"), cbf=cbf)
    maps = []
    for c in range(8):
        b, j = c // 4, c % 4
        n = OWN * (j + 1)
        xsl = np.zeros((NS, 1024), np.float32)
        xsl[NS - n:] = x[b, :n]
        ps_ = np.zeros((NS,), np.int32)
        ps_[NS - n:] = pos[b, :n]
        v = vec.copy()
        for r_ in range(3):
            v[:, V_VFLAG + r_] = 1.0 if (r_ + 1) * OWN > NS - n else 0.0
        m = dict(shared)
        m.update(xs=xsl, pos=np.ascontiguousarray(ps_.reshape(NT, 128).T), mem=np.ascontiguousarray(mem[b]), vec=v)
        maps.append(m)
    return maps


_NC = {}


def kernel(**inputs):
    maps = _prep(inputs)
    if "nc" not in _NC:
        _NC["nc"] = build(False)
    res = run_bass_kernel_spmd(_NC["nc"], maps, core_ids=list(range(8)))
    out = np.zeros((2, 8192, 1024), np.float32)
    for c in range(8):
        b, j = c // 4, c % 4
        out[b, j * OWN:(j + 1) * OWN] = res.results[c]["y"]
    return out
```

```python
import contextlib
import math
import numpy as np
import ml_dtypes
import concourse.bass as bass
import concourse.mybir as mybir
from concourse.bass_utils import run_bass_kernel_spmd

F32 = mybir.dt.float32
BF16 = mybir.dt.bfloat16
I32 = mybir.dt.int32
AF = mybir.ActivationFunctionType
ALU = mybir.AluOpType
AX = mybir.AxisListType

EPS = 1e-6
NS = 8192
NT = NS // 128
NG = NS // 512
OWN = 2048
OWN0 = NS - OWN
OT0 = OWN0 // 128
OG0 = OWN0 // 512
DFF = 2816
NF = DFF // 128

C_KV, C_KR, C_HF, C_HI, C_CQ, C_HQ, C_HG, C_MQ = 0, 256, 320, 832, 1344, 1728, 2240, 2752

V_GMIX, V_GMEM, V_GFFN, V_GQA, V_GKVA, V_GOUT, V_GHGO, V_LBL, V_VFLAG = 0, 8, 16, 24, 27, 29, 41, 45, 53
V_GQ, V_GK, V_GMQ, V_GMK, V_RMASK, V_INVF = 57, 249, 441, 569, 697, 1209
NV = 1241


class Op:
    __slots__ = ("eng", "fn", "waits", "signal", "ticket", "chan", "cidx", "embed")


class Prog:
    ENG = ["pe", "act", "dve", "pool", "sp"]

    def __init__(self):
        self.ops = {e: [] for e in self.ENG}
        self.buf = {}
        self.waited = {e: {} for e in self.ENG}
        self.chan_ops = {}

    def add(self, eng, fn, reads=(), writes=(), dma=None, embed=True):
        op = Op()
        op.eng = eng
        op.fn = fn
        op.embed = embed and dma is None
        op.signal = dma is not None
        op.ticket = None
        op.chan = dma if dma is not None else eng
        lst = self.chan_ops.setdefault(op.chan, [])
        op.cidx = len(lst)
        lst.append(op)
        deps = {}
        for k in reads:
            b = self.buf.setdefault(k, [None, []])
            d = b[0]
            if d is not None and (d.chan not in deps or deps[d.chan].cidx < d.cidx):
                deps[d.chan] = d
            if k.startswith("ps"):
                for d in b[1]:
                    if d.chan != op.chan and (d.chan not in deps or deps[d.chan].cidx < d.cidx):
                        deps[d.chan] = d
        for k in writes:
            b = self.buf.setdefault(k, [None, []])
            for d in ([b[0]] if b[0] is not None else []) + b[1]:
                if d.chan == op.chan and dma is None:
                    continue
                if d.chan not in deps or deps[d.chan].cidx < d.cidx:
                    deps[d.chan] = d
        op.waits = []
        w = self.waited[eng]
        for chan, d in deps.items():
            if chan == "pe" and eng == "pe":
                continue
            if w.get(chan, -1) >= d.cidx:
                continue
            w[chan] = d.cidx
            d.signal = True
            op.waits.append(d)
        for k in reads:
            self.buf[k][1].append(op)
        for k in writes:
            self.buf[k][0] = op
            self.buf[k][1] = []
        self.ops[eng].append(op)
        return op

    def barrier(self):
        chans = {c: l[-1] for c, l in self.chan_ops.items() if l}
        for e in self.ENG:
            w = self.waited[e]
            for chan, d in chans.items():
                if chan == e or w.get(chan, -1) >= d.cidx:
                    continue
                w[chan] = d.cidx
                d.signal = True
                op = Op()
                op.eng, op.fn, op.signal, op.ticket, op.chan, op.cidx = e, None, False, None, None, -1
                op.embed = False
                op.waits = [d]
                self.ops[e].append(op)

    def run(self, nc, stack):
        sems = {}
        for chan, lst in self.chan_ops.items():
            sems[chan] = stack.enter_context(nc.semaphore("s_" + chan))
            isdma = chan not in self.ENG
            cnt = 0
            for op in lst:
                if isdma:
                    cnt += 16
                    op.ticket = cnt
                elif op.signal:
                    cnt += 1
                    op.ticket = cnt
        block = stack.enter_context(nc.Block())

        def replay(eng, e):
            for op in self.ops[eng]:
                waits = list(op.waits)
                emb = waits.pop() if (op.fn is not None and op.embed and waits) else None
                for d in waits:
                    e.wait_ge(sems[d.chan], d.ticket)
                if op.fn is None:
                    continue
                ins = op.fn(e)
                if emb is not None:
                    ins._wait_ge(sems[emb.chan], emb.ticket)
                if op.signal:
                    ins.then_inc(sems[op.chan], 16 if op.chan not in self.ENG else 1)

        @block.tensor
        def _(e):
            replay("pe", e)

        @block.scalar
        def _(e):
            replay("act", e)

        @block.vector
        def _(e):
            replay("dve", e)

        @block.gpsimd
        def _(e):
            replay("pool", e)

        @block.sync
        def _(e):
            replay("sp", e)


class Arena:
    def __init__(self, t, ncols):
        self.t = t
        self.n = ncols
        self.off = 0

    def alloc(self, cols, dtype=BF16):
        mult = 1 if dtype == BF16 else 2
        self.off = (self.off + 1) // 2 * 2
        a = self.t[:, self.off:self.off + cols * mult]
        self.off += cols * mult
        assert self.off <= self.n, (self.off, self.n)
        return a if dtype == BF16 else a.bitcast(dtype)


class _Stop(Exception):
    pass


def build(debug=False, stop=99):
    nc = bass.Bass("TRN2", target_bir_lowering=False)

    def din(name, shape, dt=F32):
        return nc.dram_tensor(name, list(shape), dt, kind="ExternalInput").ap()

    xs = din("xs", [NS, 1024])
    posd = din("pos", [128, NT], I32)
    memd = din("mem", [256, 1024])
    vecd = din("vec", [128, NV])
    cbfd = din("cbf", [128, 512], BF16)
    w_in = din("w_in", [1024, 3264])
    w_uq = din("w_uq", [384, 768])
    w_ukv = din("w_ukv", [256, 1024])
    w_mkv = din("w_mkv", [1024, 1024])
    w_out = din("w_out", [1536, 1024])
    w_gate = din("w_gate", [1024, DFF])
    w_up = din("w_up", [1024, DFF])
    w_down = din("w_down", [DFF, 1024])
    yd = nc.dram_tensor("y", [OWN, 1024], F32, kind="ExternalOutput").ap()
    dbg = nc.dram_tensor("dbg", [128, 12 * OWN], BF16, kind="ExternalOutput").ap() if debug else None
    wgu_scr = nc.dram_tensor("wgu_scr", [NF, 128, 2048], BF16, kind="Internal").ap()
    mix_scr = nc.dram_tensor("mix_scr", [128, 8 * OWN], BF16, kind="Internal").ap()

    P = Prog()
    st = contextlib.ExitStack()
    with st:
        TOT = 106400
        arena_t = st.enter_context(nc.sbuf_tensor("arena", [128, TOT], BF16))
        A = Arena(arena_t, TOT)
        ps = st.enter_context(nc.psum_tensor("ps", [128, 4096], F32))

        def bank(i, n=512, off=0):
            return ps[:, i * 512 + off:i * 512 + off + n]

        def bankb(i, n=1024, off=0):
            return ps[:, i * 512:(i + 1) * 512].bitcast(BF16)[:, off:off + n]

        vec = A.alloc(NV, F32)
        cbf = A.alloc(512)
        ident = cbf[:, 0:128]
        onesb = cbf[:, 128:256]
        tri = cbf[:, 256:384]
        tri2 = cbf[:, 384:512]
        posi = A.alloc(NT, I32)
        posf = A.alloc(NT, F32)
        lb = A.alloc(4, F32)
        oml = A.alloc(4, F32)
        noml = A.alloc(4, F32)
        vones = A.alloc(3 * 128)
        stat = A.alloc(8 * NT, F32)
        rs1 = stat[:, 0:NT]
        rskv = stat[:, NT:2 * NT]
        sskr = stat[:, 2 * NT:3 * NT]
        rskv2 = stat[:, 3 * NT:4 * NT]
        ssmla = stat[:, 4 * NT:4 * NT + 16]
        ssmem = stat[:, 4 * NT + 16:4 * NT + 32]
        rsmla = stat[:, 4 * NT + 32:4 * NT + 48]
        rsmem = stat[:, 4 * NT + 48:4 * NT + 64]
        scr = A.alloc(64, F32)
        S32 = A.alloc(512, F32)
        Sb = A.alloc(512)
        dec = A.alloc(32, F32).rearrange("p (h c) -> p h c", h=4)
        mixT = A.alloc(12 * OWN).rearrange("p (k n) -> p k n", k=12)
        base_off = A.off
        ckvT = A.alloc(2 * NS).rearrange("p (k n) -> p k n", k=2)
        kr = A.alloc(NT * 64).rearrange("p (t d) -> p t d", d=64)
        wukv = A.alloc(2 * 1024).rearrange("p (k n) -> p k n", k=2)
        p1_off = A.off

        _dq = [0]

        def dma(out, in_, r=(), w=(), grp=None, eng=None):
            if eng is None:
                eng = "sp"
            if grp is None:
                _dq[0] += 1
                grp = "dq%d" % (_dq[0] % 12)
            P.add(eng, lambda e: e.dma_start(out=out, in_=in_), reads=r, writes=w, dma=grp)

        def act(out, in_, func, r, w, scale=1.0, bias=0.0, accum=None):
            if accum is None:
                P.add("act", lambda e: e.activation(out=out, in_=in_, func=func, scale=scale, bias=bias), reads=r, writes=w)
            else:
                P.add("act", lambda e: e.activation(out=out, in_=in_, func=func, scale=scale, bias=bias, accum_out=accum), reads=r, writes=w,
                      embed=False)

        def rsqrt_act(out, in_, n, r, w, bias=EPS):
            act(out, in_, AF.Ln, r, w, scale=1.0 / n, bias=bias)
            act(out, out, AF.Exp, w, w, scale=-0.5)

        def tt(eng, out, in0, in1, op, r, w):
            P.add(eng, lambda e: e.tensor_tensor(out=out, in0=in0, in1=in1, op=op), reads=r, writes=w)

        def ts(eng, out, in0, s1, s2, op0, op1, r, w):
            if s2 is None:
                P.add(eng, lambda e: e.tensor_scalar(out=out, in0=in0, scalar1=s1, scalar2=None, op0=op0), reads=r, writes=w)
            else:
                P.add(eng, lambda e: e.tensor_scalar(out=out, in0=in0, scalar1=s1, scalar2=s2, op0=op0, op1=op1), reads=r, writes=w)

        def stt(eng, out, in0, s, in1, op0, op1, r, w):
            P.add(eng, lambda e: e.scalar_tensor_tensor(out=out, in0=in0, scalar=s, in1=in1, op0=op0, op1=op1), reads=r, writes=w)

        def cp(eng, out, in_, r, w):
            if eng == "act":
                act(out, in_, AF.Copy, r, w)
            else:
                P.add(eng, lambda e: e.tensor_copy(out=out, in_=in_), reads=r, writes=w)

        def mm(out, lhsT, rhs, start, stop, r, w):
            P.add("pe", lambda e: e.matmul(out, lhsT=lhsT, rhs=rhs, start=start, stop=stop), reads=r, writes=w)

        def tp(out, in_, r, w, idn=None):
            idn = ident if idn is None else idn
            P.add("pe", lambda e: e.transpose(out=out, in_=in_, identity=idn), reads=list(r) + ["cbf"], writes=w)

        def sigmoid_(buf, src, r, key):
            act(buf, src, AF.Exp, r, [key], scale=-1.0)
            act(buf, buf, AF.Ln, [key], [key], bias=1.0)
            act(buf, buf, AF.Exp, [key], [key], scale=-1.0)

        def ck(level):
            if stop == level:
                raise _Stop()

        try:
            dma(vec, vecd, w=["vec"])
            dma(cbf, cbfd, w=["cbf"])
            dma(posi, posd, w=["posi"])
            cp("dve", posf, posi, ["posi"], ["posf"])
            lbl = vec[:, V_LBL:V_LBL + 8].rearrange("p (r h) -> p r h", r=2)
            tt("dve", lb, lbl[:, 0, :], lbl[:, 1, :], ALU.subtract, ["vec"], ["lb"])
            sigmoid_(lb, lb, ["lb"], "lb")
            ts("dve", oml, lb, -1.0, 1.0, ALU.mult, ALU.add, ["lb"], ["oml"])
            ts("dve", noml, oml, -1.0, None, ALU.mult, None, ["oml"], ["oml"])
            for r_ in range(3):
                ts("dve", vones[:, r_ * 128:(r_ + 1) * 128], onesb, vec[:, V_VFLAG + r_:V_VFLAG + r_ + 1], None, ALU.mult, None,
                   ["cbf", "vec"], ["vones"])
            P.add("dve", lambda e: e.memset(S32, 0.0), writes=["S32"])
            P.add("dve", lambda e: e.memset(Sb, 0.0), writes=["Sb"])
            P.add("dve", lambda e: e.memset(stat, 0.0), writes=["stat"])

            ck(1)
            NA = 2368
            win = A.alloc(8 * NA).rearrange("p (k n) -> p k n", k=8)
            wl_off = A.off
            stg = [A.alloc(1024, F32) for _ in range(6)]
            _wl = [0]

            def load_w(dst, src, ncols, scale, dkey):
                ns = len(stg)
                cw = stg[0].shape[1]
                for c0 in range(0, ncols, cw):
                    n = min(cw, ncols - c0)
                    i = _wl[0] % ns
                    _wl[0] += 1
                    dma(stg[i][:, 0:n], src[:, c0:c0 + n], w=["stg%d" % i], grp="dstg%d" % i, eng=("sp", "pool", "act")[i % 3])
                    if i % 2 == 0:
                        if scale is None:
                            cp("dve", dst[:, c0:c0 + n], stg[i][:, 0:n], ["stg%d" % i], [dkey])
                        else:
                            ts("dve", dst[:, c0:c0 + n], stg[i][:, 0:n], scale, None, ALU.mult, None, ["stg%d" % i, "vec"], [dkey])
                    else:
                        act(dst[:, c0:c0 + n], stg[i][:, 0:n], AF.Copy, ["stg%d" % i, "vec"], [dkey], scale=(1.0 if scale is None else scale))

            w_in_v = w_in.rearrange("(k p) n -> p k n", p=128)
            for k in range(8):
                load_w(win[:, k, 0:1344], w_in_v[:, k, 0:1344], 1344, vec[:, V_GMIX + k:V_GMIX + k + 1], "win")
                load_w(win[:, k, 1344:NA], w_in_v[:, k, C_HQ:C_HQ + 1024], 1024, vec[:, V_GMIX + k:V_GMIX + k + 1], "win")
            w_ukv_v = w_ukv.rearrange("(k p) n -> p k n", p=128)
            for k in range(2):
                load_w(wukv[:, k, :], w_ukv_v[:, k, :], 1024, vec[:, V_GKVA + k:V_GKVA + k + 1], "wukv")
            A_HQ, A_HG = 1344, 1344 + 512
            A.off = wl_off
            W1b = A.alloc(512, F32)
            W2b = A.alloc(512, F32)

            xt = [A.alloc(1024, F32) for _ in range(2)]
            junk = A.alloc(1024)
            hb = [A.alloc(1024)]
            hT = A.alloc(8 * 512).rearrange("p (k n) -> p k n", k=8)
            sqt = A.alloc(2 * 512).rearrange("p (k n) -> p k n", k=2)
            W1 = A.alloc(512, F32)
            W2 = A.alloc(512, F32)
            W3 = A.alloc(512, F32)
            W4 = A.alloc(512, F32)
            W3b = A.alloc(512, F32)
            W4b = A.alloc(512, F32)
            kT = A.alloc(4 * 512).rearrange("p (h n) -> p h n", h=4)
            qT = A.alloc(4 * 512).rearrange("p (h n) -> p h n", h=4)
            sgT = A.alloc(4 * 512).rearrange("p (h n) -> p h n", h=4)
            ktok = A.alloc(4 * 4 * 128).rearrange("p (t h d) -> p t h d", t=4, h=4)
            vtok = A.alloc(4 * 512).rearrange("p (t n) -> p t n", t=4)
            Amb = A.alloc(4 * 128).rearrange("p (h n) -> p h n", h=4)
            krg = A.alloc(4 * 64, F32).rearrange("p (t d) -> p t d", d=64)
            rp = [A.alloc(4 * 32, F32).rearrange("p (t d) -> p t d", d=32) for _ in range(4)]
            cst = A.alloc(4 * 32, F32).rearrange("p (t d) -> p t d", d=32)
            snt = A.alloc(4 * 32, F32).rearrange("p (t d) -> p t d", d=32)
            tA = A.alloc(128, F32)
            tB = A.alloc(128, F32)
            tI = A.alloc(128, I32)
            pm = [A.alloc(512)]
            rmask = vec[:, V_RMASK:V_RMASK + 512]
            invf = vec[:, V_INVF:V_INVF + 32]

            def cs_tables(g4):
                tt("dve", tA.rearrange("p (t d) -> p t d", d=32), posf[:, g4].unsqueeze(2).to_broadcast([128, 4, 32]),
                   invf.unsqueeze(1).to_broadcast([128, 4, 32]), ALU.mult, ["posf", "vec"], ["tA"])
                for (shift, dst, dkey) in ((0.0, snt, "snt"), (0.25, cst, "cst")):
                    dflat = dst.rearrange("p t d -> p (t d)")
                    ts("dve", tB, tA, shift, None, ALU.add, None, ["tA"], ["tB"])
                    cp("dve", tI, tB, ["tB"], ["tI"])
                    cp("dve", dflat, tI, ["tI"], [dkey])
                    tt("dve", tB, tB, dflat, ALU.subtract, ["tB", dkey], ["tB"])
                    ts("dve", dflat, tB, 0.5, None, ALU.is_gt, None, ["tB"], [dkey])
                    tt("dve", tB, tB, dflat, ALU.subtract, ["tB", dkey], ["tB"])
                    ts("dve", dflat, tB, -0.5, None, ALU.is_lt, None, ["tB"], [dkey])
                    tt("dve", tB, tB, dflat, ALU.add, ["tB", dkey], ["tB"])
                    act(dflat, tB, AF.Sin, ["tB"], [dkey], scale=2.0 * math.pi)

            def rope(dst_lo, dst_hi, src_lo, src_hi, cs, sn, rkeys, wkeys):
                tt("dve", rp[0], src_lo, cs, ALU.mult, rkeys, ["rp0"])
                tt("dve", rp[1], src_hi, sn, ALU.mult, rkeys, ["rp1"])
                tt("dve", dst_lo, rp[0], rp[1], ALU.subtract, ["rp0", "rp1"], wkeys)
                tt("dve", rp[2], src_hi, cs, ALU.mult, rkeys, ["rp2"])
                tt("dve", rp[3], src_lo, sn, ALU.mult, rkeys, ["rp3"])
                tt("dve", dst_hi, rp[2], rp[3], ALU.add, ["rp2", "rp3"], wkeys)

            def rms_load_transpose(src_rows, xbuf, xkey, hbuf, hkey, rcol, rkey, dstT, dkey, tsl, eng_q):
                dma(xbuf, src_rows, w=[xkey], grp="d" + xkey, eng=eng_q)
                act(junk, xbuf, AF.Square, [xkey], ["junk", rkey], accum=rcol)
                rsqrt_act(rcol, rcol, 1024, [rkey], [rkey])
                ts("dve", hbuf, xbuf, rcol, None, ALU.mult, None, [xkey, rkey], [hkey])
                pb = bankb(0)
                for k in range(8):
                    tp(pb[:, k * 128:(k + 1) * 128], hbuf[:, k * 128:(k + 1) * 128], [hkey], ["ps0"])
                cp("act", dstT[:, :, tsl], pb.rearrange("p (k n) -> p k n", k=8), ["ps0"], [dkey])

            def fproj(col0, bnk):
                for k in range(8):
                    mm(bank(bnk), win[:, k, col0:col0 + 128], hT[:, k, :], k == 0, k == 7, ["hT", "win"], ["ps%d" % bnk])

            P.barrier()
            ck(2)
            mA = [mixT[:, r_, :] for r_ in range(4)]
            mB = [mixT[:, 8 + r_, :] for r_ in range(4)]
            kT_ = [kT, mA[0].rearrange("p (h n) -> p h n", h=4)]
            qT_ = [qT, mA[1].rearrange("p (h n) -> p h n", h=4)]
            sgT_ = [sgT, mA[2].rearrange("p (h n) -> p h n", h=4)]
            ktok_ = [ktok, mA[3].rearrange("p (t h d) -> p t h d", t=4, h=4)]
            vtok_ = [vtok, mB[0].rearrange("p (t n) -> p t n", t=4)]
            hT_ = [hT, mixT[:, 9:11, :].rearrange("p a n -> p (a n)").rearrange("p (k n) -> p k n", k=8)]
            RW3 = mB[3][:, 0:1024].bitcast(F32)
            RW1 = mB[3][:, 1024:2048].bitcast(F32)
            dec_ = [dec, A.alloc(32, F32).rearrange("p (h c) -> p h c", h=4)]
            Wsets = [(W1, W2, W3, W4, "a"), (W1b, W2b, W3b, W4b, "b")]
            FM = [1, 2, 3]
            _fm = [0]

            def fm_next():
                bk = FM[_fm[0] % 3]
                _fm[0] += 1
                return bk

            def fprojp(col0, p_):
                bk = fm_next()
                for k in range(8):
                    mm(bank(bk), win[:, k, col0:col0 + 128], hT_[p_][:, k, :], k == 0, k == 7, ["hT%d" % p_, "win"], ["ps%d" % bk])
                return bk

            def front(g):
                p_ = g % 2
                for t in range(4):
                    gt = 4 * g + t
                    xb, xk = xt[gt % 2], "xt%d" % (gt % 2)
                    rcol = rs1[:, gt:gt + 1]
                    dma(xb, xs[gt * 128:(gt + 1) * 128, :], w=[xk], grp="d" + xk, eng="sp" if gt % 2 == 0 else "pool")
                    act(junk, xb, AF.Square, [xk], ["junk", "rs1"], accum=rcol)
                    rsqrt_act(rcol, rcol, 1024, ["rs1"], ["rs1"])
                    ts("dve", hb[0], xb, rcol, None, ALU.mult, None, [xk, "rs1"], ["hb0"])
                    yield
                    pb = bankb(0)
                    for k in range(8):
                        tp(pb[:, k * 128:(k + 1) * 128], hb[0][:, k * 128:(k + 1) * 128], ["hb0"], ["ps0"])
                    cp("act", hT_[p_][:, :, t * 128:(t + 1) * 128], pb.rearrange("p (k n) -> p k n", k=8), ["ps0"], ["hT%d" % p_])
                    yield

            def f_chain(h, bk, p_):
                Wa, Wb, Wc, Wd, sfx = Wsets[h % 2]
                k1, k2, k3, k4 = "W1" + sfx, "W2" + sfx, "W3" + sfx, "W4" + sfx
                pk = "ps%d" % bk
                sigmoid_(Wa, bank(bk), [pk], k1)
                act(Wb, Wa, AF.Ln, [k1, "lb", "oml"], [k2], scale=oml[:, h:h + 1], bias=lb[:, h:h + 1])
                P.add("dve", lambda e, Wc=Wc, Wb=Wb, rmask=rmask: e.tensor_tensor_scan(out=Wc, data0=rmask, data1=Wb, initial=0.0,
                                                                                       op0=ALU.mult, op1=ALU.add),
                      reads=[k2, "vec"], writes=[k3])
                ts("dve", Wa, Wa, noml[:, h:h + 1], oml[:, h:h + 1], ALU.mult, ALU.add, [k1, "oml"], [k1])
                act(Wb, Wc, AF.Exp, [k3], [k2], scale=-1.0)
                act(Wd, Wc, AF.Exp, [k3], [k4])
                cp("dve", dec_[p_][:, h, :], Wd.rearrange("p (c t) -> p c t", t=64)[:, :, 63], [k4], ["dec%d" % p_])
                tt("dve", kT_[p_][:, h, :], Wa, Wb, ALU.mult, [k1, k2], ["kT%d" % p_])

            def kT_transpose(h, p_):
                pb = bankb(0)
                for t in range(4):
                    tp(pb[:, t * 128:(t + 1) * 128], kT_[p_][:, h, t * 128:(t + 1) * 128], ["kT%d" % p_], ["ps0"])
                cp("act", ktok_[p_][:, :, h, :], pb[:, 0:512].rearrange("p (t d) -> p t d", t=4), ["ps0"], ["ktok%d" % p_])

            def qg_chain(h, bq, bg_, p_):
                Wa, Wb, Wc, Wd, sfx = Wsets[h % 2]
                k1, k4 = "W1" + sfx, "W4" + sfx
                sigmoid_(Wa, bank(bq), ["ps%d" % bq], k1)
                tt("dve", Wa, bank(bq), Wa, ALU.mult, ["ps%d" % bq, k1], [k1])
                stt("dve", qT_[p_][:, h, :], Wa, 128 ** -0.5, Wd, ALU.mult, ALU.mult, [k1, k4], ["qT%d" % p_])
                sigmoid_(Wa, bank(bg_), ["ps%d" % bg_], k1)
                tt("dve", sgT_[p_][:, h, :], bank(bg_), Wa, ALU.mult, ["ps%d" % bg_, k1], ["sgT%d" % p_])

            def proj(g, own):
                p_ = g % 2
                hk = "hT%d" % p_
                gsl = slice(g * 512, (g + 1) * 512)
                g4 = slice(4 * g, 4 * g + 4)
                if not own:
                    bf = [fprojp(C_HF + h * 128, p_) for h in range(2)]
                    f_chain(0, bf[0], p_)
                    yield
                    bf.append(fprojp(C_HF + 2 * 128, p_))
                    f_chain(1, bf[1], p_)
                    yield
                    kT_transpose(0, p_)
                    bf.append(fprojp(C_HF + 3 * 128, p_))
                    f_chain(2, bf[2], p_)
                    yield
                    kT_transpose(1, p_)
                    f_chain(3, bf[3], p_)
                    yield
                    kT_transpose(2, p_)
                    yield
                    kT_transpose(3, p_)
                    yield
                else:
                    for h in range(4):
                        bf_ = fprojp(C_HF + h * 128, p_)
                        bq = fprojp(A_HQ + h * 128, p_)
                        f_chain(h, bf_, p_)
                        yield
                        bg_ = fprojp(A_HG + h * 128, p_)
                        qg_chain(h, bq, bg_, p_)
                        yield
                        kT_transpose(h, p_)
                        yield
                for t in range(4):
                    bk = fm_next()
                    for k in range(8):
                        mm(bank(bk), hT_[p_][:, k, t * 128:(t + 1) * 128], win[:, k, C_HI:C_HI + 512], k == 0, k == 7, [hk, "win"], ["ps%d" % bk])
                    cp("dve", vtok_[p_][:, t, :], bank(bk), ["ps%d" % bk], ["vtok%d" % p_])
                    yield
                bc = [fprojp(C_KV + m * 128, p_) for m in range(2)]
                for m in range(2):
                    bk = bc[m]
                    cp("act", ckvT[:, m, gsl], bank(bk), ["ps%d" % bk], ["ckvT"])
                    act(sqt[:, m, :], bank(bk), AF.Square, ["ps%d" % bk], ["sqt"])
                yield
                for t in range(4):
                    for m in range(2):
                        mm(bank(4, 1, 256 + t), sqt[:, m, t * 128:(t + 1) * 128], onesb[:, 0:1], m == 0, m == 1, ["sqt", "cbf"], ["ps4"])
                rsqrt_act(rskv[:, g4], bank(4, 4, 256), 256, ["ps4"], ["rskv"])
                tt("dve", rskv2[:, g4], rskv[:, g4], rskv[:, g4], ALU.mult, ["rskv"], ["rskv2"])
                yield
                for t in range(4):
                    for k in range(8):
                        mm(bank(4, 64, t * 64), hT_[p_][:, k, t * 128:(t + 1) * 128], win[:, k, C_KR:C_KR + 64], k == 0, k == 7,
                           [hk, "win"], ["ps4"])
                for t in range(4):
                    act(junk[:, 0:64], bank(4, 64, t * 64), AF.Square, ["ps4"], ["junk", "sskr"], accum=sskr[:, 4 * g + t:4 * g + t + 1])
                yield
                for t in range(4):
                    tt("dve", krg[:, t, :], bank(4, 64, t * 64), vec[:, V_GK + 128:V_GK + 192], ALU.mult, ["ps4", "vec"], ["krg"])
                yield
                cs_tables(g4)
                yield
                rope(kr[:, g4, 0:32], kr[:, g4, 32:64], krg[:, :, 0:32], krg[:, :, 32:64], cst, snt, ["krg", "cst", "snt"], ["kr"])
                yield

            def rec(g, own):
                p_ = g % 2
                kTp, qTp, sgTp, ktokp, vtokp, decp = kT_[p_], qT_[p_], sgT_[p_], ktok_[p_], vtok_[p_], dec_[p_]
                kk, qk, sgk, ktk, vtk, dk = ["%s%d" % (n_, p_) for n_ in ("kT", "qT", "sgT", "ktok", "vtok", "dec")]
                osl0 = (g - OG0) * 512
                for t in range(4):
                    tsl = slice(t * 128, (t + 1) * 128)
                    if own:
                        for h in range(4):
                            mm(bank(6, 128, h * 128), kTp[:, h, tsl], qTp[:, h, tsl], True, True, [kk, qk], ["ps6"])
                        tt("dve", Amb, bank(6).rearrange("p (h n) -> p h n", h=4), tri2.unsqueeze(1).to_broadcast([128, 4, 128]),
                           ALU.mult, ["ps6", "cbf"], ["Amb"])
                    for c2 in range(2):
                        rows = slice(c2 * 64, (c2 + 1) * 64)
                        csl = slice(t * 128 + c2 * 64, t * 128 + (c2 + 1) * 64)
                        ci = t * 2 + c2
                        if own:
                            for h in range(4):
                                mm(bank(7, 64, h * 128 + c2 * 64), vtokp[rows, t, h * 128:(h + 1) * 128], Amb[rows, h, rows], True, False,
                                   [vtk, "Amb"], ["ps7"])
                                mm(bank(7, 64, h * 128 + c2 * 64), Sb[:, h * 128:(h + 1) * 128], qTp[:, h, csl], False, True,
                                   ["Sb", qk], ["ps7"])
                        for h in range(4):
                            mm(bank(5, 128, h * 128), ktokp[rows, t, h, :], vtokp[rows, t, h * 128:(h + 1) * 128], True, True,
                               [ktk, vtk], ["ps5"])
                        tt("dve", RW3, S32, bank(5), ALU.add, ["S32", "ps5"], ["RW3"])
                        tt("dve", S32.rearrange("p (h n) -> p h n", h=4), RW3.rearrange("p (h n) -> p h n", h=4),
                           decp[:, :, ci:ci + 1].to_broadcast([128, 4, 128]), ALU.mult, ["RW3", dk], ["S32"])
                        if own or (g == OG0 - 1 and ci == 7):
                            cp("act", Sb, S32, ["S32"], ["Sb"])
                        yield
                    if own:
                        act(pm[0], bank(7), AF.Square, ["ps7"], ["pm0"])
                        mm(bank(6), onesb, pm[0], True, True, ["cbf", "pm0"], ["ps6"])
                        rsqrt_act(RW1, bank(6), 128, ["ps6"], ["RW1"])
                        tt("dve", RW1, RW1, bank(7), ALU.mult, ["RW1", "ps7"], ["RW1"])
                        for h in range(4):
                            stt("dve", mixT[:, 4 + h, osl0 + t * 128:osl0 + (t + 1) * 128], RW1[:, h * 128:(h + 1) * 128],
                                vec[:, V_GHGO + h:V_GHGO + h + 1], sgTp[:, h, tsl], ALU.mult, ALU.mult, ["RW1", "vec", sgk], ["mixT"])

            def interleave(gens):
                gens = [g_ for g_ in gens if g_ is not None]
                while gens:
                    for g_ in list(gens):
                        try:
                            next(g_)
                        except StopIteration:
                            gens.remove(g_)

            interleave([front(0)])
            interleave([proj(0, False), front(1)])
            for g in range(NG):
                if g == 1:
                    ck(3)
                if g == OG0 + 1:
                    ck(35)
                interleave([rec(g, g >= OG0),
                            proj(g + 1, g + 1 >= OG0) if g + 1 < NG else None,
                            front(g + 2) if g + 2 < NG else None])

            P.barrier()
            ck(4)
            A.off = p1_off
            qTn = A.alloc(4 * OWN).rearrange("p (h n) -> p h n", h=4)
            qTr = A.alloc(2 * OWN).rearrange("p (h n) -> p h n", h=2)
            p2_off = A.off
            NB = 896
            win = A.alloc(8 * NB).rearrange("p (k n) -> p k n", k=8)
            wuq = A.alloc(3 * 768).rearrange("p (k n) -> p k n", k=3)
            kmemT = A.alloc(4 * 256).rearrange("p (h n) -> p h n", h=4)
            vmem = A.alloc(2 * 512).rearrange("p (t n) -> p t n", t=2)
            xt = [A.alloc(1024, F32) for _ in range(2)]
            junk = A.alloc(1024)
            hb = [A.alloc(1024) for _ in range(2)]
            W1 = A.alloc(512, F32)
            W2 = A.alloc(512, F32)
            p1b_off = A.off
            stg = [A.alloc(1024, F32) for _ in range(3)]
            wmkv = A.alloc(8 * 1024).rearrange("p (k n) -> p k n", k=8)
            memT = A.alloc(8 * 256).rearrange("p (k n) -> p k n", k=8)
            kmtok = A.alloc(512).rearrange("p (h d) -> p h d", h=4)
            for k in range(8):
                load_w(win[:, k, 0:384], w_in_v[:, k, C_CQ:C_CQ + 384], 384, vec[:, V_GMIX + k:V_GMIX + k + 1], "win")
                load_w(win[:, k, 384:NB], w_in_v[:, k, C_MQ:C_MQ + 512], 512, vec[:, V_GMIX + k:V_GMIX + k + 1], "win")
            w_uq_v = w_uq.rearrange("(k p) n -> p k n", p=128)
            for k in range(3):
                load_w(wuq[:, k, :], w_uq_v[:, k, :], 768, vec[:, V_GQA + k:V_GQA + k + 1], "wuq")
            w_mkv_v = w_mkv.rearrange("(k p) n -> p k n", p=128)
            for k in range(8):
                load_w(wmkv[:, k, :], w_mkv_v[:, k, :], 1024, vec[:, V_GMEM + k:V_GMEM + k + 1], "wmkv")
            for mt in range(2):
                rms_load_transpose(memd[mt * 128:(mt + 1) * 128, :], xt[mt], "xt%d" % mt, hb[mt], "hb%d" % mt,
                                   scr[:, mt:mt + 1], "scr%d" % mt, memT, "memT", slice(mt * 128, (mt + 1) * 128), "sp" if mt == 0 else "pool")
            for mt in range(2):
                for half in range(2):
                    for k in range(8):
                        mm(bank(1 + half), memT[:, k, mt * 128:(mt + 1) * 128], wmkv[:, k, half * 512:(half + 1) * 512],
                           k == 0, k == 7, ["memT", "wmkv"], ["ps%d" % (1 + half)])
                kps = bank(1).rearrange("p (h d) -> p h d", h=4)
                act(W1, bank(1), AF.Square, ["ps1"], ["W1"])
                P.add("dve", lambda e: e.tensor_reduce(out=scr[:, 8:12], in_=W1.rearrange("p (h d) -> p h d", h=4), axis=AX.X, op=ALU.add),
                      reads=["W1"], writes=["scr8"])
                rsqrt_act(scr[:, 8:12], scr[:, 8:12], 128, ["scr8"], ["scr8"])
                W1v = W1.rearrange("p (h d) -> p h d", h=4)
                tt("dve", W1v, kps, scr[:, 8:12].unsqueeze(2).to_broadcast([128, 4, 128]), ALU.mult, ["ps1", "scr8"], ["W1"])
                tt("dve", kmtok, W1v, vec[:, V_GMK:V_GMK + 128].unsqueeze(1).to_broadcast([128, 4, 128]), ALU.mult,
                   ["W1", "vec"], ["kmtok"])
                pb = bankb(0)
                for h in range(4):
                    tp(pb[:, h * 128:(h + 1) * 128], kmtok[:, h, :], ["kmtok"], ["ps0"])
                cp("act", kmemT[:, :, mt * 128:(mt + 1) * 128], pb[:, 0:512].rearrange("p (h n) -> p h n", h=4), ["ps0"], ["kmemT"])
                cp("act", vmem[:, mt, :], bank(2), ["ps2"], ["vmem"])
            P.barrier()
            ck(5)
            A.off = p1b_off
            hT = A.alloc(8 * 512).rearrange("p (k n) -> p k n", k=8)
            sqt = A.alloc(3 * 512).rearrange("p (k n) -> p k n", k=3)
            cqT = A.alloc(3 * 512).rearrange("p (k n) -> p k n", k=3)
            rp = [A.alloc(4 * 32, F32).rearrange("p (t d) -> p t d", d=32) for _ in range(4)]
            cst = A.alloc(4 * 32, F32).rearrange("p (t d) -> p t d", d=32)
            snt = A.alloc(4 * 32, F32).rearrange("p (t d) -> p t d", d=32)
            tA = A.alloc(128, F32)
            tB = A.alloc(128, F32)
            tI = A.alloc(128, I32)
            qn = A.alloc(768, F32).rearrange("p (h d) -> p h d", h=4)
            qbn = A.alloc(4 * 128).rearrange("p (h d) -> p h d", h=4)
            qbr = A.alloc(4 * 64).rearrange("p (h d) -> p h d", h=4)
            mqT = A.alloc(4 * 512).rearrange("p (h n) -> p h n", h=4)
            pm = [A.alloc(512) for _ in range(2)]
            sqy = A.alloc(4 * 512).rearrange("p (h n) -> p h n", h=4)
            qn_ = [qn, A.alloc(768, F32).rearrange("p (h d) -> p h d", h=4)]
            qbn_ = [qbn, A.alloc(4 * 128).rearrange("p (h d) -> p h d", h=4)]
            qbr_ = [qbr, A.alloc(4 * 64).rearrange("p (h d) -> p h d", h=4)]
            mqb = A.alloc(4 * 128).rearrange("p (h d) -> p h d", h=4)

            def fb(g):
                for t in range(4):
                    gt = 4 * g + t
                    rms_load_transpose(xs[gt * 128:(gt + 1) * 128, :], xt[gt % 2], "xt%d" % (gt % 2), hb[gt % 2], "hb%d" % (gt % 2),
                                       scr[:, 2 + t:3 + t], "scr%d" % (2 + t), hT, "hT", slice(t * 128, (t + 1) * 128), "sp" if gt % 2 == 0 else "pool")
                    yield

            def qstream(g):
                g4 = slice(4 * g, 4 * g + 4)
                cs_tables(g4)
                yield
                for m in range(3):
                    b_ = 1 + (m % 2)
                    fproj(m * 128, b_)
                    cp("act", cqT[:, m, :], bank(b_), ["ps%d" % b_], ["cqT"])
                    act(sqt[:, m, :], bank(b_), AF.Square, ["ps%d" % b_], ["sqt"])
                yield
                for t in range(4):
                    for m in range(3):
                        mm(bank(4, 1, t), sqt[:, m, t * 128:(t + 1) * 128], onesb[:, 0:1], m == 0, m == 2, ["sqt", "cbf"], ["ps4"])
                ts("dve", scr[:, 16:20], bank(4, 4), EPS / 384.0, EPS * EPS, ALU.mult, ALU.add, ["ps4"], ["scr16"])
                yield
                for t in range(4):
                    q_ = t % 2
                    gt = 4 * g + t
                    tsl = slice(t * 128, (t + 1) * 128)
                    qb0 = 6 if q_ == 0 else 2
                    pk = ["ps%d" % qb0, "ps%d" % (qb0 + 1)]
                    qnq, qbnq, qbrq = qn_[q_], qbn_[q_], qbr_[q_]
                    nk, bnk, brk = "qn%d" % q_, "qbn%d" % q_, "qbr%d" % q_
                    sc0 = 20 + 4 * q_
                    sck = "scr%d" % sc0
                    scq = scr[:, sc0:sc0 + 4]
                    for (c0, n, off) in ((0, 512, 0), (512, 256, 512)):
                        for m in range(3):
                            mm(ps[:, qb0 * 512 + off:qb0 * 512 + off + n], cqT[:, m, tsl], wuq[:, m, c0:c0 + n], m == 0, m == 2,
                               ["cqT", "wuq"], [pk[off // 512]])
                    qps = ps[:, qb0 * 512:qb0 * 512 + 768]
                    qps3 = qps.rearrange("p (h d) -> p h d", h=4)
                    act(qnq.rearrange("p h d -> p (h d)"), qps, AF.Square, pk, [nk])
                    P.add("dve", lambda e, scq=scq, qnq=qnq: e.tensor_reduce(out=scq, in_=qnq, axis=AX.X, op=ALU.add), reads=[nk], writes=[sck])
                    ts("dve", scq, scq, 1.0 / 192, scr[:, 16 + t:17 + t], ALU.mult, ALU.add, [sck, "scr16"], [sck])
                    yield
                    act(scq, scq, AF.Ln, [sck], [sck])
                    act(scq, scq, AF.Exp, [sck], [sck], scale=-0.5)
                    tt("dve", qnq, qps3, scq.unsqueeze(2).to_broadcast([128, 4, 192]), ALU.mult, pk + [sck], [nk])
                    tt("dve", qnq, qnq, vec[:, V_GQ:V_GQ + 192].unsqueeze(1).to_broadcast([128, 4, 192]), ALU.mult, [nk, "vec"], [nk])
                    yield
                    cp("act", qbnq, qnq[:, :, 0:128], [nk], [bnk])
                    cs = cst[:, t, :].unsqueeze(1).to_broadcast([128, 4, 32])
                    sn = snt[:, t, :].unsqueeze(1).to_broadcast([128, 4, 32])
                    rope(qbrq[:, :, 0:32], qbrq[:, :, 32:64], qnq[:, :, 128:160], qnq[:, :, 160:192], cs, sn, [nk, "cst", "snt"], [brk])
                    yield
                    pb = bankb(0)
                    for h in range(4):
                        tp(pb[:, h * 128:(h + 1) * 128], qbnq[:, h, :], [bnk], ["ps0"])
                    qbrf = qbrq.rearrange("p h d -> p (h d)")
                    for pr in range(2):
                        tp(pb[:, 512 + pr * 128:512 + (pr + 1) * 128], qbrf[:, pr * 128:(pr + 1) * 128], [brk], ["ps0"])
                    osl_t = slice((g - OG0) * 512 + t * 128, (g - OG0) * 512 + (t + 1) * 128)
                    cp("act", qTn[:, :, osl_t], pb[:, 0:512].rearrange("p (h n) -> p h n", h=4), ["ps0"], ["qTn"])
                    cp("act", qTr[:, :, osl_t], pb[:, 512:768].rearrange("p (h n) -> p h n", h=2), ["ps0"], ["qTr"])
                    yield

            def mstream(g):
                for t in range(4):
                    tsl = slice(t * 128, (t + 1) * 128)
                    for k in range(8):
                        mm(bank(5), hT[:, k, tsl], win[:, k, 384:NB], k == 0, k == 7, ["hT", "win"], ["ps5"])
                    act(W1, bank(5), AF.Square, ["ps5"], ["W1"])
                    P.add("dve", lambda e: e.tensor_reduce(out=scr[:, 28:32], in_=W1.rearrange("p (h d) -> p h d", h=4), axis=AX.X, op=ALU.add),
                          reads=["W1"], writes=["scr28"])
                    rsqrt_act(scr[:, 28:32], scr[:, 28:32], 128, ["scr28"], ["scr28"])
                    yield
                    W1v = W1.rearrange("p (h d) -> p h d", h=4)
                    tt("dve", W1v, bank(5).rearrange("p (h d) -> p h d", h=4), scr[:, 28:32].unsqueeze(2).to_broadcast([128, 4, 128]),
                       ALU.mult, ["ps5", "scr28"], ["W1"])
                    tt("dve", mqb, W1v, vec[:, V_GMQ:V_GMQ + 128].unsqueeze(1).to_broadcast([128, 4, 128]), ALU.mult, ["W1", "vec"], ["mqb"])
                    yield
                    pb = bankb(5)
                    for h in range(4):
                        tp(pb[:, h * 128:(h + 1) * 128], mqb[:, h, :], ["mqb"], ["ps5"])
                    cp("act", mqT[:, :, tsl], pb[:, 0:512].rearrange("p (h n) -> p h n", h=4), ["ps5"], ["mqT"])
                    yield

            def ystream(g):
                osl = slice((g - OG0) * 512, (g - OG0 + 1) * 512)
                for h in range(4):
                    for mt in range(2):
                        mm(bank(1 + mt), kmemT[:, h, mt * 128:(mt + 1) * 128], mqT[:, h, :], True, True, ["kmemT", "mqT"], ["ps%d" % (1 + mt)])
                        act(pm[mt], bank(1 + mt), AF.Exp, ["ps%d" % (1 + mt)], ["pm%d" % mt], scale=128 ** -0.5)
                    yield
                    for mt in range(2):
                        mm(bank(6), vmem[:, mt, h * 128:(h + 1) * 128], pm[mt], mt == 0, mt == 1, ["vmem", "pm%d" % mt], ["ps6"])
                    for mt in range(2):
                        mm(bank(7), onesb, pm[mt], mt == 0, mt == 1, ["cbf", "pm%d" % mt], ["ps7"])
                    yield
                    P.add("dve", lambda e: e.reciprocal(out=W2, in_=bank(7)), reads=["ps7"], writes=["W2"])
                    tt("dve", W2, bank(6), W2, ALU.mult, ["ps6", "W2"], ["W2"])
                    cp("act", mixT[:, 8 + h, osl], W2, ["W2"], ["mixT"])
                    act(sqy[:, h, :], W2, AF.Square, ["W2"], ["sqy"])
                    yield
                for t in range(4):
                    for h in range(4):
                        mm(bank(4, 1, t), sqy[:, h, t * 128:(t + 1) * 128], onesb[:, 0:1], h == 0, h == 3, ["sqy", "cbf"], ["ps4"])
                o4 = slice(4 * (g - OG0), 4 * (g - OG0) + 4)
                cp("dve", ssmem[:, o4], bank(4, 4), ["ps4"], ["ssmem"])
                yield

            interleave([fb(OG0)])
            for g in range(OG0, NG):
                interleave([qstream(g), mstream(g)])
                interleave([ystream(g), fb(g + 1) if g + 1 < NG else None])

            P.barrier()
            ck(6)
            A.off = p2_off
            junk = A.alloc(1024)
            KTn = A.alloc(NS)
            KTr = A.alloc(NS)
            vt = A.alloc(NT * 128).rearrange("p (t d) -> p t d", d=128)
            kbn2 = [A.alloc(4 * 128).rearrange("p (t d) -> p t d", t=4) for _ in range(2)]
            kbr2 = [A.alloc(4 * 128).rearrange("p (t d) -> p t d", t=4) for _ in range(2)]
            pt = [A.alloc(512) for _ in range(4)]
            R1 = A.alloc(512, F32)
            Y1 = A.alloc(512, F32)
            sq2 = A.alloc(512)
            for p_ in range(2):
                P.add("dve", lambda e, p_=p_: e.memset(kbr2[p_].rearrange("p t d -> p (t d)"), 0.0), writes=["kbr%d" % p_])
            bgS = [A.alloc(1024, F32) for _ in range(2)]
            bgC = [A.alloc(1024) for _ in range(2)]

            def bg_convert():
                cnt = 0
                for a_, wsrc in enumerate((w_gate, w_up)):
                    wv = wsrc.rearrange("(k p) n -> p k n", p=128)
                    for k in range(8):
                        for c0 in range(0, DFF, 1024):
                            n = min(1024, DFF - c0)
                            i = cnt % 2
                            cnt += 1
                            dma(bgS[i][:, 0:n], wv[:, k, c0:c0 + n], w=["bgS%d" % i], grp="dbgs%d" % i, eng="sp")
                            ts("dve", bgC[i][:, 0:n], bgS[i][:, 0:n], vec[:, V_GFFN + k:V_GFFN + k + 1], None, ALU.mult, None,
                               ["bgS%d" % i, "vec"], ["bgC%d" % i])
                            f0, f1 = c0 // 128, (c0 + n) // 128
                            cc = a_ * 1024 + k * 128
                            dma(wgu_scr[f0:f1, :, cc:cc + 128].rearrange("f p n -> p f n"),
                                bgC[i][:, 0:n].rearrange("p (f n) -> p f n", n=128), r=["bgC%d" % i], w=["wgu_scr"],
                                grp="dbgo%d" % i, eng="pool")
                            yield

            bg = bg_convert()
            SC = 192 ** -0.5
            mixhm = mixT[:, 4:12, :].rearrange("p a n -> p (a n)")
            dma(mix_scr, mixhm, r=["mixT"], w=["mix_scr"], grp="dmsp")
            P.barrier()
            KTn_ = [KTn, mixT[:, 4:8, :].rearrange("p a n -> p (a n)")]
            vt_ = [vt, mixT[:, 8:12, :].rearrange("p a n -> p (a n)").rearrange("p (t d) -> p t d", d=128)]
            ksc = A.alloc(512, F32)
            P.add("dve", lambda e: e.memset(scr[:, 48:52], -0.5), writes=["scr48"])

            def kbuild(h, alone=False):
                ro = (h % 2) * 64
                s_ = h % 2
                KTs, vts = KTn_[s_], vt_[s_]
                kn_k, vt_k, kr_k = "KTn%d" % s_, "vt%d" % s_, "KTr%d" % s_
                for g in range(NG):
                    p_ = g % 2
                    b0, bT = 1, 0
                    kscb, kkey = ksc, "ksc"
                    if alone and p_ == 1:
                        b0, bT = 3, 5
                        kscb, kkey = Y1, "Y1"
                    pk = ["ps%d" % b0, "ps%d" % (b0 + 1)]
                    g4 = slice(4 * g, 4 * g + 4)
                    for ti in range(4):
                        tl = g * 4 + ti
                        for m in range(2):
                            mm(ps[:, b0 * 512 + ti * 256:b0 * 512 + (ti + 1) * 256], ckvT[:, m, tl * 128:(tl + 1) * 128],
                               wukv[:, m, h * 256:(h + 1) * 256], m == 0, m == 1, ["ckvT", "wukv"], [pk[ti // 2]])
                    kvv = ps[:, b0 * 512:b0 * 512 + 1024].rearrange("p (t c) -> p t c", c=256)
                    ksc3 = kscb.rearrange("p (t d) -> p t d", d=128)
                    c_ = 32 + p_ * 8
                    ssq_, c1_ = scr[:, c_:c_ + 4], scr[:, c_ + 4:c_ + 8]
                    sk = "scr%d" % c_
                    tt("dve", vts[:, g4, :], kvv[:, :, 128:256], rskv[:, g4].unsqueeze(2).to_broadcast([128, 4, 128]), ALU.mult,
                       pk + ["rskv"], [vt_k])
                    if alone:
                        act(ksc3, kvv[:, :, 0:128], AF.Square, pk, [kkey])
                        P.add("dve", lambda e, ssq_=ssq_, ksc3=ksc3: e.tensor_reduce(out=ssq_, in_=ksc3, axis=AX.X, op=ALU.add),
                              reads=[kkey], writes=[sk])
                        tt("dve", ssq_, ssq_, rskv2[:, g4], ALU.mult, [sk, "rskv2"], [sk])
                        tt("dve", ssq_, ssq_, sskr[:, g4], ALU.add, [sk, "sskr"], [sk])
                        yield
                        rsqrt_act(ssq_, ssq_, 192, [sk], [sk])
                        tt("dve", c1_, ssq_, rskv[:, g4], ALU.mult, [sk, "rskv"], [sk])
                        tt("dve", ksc3, kvv[:, :, 0:128], c1_.unsqueeze(2).to_broadcast([128, 4, 128]), ALU.mult, pk + [sk], [kkey])
                    else:
                        sqb = junk[:, 0:512].rearrange("p (t d) -> p t d", d=128)
                        cp("dve", ksc3, kvv[:, :, 0:128], pk, [kkey])
                        tt("dve", sqb, ksc3, ksc3, ALU.mult, [kkey], ["junk"])
                        P.add("dve", lambda e, ssq_=ssq_, sqb=sqb: e.tensor_reduce(out=ssq_, in_=sqb, axis=AX.X, op=ALU.add),
                              reads=["junk"], writes=[sk])
                        tt("dve", ssq_, ssq_, rskv2[:, g4], ALU.mult, [sk, "rskv2"], [sk])
                        tt("dve", ssq_, ssq_, sskr[:, g4], ALU.add, [sk, "sskr"], [sk])
                        ts("dve", ssq_, ssq_, 1.0 / 192, EPS, ALU.mult, ALU.add, [sk], [sk])
                        yield
                        tt("pool", ssq_, ssq_, scr[:, 48:52], ALU.pow, [sk, "scr48"], [sk])
                        tt("dve", c1_, ssq_, rskv[:, g4], ALU.mult, [sk, "rskv"], [sk])
                        tt("dve", ksc3, ksc3, c1_.unsqueeze(2).to_broadcast([128, 4, 128]), ALU.mult, [kkey, sk], [kkey])
                    tt("dve", kbn2[p_], ksc3, vec[:, V_GK:V_GK + 128].unsqueeze(1).to_broadcast([128, 4, 128]), ALU.mult,
                       [kkey, "vec"], ["kbn%d" % p_])
                    tt("dve", kbr2[p_][:, :, ro:ro + 64], kr[:, g4, :], ssq_.unsqueeze(2).to_broadcast([128, 4, 64]), ALU.mult,
                       ["kr", sk], ["kbr%d" % p_])
                    yield
                    pb = bankb(bT)
                    for ti in range(4):
                        tp(pb[:, ti * 128:(ti + 1) * 128], kbn2[p_][:, ti, :], ["kbn%d" % p_], ["ps%d" % bT])
                        tp(pb[:, 512 + ti * 128:512 + (ti + 1) * 128], kbr2[p_][:, ti, :], ["kbr%d" % p_], ["ps%d" % bT])
                    ev = "act" if alone else "dve"
                    cp(ev, KTs[:, g * 512:(g + 1) * 512], pb[:, 0:512], ["ps%d" % bT], [kn_k])
                    cp(ev, KTr[ro:ro + 64, g * 512:(g + 1) * 512], pb[ro:ro + 64, 512:1024], ["ps%d" % bT], [kr_k])
                    if g % 4 == 3:
                        next(bg, None)
                    yield

            for _ in kbuild(0, alone=True):
                pass
            for h in range(4):
                ro = (h % 2) * 64
                KTn, vt = KTn_[h % 2], vt_[h % 2]
                kn_k, vt_k, kr_k = "KTn%d" % (h % 2), "vt%d" % (h % 2), "KTr%d" % (h % 2)
                kb = kbuild(h + 1) if h < 3 else iter(())
                tile_ctr = [0]
                for Q in range(4):
                    qsl0 = Q * 512
                    nkt = OT0 + 4 * Q + 4
                    def tile_geo(kt):
                        j = kt - (OT0 + 4 * Q)
                        q0 = 0 if j < 0 else 128 * j
                        return j, q0, 512 - q0

                    def emit_S(kt):
                        j, q0, n = tile_geo(kt)
                        sb_ = 3 + (kt % 2)
                        ksl = slice(kt * 128, (kt + 1) * 128)
                        qs = slice(qsl0 + q0, qsl0 + 512)
                        mm(bank(sb_, n, q0), KTn[:, ksl], qTn[:, h, qs], True, False, [kn_k, "qTn"], ["ps%d" % sb_])
                        mm(bank(sb_, n, q0), KTr[ro:ro + 64, ksl], qTr[ro:ro + 64, h // 2, qs], False, True, [kr_k, "qTr"], ["ps%d" % sb_])

                    emit_S(0)
                    for kt in range(nkt):
                        if kt % 16 == 8:
                            next(bg, None)
                        tile_ctr[0] += 1
                        if tile_ctr[0] % 4 == 2:
                            next(kb, None)
                        if kt + 1 < nkt:
                            emit_S(kt + 1)
                        j, q0, n = tile_geo(kt)
                        sb_ = 3 + (kt % 2)
                        pti = kt % 4
                        act(pt[pti][:, q0:512], bank(sb_, n, q0), AF.Exp, ["ps%d" % sb_], ["pt%d" % pti], scale=SC)
                        if j >= 0:
                            tt("dve", pt[pti][:, q0:q0 + 128], pt[pti][:, q0:q0 + 128], tri, ALU.mult, ["pt%d" % pti, "cbf"], ["pt%d" % pti])
                        reg = kt // 16
                        von = onesb if reg >= 3 else vones[:, reg * 128:(reg + 1) * 128]
                        mm(bank(5, n, q0), vt[:, kt, :], pt[pti][:, q0:512], kt == 0, kt == nkt - 1, [vt_k, "pt%d" % pti], ["ps5"])
                        mm(bank(6, n, q0), von, pt[pti][:, q0:512], kt == 0, kt == nkt - 1, ["vones", "cbf", "pt%d" % pti], ["ps6"])
                    P.add("dve", lambda e: e.reciprocal(out=R1, in_=bank(6)), reads=["ps6"], writes=["R1"])
                    tt("dve", Y1, bank(5), R1, ALU.mult, ["ps5", "R1"], ["Y1"])
                    cp("act", mixT[:, h, qsl0:qsl0 + 512], Y1, ["Y1"], ["mixT"])
                    act(sq2, Y1, AF.Square, ["Y1"], ["sq2"])
                    for t in range(4):
                        mm(bank(7, 1, t), sq2[:, t * 128:(t + 1) * 128], onesb[:, 0:1], True, True, ["sq2", "cbf"], ["ps7"])
                    o4 = slice(4 * Q, 4 * Q + 4)
                    tt("dve", ssmla[:, o4], ssmla[:, o4], bank(7, 4), ALU.add, ["ssmla", "ps7"], ["ssmla"])
                for _ in kb:
                    pass

            for _ in bg:
                pass
            P.barrier()
            ck(7)
            dma(mixhm, mix_scr, r=["mix_scr"], w=["mixT"], grp="dmsp")
            A.off = base_off
            wout = A.alloc(12 * 1024).rearrange("p (k n) -> p k n", k=12)
            wdn = A.alloc(NF * 1024).rearrange("p (k n) -> p k n", k=NF)
            p3_off = A.off
            stg4 = [A.alloc(1024, F32) for _ in range(6)]
            rsqrt_act(rsmla, ssmla, 512, ["ssmla"], ["rsmla"])
            rsqrt_act(rsmem, ssmem, 512, ["ssmem"], ["rsmem"])
            w_out_v = w_out.rearrange("(k p) n -> p k n", p=128)
            w_dn_v = w_down.rearrange("(k p) n -> p k n", p=128)
            jobs = [(wout[:, k, :], w_out_v[:, k, :], None if 4 <= k < 8 else vec[:, V_GOUT + k:V_GOUT + k + 1], "wout") for k in range(12)]
            jobs += [(wdn[:, k, :], w_dn_v[:, k, :], None, "wdn") for k in range(NF)]
            for ji, (dst, src, sc, dkey) in enumerate(jobs):
                i = ji % 6
                dma(stg4[i], src, w=["stg4_%d" % i], grp="dstg4_%d" % i, eng=("sp", "pool", "act")[ji % 3])
                if ji % 2 == 0:
                    if sc is None:
                        cp("dve", dst, stg4[i], ["stg4_%d" % i], [dkey])
                    else:
                        ts("dve", dst, stg4[i], sc, None, ALU.mult, None, ["stg4_%d" % i, "vec"], [dkey])
                else:
                    act(dst, stg4[i], AF.Copy, ["stg4_%d" % i, "vec"], [dkey], scale=(1.0 if sc is None else sc))
            P.barrier()
            A.off = p3_off
            xo = [A.alloc(1024, F32) for _ in range(4)]
            h2b = [A.alloc(1024)] * 2
            h2T = A.alloc(8 * 512).rearrange("p (k n) -> p k n", k=8)
            actT = A.alloc(NF * 512).rearrange("p (k n) -> p k n", k=NF)
            wgu = [A.alloc(2048).rearrange("p (a k n) -> p a k n", a=2, k=8) for _ in range(3)]
            G1 = A.alloc(512, F32)
            G2 = A.alloc(512, F32)
            yo = [A.alloc(512, F32) for _ in range(2)]
            junk3 = A.alloc(1024)
            ck(8)
            for Gq in range(4):
                for t in range(4):
                    ot = Gq * 4 + t
                    tok = slice(ot * 128, (ot + 1) * 128)
                    dma(xo[t], xs[OWN0 + ot * 128:OWN0 + (ot + 1) * 128, :], w=["xo%d" % t], grp="dxo%d" % t, eng="sp" if t % 2 == 0 else "pool")
                    pa = ps[:, 1 * 512:3 * 512]
                    pbk = ps[:, 3 * 512:5 * 512]
                    pc = ps[:, 5 * 512:7 * 512]
                    for (k0, pk, keys) in ((0, 1, ["ps1", "ps2"]), (4, 3, ["ps3", "ps4"]), (8, 5, ["ps5", "ps6"])):
                        for nh in range(2):
                            for k in range(4):
                                mm(bank(pk + nh), mixT[:, k0 + k, tok], wout[:, k0 + k, nh * 512:(nh + 1) * 512], k == 0, k == 3,
                                   ["mixT", "wout"], [keys[nh]])
                    stt("dve", xo[t], pa, rsmla[:, ot:ot + 1], xo[t], ALU.mult, ALU.add, ["ps1", "ps2", "rsmla", "xo%d" % t], ["xo%d" % t])
                    tt("dve", xo[t], xo[t], pbk, ALU.add, ["xo%d" % t, "ps3", "ps4"], ["xo%d" % t])
                    stt("dve", xo[t], pc, rsmem[:, ot:ot + 1], xo[t], ALU.mult, ALU.add, ["ps5", "ps6", "rsmem", "xo%d" % t], ["xo%d" % t])
                    c_ = 40 + t
                    act(junk3, xo[t], AF.Square, ["xo%d" % t], ["junk3", "scr%d" % c_], accum=scr[:, c_:c_ + 1])
                    rsqrt_act(scr[:, c_:c_ + 1], scr[:, c_:c_ + 1], 1024, ["scr%d" % c_], ["scr%d" % c_])
                    ts("dve", h2b[t % 2], xo[t], scr[:, c_:c_ + 1], None, ALU.mult, None, ["xo%d" % t, "scr%d" % c_], ["h2b"])
                    pb = bankb(0)
                    for k in range(8):
                        tp(pb[:, k * 128:(k + 1) * 128], h2b[t % 2][:, k * 128:(k + 1) * 128], ["h2b"], ["ps0"])
                    cp("act", h2T[:, :, t * 128:(t + 1) * 128], pb.rearrange("p (k n) -> p k n", k=8), ["ps0"], ["h2T"])
                for f in range(NF):
                    wi = f % 3
                    dma(wgu[wi].rearrange("p a k n -> p (a k n)"), wgu_scr[f, :, :], r=["wgu_scr"], w=["wgu%d" % wi], grp="dwgu%d" % wi,
                        eng="pool" if wi == 1 else "sp")
                    bg = 1 + 2 * (f % 2)
                    for a_ in range(2):
                        for k in range(8):
                            mm(bank(bg + a_), wgu[wi][:, a_, k, :], h2T[:, k, :], k == 0, k == 7, ["wgu%d" % wi, "h2T"], ["ps%d" % (bg + a_)])
                    Gb = G1 if f % 2 == 0 else G2
                    gk = "G%d" % (f % 2)
                    act(Gb, bank(bg), AF.Silu, ["ps%d" % bg], [gk])
                    tt("dve", actT[:, f, :], bank(bg + 1), Gb, ALU.mult, ["ps%d" % (bg + 1), gk], ["actT"])
                for t in range(4):
                    ot = Gq * 4 + t
                    for nh in range(2):
                        bi = 5 + nh
                        for f in range(NF):
                            mm(bank(bi), actT[:, f, t * 128:(t + 1) * 128], wdn[:, f, nh * 512:(nh + 1) * 512], f == 0, f == NF - 1,
                               ["actT", "wdn"], ["ps%d" % bi])
                        tt("dve", yo[nh], bank(bi), xo[t][:, nh * 512:(nh + 1) * 512], ALU.add, ["ps%d" % bi, "xo%d" % t], ["yo%d" % nh])
                        dma(yd[ot * 128:(ot + 1) * 128, nh * 512:(nh + 1) * 512], yo[nh], r=["yo%d" % nh], grp="dyo%d" % nh)
        except _Stop:
            pass
        if debug:
            P.barrier()
            dma(dbg, mixT.rearrange("p k n -> p (k n)"), r=["mixT"], grp="ddbg")
        P.barrier()
        P.run(nc, st)
    return nc


def _prep(inputs):
    x = np.asarray(inputs["x"], np.float32)
    mem = np.asarray(inputs["mem"], np.float32)
    pos = np.asarray(inputs["positions"], np.int32)
    g = lambda k: np.asarray(inputs[k], np.float32)[0]
    perm = np.concatenate([np.arange(384, 640), np.arange(640, 704), np.arange(1216, 1728), np.arange(1728, 2240),
                           np.arange(0, 384), np.arange(704, 1216), np.arange(2240, 2752), np.arange(2752, 3264)])
    w_in = np.ascontiguousarray(g("w_in")[:, perm])
    colmaj = lambda v, k: np.ascontiguousarray(v.reshape(k, 128).T)
    rep = lambda v: np.broadcast_to(v[None, :], (128, v.shape[0]))
    vec = np.zeros((128, NV), np.float32)
    vec[:, V_GMIX:V_GMIX + 8] = colmaj(g("norm_mix"), 8)
    vec[:, V_GMEM:V_GMEM + 8] = colmaj(g("norm_mem"), 8)
    vec[:, V_GFFN:V_GFFN + 8] = colmaj(g("norm_ffn"), 8)
    vec[:, V_GQA:V_GQA + 3] = colmaj(g("q_a_norm"), 3)
    vec[:, V_GKVA:V_GKVA + 2] = colmaj(g("kv_a_norm"), 2)
    vec[:, V_GOUT:V_GOUT + 4] = colmaj(g("mla_out_norm"), 4)
    vec[:, V_GOUT + 8:V_GOUT + 12] = colmaj(g("mem_out_norm"), 4)
    vec[:, V_GHGO:V_GHGO + 4] = colmaj(g("hg_out_norm"), 4)
    lbl = np.asarray(inputs["hg_lb_logits"], np.float32)
    vec[:, V_LBL:V_LBL + 4] = colmaj(lbl[0], 4)
    vec[:, V_LBL + 4:V_LBL + 8] = colmaj(lbl[1], 4)
    vec[:, V_GQ:V_GQ + 192] = rep(g("mla_q_norm"))
    vec[:, V_GK:V_GK + 192] = rep(g("mla_k_norm"))
    vec[:, V_GMQ:V_GMQ + 128] = rep(g("mem_q_norm"))
    vec[:, V_GMK:V_GMK + 128] = rep(g("mem_k_norm"))
    rm = np.ones(512, np.float32)
    rm[::64] = 0.0
    vec[:, V_RMASK:V_RMASK + 512] = rm[None, :]
    half = 32
    invf = (10000.0 ** (-np.arange(half, dtype=np.float64) / half)) / (2.0 * np.pi)
    vec[:, V_INVF:V_INVF + 32] = invf.astype(np.float32)[None, :]
    cb = np.zeros((128, 512), np.float32)
    cb[:, 0:128] = np.eye(128)
    cb[:, 128:256] = 1.0
    kk = np.arange(128)
    cb[:, 256:384] = (kk[None, :] >= kk[:, None])
    cb[:, 384:512] = (kk[None, :] >= kk[:, None]) & ((kk[None, :] // 64) == (kk[:, None] // 64))
    cbf = cb.astype(ml_dtypes.bfloat16)
    shared = dict(w_in=w_in, w_uq=g("w_uq"), w_ukv=g("w_ukv"), w_mkv=g("w_mem_kv"), w_out=g("w_out"),
                  w_gate=g("w_gate"), w_up=g("w_up"), w_down=g("w_down"), cbf=cbf)
    maps = []
    for c in range(8):
        b, j = c // 4, c % 4
        n = OWN * (j + 1)
        xsl = np.zeros((NS, 1024), np.float32)
        xsl[NS - n:] = x[b, :n]
        ps_ = np.zeros((NS,), np.int32)
        ps_[NS - n:] = pos[b, :n]
        v = vec.copy()
        for r_ in range(3):
            v[:, V_VFLAG + r_] = 1.0 if (r_ + 1) * OWN > NS - n else 0.0
        m = dict(shared)
        m.update(xs=xsl, pos=np.ascontiguousarray(ps_.reshape(NT, 128).T), mem=np.ascontiguousarray(mem[b]), vec=v)
        maps.append(m)
    return maps


_NC = {}


def kernel(**inputs):
    maps = _prep(inputs)
    if "nc" not in _NC:
        _NC["nc"] = build(False)
    res = run_bass_kernel_spmd(_NC["nc"], maps, core_ids=list(range(8)))
    out = np.zeros((2, 8192, 1024), np.float32)
    for c in range(8):
        b, j = c // 4, c % 4
        out[b, j * OWN:(j + 1) * OWN] = res.results[c]["y"]
    return out
```

```python
import contextlib
import math
import numpy as np
import ml_dtypes
import concourse.bass as bass
import concourse.mybir as mybir
from concourse.bass_utils import run_bass_kernel_spmd

F32 = mybir.dt.float32
BF16 = mybir.dt.bfloat16
I32 = mybir.dt.int32
AF = mybir.ActivationFunctionType
ALU = mybir.AluOpType
AX = mybir.AxisListType

EPS = 1e-6
NS = 8192
NT = NS // 128
NG = NS // 512
OWN = 2048
OWN0 = NS - OWN
OT0 = OWN0 // 128
OG0 = OWN0 // 512
DFF = 2816
NF = DFF // 128

C_KV, C_KR, C_HF, C_HI, C_CQ, C_HQ, C_HG, C_MQ = 0, 256, 320, 832, 1344, 1728, 2240, 2752

V_GMIX, V_GMEM, V_GFFN, V_GQA, V_GKVA, V_GOUT, V_GHGO, V_LBL, V_VFLAG = 0, 8, 16, 24, 27, 29, 41, 45, 53
V_GQ, V_GK, V_GMQ, V_GMK, V_RMASK, V_INVF = 57, 249, 441, 569, 697, 1209
NV = 1241


class Op:
    __slots__ = ("eng", "fn", "waits", "signal", "ticket", "chan", "cidx", "embed")


class Prog:
    ENG = ["pe", "act", "dve", "pool", "sp"]

    def __init__(self):
        self.ops = {e: [] for e in self.ENG}
        self.buf = {}
        self.waited = {e: {} for e in self.ENG}
        self.chan_ops = {}

    def add(self, eng, fn, reads=(), writes=(), dma=None, embed=True):
        op = Op()
        op.eng = eng
        op.fn = fn
        op.embed = embed and dma is None
        op.signal = dma is not None
        op.ticket = None
        op.chan = dma if dma is not None else eng
        lst = self.chan_ops.setdefault(op.chan, [])
        op.cidx = len(lst)
        lst.append(op)
        deps = {}
        for k in reads:
            b = self.buf.setdefault(k, [None, []])
            d = b[0]
            if d is not None and (d.chan not in deps or deps[d.chan].cidx < d.cidx):
                deps[d.chan] = d
            if k.startswith("ps"):
                for d in b[1]:
                    if d.chan != op.chan and (d.chan not in deps or deps[d.chan].cidx < d.cidx):
                        deps[d.chan] = d
        for k in writes:
            b = self.buf.setdefault(k, [None, []])
            for d in ([b[0]] if b[0] is not None else []) + b[1]:
                if d.chan == op.chan and dma is None:
                    continue
                if d.chan not in deps or deps[d.chan].cidx < d.cidx:
                    deps[d.chan] = d
        op.waits = []
        w = self.waited[eng]
        for chan, d in deps.items():
            if chan == "pe" and eng == "pe":
                continue
            if w.get(chan, -1) >= d.cidx:
                continue
            w[chan] = d.cidx
            d.signal = True
            op.waits.append(d)
        for k in reads:
            self.buf[k][1].append(op)
        for k in writes:
            self.buf[k][0] = op
            self.buf[k][1] = []
        self.ops[eng].append(op)
        return op

    def barrier(self):
        chans = {c: l[-1] for c, l in self.chan_ops.items() if l}
        for e in self.ENG:
            w = self.waited[e]
            for chan, d in chans.items():
                if chan == e or w.get(chan, -1) >= d.cidx:
                    continue
                w[chan] = d.cidx
                d.signal = True
                op = Op()
                op.eng, op.fn, op.signal, op.ticket, op.chan, op.cidx = e, None, False, None, None, -1
                op.embed = False
                op.waits = [d]
                self.ops[e].append(op)

    def run(self, nc, stack):
        sems = {}
        for chan, lst in self.chan_ops.items():
            sems[chan] = stack.enter_context(nc.semaphore("s_" + chan))
            isdma = chan not in self.ENG
            cnt = 0
            for op in lst:
                if isdma:
                    cnt += 16
                    op.ticket = cnt
                elif op.signal:
                    cnt += 1
                    op.ticket = cnt
        block = stack.enter_context(nc.Block())

        def replay(eng, e):
            for op in self.ops[eng]:
                waits = list(op.waits)
                emb = waits.pop() if (op.fn is not None and op.embed and waits) else None
                for d in waits:
                    e.wait_ge(sems[d.chan], d.ticket)
                if op.fn is None:
                    continue
                ins = op.fn(e)
                if emb is not None:
                    ins._wait_ge(sems[emb.chan], emb.ticket)
                if op.signal:
                    ins.then_inc(sems[op.chan], 16 if op.chan not in self.ENG else 1)

        @block.tensor
        def _(e):
            replay("pe", e)

        @block.scalar
        def _(e):
            replay("act", e)

        @block.vector
        def _(e):
            replay("dve", e)

        @block.gpsimd
        def _(e):
            replay("pool", e)

        @block.sync
        def _(e):
            replay("sp", e)


class Arena:
    def __init__(self, t, ncols):
        self.t = t
        self.n = ncols
        self.off = 0

    def alloc(self, cols, dtype=BF16):
        mult = 1 if dtype == BF16 else 2
        self.off = (self.off + 1) // 2 * 2
        a = self.t[:, self.off:self.off + cols * mult]
        self.off += cols * mult
        assert self.off <= self.n, (self.off, self.n)
        return a if dtype == BF16 else a.bitcast(dtype)


class _Stop(Exception):
    pass


def build(debug=False, stop=99):
    nc = bass.Bass("TRN2", target_bir_lowering=False)

    def din(name, shape, dt=F32):
        return nc.dram_tensor(name, list(shape), dt, kind="ExternalInput").ap()

    xs = din("xs", [NS, 1024])
    posd = din("pos", [128, NT], I32)
    memd = din("mem", [256, 1024])
    vecd = din("vec", [128, NV])
    cbfd = din("cbf", [128, 512], BF16)
    w_in = din("w_in", [1024, 3264])
    w_uq = din("w_uq", [384, 768])
    w_ukv = din("w_ukv", [256, 1024])
    w_mkv = din("w_mkv", [1024, 1024])
    w_out = din("w_out", [1536, 1024])
    w_gate = din("w_gate", [1024, DFF])
    w_up = din("w_up", [1024, DFF])
    w_down = din("w_down", [DFF, 1024])
    yd = nc.dram_tensor("y", [OWN, 1024], F32, kind="ExternalOutput").ap()
    dbg = nc.dram_tensor("dbg", [128, 12 * OWN], BF16, kind="ExternalOutput").ap() if debug else None
    wgu_scr = nc.dram_tensor("wgu_scr", [NF, 128, 2048], BF16, kind="Internal").ap()
    mix_scr = nc.dram_tensor("mix_scr", [128, 8 * OWN], BF16, kind="Internal").ap()

    P = Prog()
    st = contextlib.ExitStack()
    with st:
        TOT = 106400
        arena_t = st.enter_context(nc.sbuf_tensor("arena", [128, TOT], BF16))
        A = Arena(arena_t, TOT)
        ps = st.enter_context(nc.psum_tensor("ps", [128, 4096], F32))

        def bank(i, n=512, off=0):
            return ps[:, i * 512 + off:i * 512 + off + n]

        def bankb(i, n=1024, off=0):
            return ps[:, i * 512:(i + 1) * 512].bitcast(BF16)[:, off:off + n]

        vec = A.alloc(NV, F32)
        cbf = A.alloc(512)
        ident = cbf[:, 0:128]
        onesb = cbf[:, 128:256]
        tri = cbf[:, 256:384]
        tri2 = cbf[:, 384:512]
        posi = A.alloc(NT, I32)
        posf = A.alloc(NT, F32)
        lb = A.alloc(4, F32)
        oml = A.alloc(4, F32)
        noml = A.alloc(4, F32)
        vones = A.alloc(3 * 128)
        stat = A.alloc(8 * NT, F32)
        rs1 = stat[:, 0:NT]
        rskv = stat[:, NT:2 * NT]
        sskr = stat[:, 2 * NT:3 * NT]
        rskv2 = stat[:, 3 * NT:4 * NT]
        ssmla = stat[:, 4 * NT:4 * NT + 16]
        ssmem = stat[:, 4 * NT + 16:4 * NT + 32]
        rsmla = stat[:, 4 * NT + 32:4 * NT + 48]
        rsmem = stat[:, 4 * NT + 48:4 * NT + 64]
        scr = A.alloc(64, F32)
        S32 = A.alloc(512, F32)
        Sb = A.alloc(512)
        dec = A.alloc(32, F32).rearrange("p (h c) -> p h c", h=4)
        mixT = A.alloc(12 * OWN).rearrange("p (k n) -> p k n", k=12)
        base_off = A.off
        ckvT = A.alloc(2 * NS).rearrange("p (k n) -> p k n", k=2)
        kr = A.alloc(NT * 64).rearrange("p (t d) -> p t d", d=64)
        wukv = A.alloc(2 * 1024).rearrange("p (k n) -> p k n", k=2)
        p1_off = A.off

        _dq = [0]

        def dma(out, in_, r=(), w=(), grp=None, eng=None):
            if eng is None:
                eng = "sp"
            if grp is None:
                _dq[0] += 1
                grp = "dq%d" % (_dq[0] % 12)
            P.add(eng, lambda e: e.dma_start(out=out, in_=in_), reads=r, writes=w, dma=grp)

        def act(out, in_, func, r, w, scale=1.0, bias=0.0, accum=None):
            if accum is None:
                P.add("act", lambda e: e.activation(out=out, in_=in_, func=func, scale=scale, bias=bias), reads=r, writes=w)
            else:
                P.add("act", lambda e: e.activation(out=out, in_=in_, func=func, scale=scale, bias=bias, accum_out=accum), reads=r, writes=w)

        def rsqrt_act(out, in_, n, r, w, bias=EPS):
            act(out, in_, AF.Ln, r, w, scale=1.0 / n, bias=bias)
            act(out, out, AF.Exp, w, w, scale=-0.5)

        def tt(eng, out, in0, in1, op, r, w):
            P.add(eng, lambda e: e.tensor_tensor(out=out, in0=in0, in1=in1, op=op), reads=r, writes=w)

        def ts(eng, out, in0, s1, s2, op0, op1, r, w):
            if s2 is None:
                P.add(eng, lambda e: e.tensor_scalar(out=out, in0=in0, scalar1=s1, scalar2=None, op0=op0), reads=r, writes=w)
            else:
                P.add(eng, lambda e: e.tensor_scalar(out=out, in0=in0, scalar1=s1, scalar2=s2, op0=op0, op1=op1), reads=r, writes=w)

        def stt(eng, out, in0, s, in1, op0, op1, r, w):
            P.add(eng, lambda e: e.scalar_tensor_tensor(out=out, in0=in0, scalar=s, in1=in1, op0=op0, op1=op1), reads=r, writes=w)

        def cp(eng, out, in_, r, w):
            if eng == "act":
                act(out, in_, AF.Copy, r, w)
            else:
                P.add(eng, lambda e: e.tensor_copy(out=out, in_=in_), reads=r, writes=w)

        def mm(out, lhsT, rhs, start, stop, r, w):
            P.add("pe", lambda e: e.matmul(out, lhsT=lhsT, rhs=rhs, start=start, stop=stop), reads=r, writes=w)

        def tp(out, in_, r, w, idn=None):
            idn = ident if idn is None else idn
            P.add("pe", lambda e: e.transpose(out=out, in_=in_, identity=idn), reads=list(r) + ["cbf"], writes=w)

        def sigmoid_(buf, src, r, key):
            act(buf, src, AF.Exp, r, [key], scale=-1.0)
            act(buf, buf, AF.Ln, [key], [key], bias=1.0)
            act(buf, buf, AF.Exp, [key], [key], scale=-1.0)

        def ck(level):
            if stop == level:
                raise _Stop()

        try:
            dma(vec, vecd, w=["vec"])
            dma(cbf, cbfd, w=["cbf"])
            dma(posi, posd, w=["posi"])
            cp("dve", posf, posi, ["posi"], ["posf"])
            lbl = vec[:, V_LBL:V_LBL + 8].rearrange("p (r h) -> p r h", r=2)
            tt("dve", lb, lbl[:, 0, :], lbl[:, 1, :], ALU.subtract, ["vec"], ["lb"])
            sigmoid_(lb, lb, ["lb"], "lb")
            ts("dve", oml, lb, -1.0, 1.0, ALU.mult, ALU.add, ["lb"], ["oml"])
            ts("dve", noml, oml, -1.0, None, ALU.mult, None, ["oml"], ["oml"])
            for r_ in range(3):
                ts("dve", vones[:, r_ * 128:(r_ + 1) * 128], onesb, vec[:, V_VFLAG + r_:V_VFLAG + r_ + 1], None, ALU.mult, None,
                   ["cbf", "vec"], ["vones"])
            P.add("dve", lambda e: e.memset(S32, 0.0), writes=["S32"])
            P.add("dve", lambda e: e.memset(Sb, 0.0), writes=["Sb"])
            P.add("dve", lambda e: e.memset(stat, 0.0), writes=["stat"])

            ck(1)
            NA = 2368
            win = A.alloc(8 * NA).rearrange("p (k n) -> p k n", k=8)
            wl_off = A.off
            stg = [A.alloc(1024, F32) for _ in range(6)]
            _wl = [0]

            def load_w(dst, src, ncols, scale, dkey):
                ns = len(stg)
                cw = stg[0].shape[1]
                for c0 in range(0, ncols, cw):
                    n = min(cw, ncols - c0)
                    i = _wl[0] % ns
                    _wl[0] += 1
                    dma(stg[i][:, 0:n], src[:, c0:c0 + n], w=["stg%d" % i], grp="dstg%d" % i, eng=("sp", "pool", "act")[i % 3])
                    if i % 2 == 0:
                        if scale is None:
                            cp("dve", dst[:, c0:c0 + n], stg[i][:, 0:n], ["stg%d" % i], [dkey])
                        else:
                            ts("dve", dst[:, c0:c0 + n], stg[i][:, 0:n], scale, None, ALU.mult, None, ["stg%d" % i, "vec"], [dkey])
                    else:
                        act(dst[:, c0:c0 + n], stg[i][:, 0:n], AF.Copy, ["stg%d" % i, "vec"], [dkey], scale=(1.0 if scale is None else scale))

            w_in_v = w_in.rearrange("(k p) n -> p k n", p=128)
            for k in range(8):
                load_w(win[:, k, 0:1344], w_in_v[:, k, 0:1344], 1344, vec[:, V_GMIX + k:V_GMIX + k + 1], "win")
                load_w(win[:, k, 1344:NA], w_in_v[:, k, C_HQ:C_HQ + 1024], 1024, vec[:, V_GMIX + k:V_GMIX + k + 1], "win")
            w_ukv_v = w_ukv.rearrange("(k p) n -> p k n", p=128)
            for k in range(2):
                load_w(wukv[:, k, :], w_ukv_v[:, k, :], 1024, vec[:, V_GKVA + k:V_GKVA + k + 1], "wukv")
            A_HQ, A_HG = 1344, 1344 + 512
            A.off = wl_off
            W1b = A.alloc(512, F32)
            W2b = A.alloc(512, F32)

            xt = [A.alloc(1024, F32) for _ in range(2)]
            junk = A.alloc(1024)
            hb = [A.alloc(1024)]
            hT = A.alloc(8 * 512).rearrange("p (k n) -> p k n", k=8)
            sqt = A.alloc(2 * 512).rearrange("p (k n) -> p k n", k=2)
            W1 = A.alloc(512, F32)
            W2 = A.alloc(512, F32)
            W3 = A.alloc(512, F32)
            W4 = A.alloc(512, F32)
            W3b = A.alloc(512, F32)
            W4b = A.alloc(512, F32)
            kT = A.alloc(4 * 512).rearrange("p (h n) -> p h n", h=4)
            qT = A.alloc(4 * 512).rearrange("p (h n) -> p h n", h=4)
            sgT = A.alloc(4 * 512).rearrange("p (h n) -> p h n", h=4)
            ktok = A.alloc(4 * 4 * 128).rearrange("p (t h d) -> p t h d", t=4, h=4)
            vtok = A.alloc(4 * 512).rearrange("p (t n) -> p t n", t=4)
            Amb = A.alloc(4 * 128).rearrange("p (h n) -> p h n", h=4)
            krg = A.alloc(4 * 64, F32).rearrange("p (t d) -> p t d", d=64)
            rp = [A.alloc(4 * 32, F32).rearrange("p (t d) -> p t d", d=32) for _ in range(4)]
            cst = A.alloc(4 * 32, F32).rearrange("p (t d) -> p t d", d=32)
            snt = A.alloc(4 * 32, F32).rearrange("p (t d) -> p t d", d=32)
            tA = A.alloc(128, F32)
            tB = A.alloc(128, F32)
            tI = A.alloc(128, I32)
            pm = [A.alloc(512)]
            rmask = vec[:, V_RMASK:V_RMASK + 512]
            invf = vec[:, V_INVF:V_INVF + 32]

            def cs_tables(g4):
                tt("dve", tA.rearrange("p (t d) -> p t d", d=32), posf[:, g4].unsqueeze(2).to_broadcast([128, 4, 32]),
                   invf.unsqueeze(1).to_broadcast([128, 4, 32]), ALU.mult, ["posf", "vec"], ["tA"])
                for (shift, dst, dkey) in ((0.0, snt, "snt"), (0.25, cst, "cst")):
                    dflat = dst.rearrange("p t d -> p (t d)")
                    ts("dve", tB, tA, shift, None, ALU.add, None, ["tA"], ["tB"])
                    cp("dve", tI, tB, ["tB"], ["tI"])
                    cp("dve", dflat, tI, ["tI"], [dkey])
                    tt("dve", tB, tB, dflat, ALU.subtract, ["tB", dkey], ["tB"])
                    ts("dve", dflat, tB, 0.5, None, ALU.is_gt, None, ["tB"], [dkey])
                    tt("dve", tB, tB, dflat, ALU.subtract, ["tB", dkey], ["tB"])
                    ts("dve", dflat, tB, -0.5, None, ALU.is_lt, None, ["tB"], [dkey])
                    tt("dve", tB, tB, dflat, ALU.add, ["tB", dkey], ["tB"])
                    act(dflat, tB, AF.Sin, ["tB"], [dkey], scale=2.0 * math.pi)

            def rope(dst_lo, dst_hi, src_lo, src_hi, cs, sn, rkeys, wkeys):
                tt("dve", rp[0], src_lo, cs, ALU.mult, rkeys, ["rp0"])
                tt("dve", rp[1], src_hi, sn, ALU.mult, rkeys, ["rp1"])
                tt("dve", dst_lo, rp[0], rp[1], ALU.subtract, ["rp0", "rp1"], wkeys)
                tt("dve", rp[2], src_hi, cs, ALU.mult, rkeys, ["rp2"])
                tt("dve", rp[3], src_lo, sn, ALU.mult, rkeys, ["rp3"])
                tt("dve", dst_hi, rp[2], rp[3], ALU.add, ["rp2", "rp3"], wkeys)

            def rms_load_transpose(src_rows, xbuf, xkey, hbuf, hkey, rcol, rkey, dstT, dkey, tsl, eng_q):
                dma(xbuf, src_rows, w=[xkey], grp="d" + xkey, eng=eng_q)
                act(junk, xbuf, AF.Square, [xkey], ["junk", rkey], accum=rcol)
                rsqrt_act(rcol, rcol, 1024, [rkey], [rkey])
                ts("dve", hbuf, xbuf, rcol, None, ALU.mult, None, [xkey, rkey], [hkey])
                pb = bankb(0)
                for k in range(8):
                    tp(pb[:, k * 128:(k + 1) * 128], hbuf[:, k * 128:(k + 1) * 128], [hkey], ["ps0"])
                cp("act", dstT[:, :, tsl], pb.rearrange("p (k n) -> p k n", k=8), ["ps0"], [dkey])

            def fproj(col0, bnk):
                for k in range(8):
                    mm(bank(bnk), win[:, k, col0:col0 + 128], hT[:, k, :], k == 0, k == 7, ["hT", "win"], ["ps%d" % bnk])

            P.barrier()
            ck(2)
            mA = [mixT[:, r_, :] for r_ in range(4)]
            mB = [mixT[:, 8 + r_, :] for r_ in range(4)]
            kT_ = [kT, mA[0].rearrange("p (h n) -> p h n", h=4)]
            qT_ = [qT, mA[1].rearrange("p (h n) -> p h n", h=4)]
            sgT_ = [sgT, mA[2].rearrange("p (h n) -> p h n", h=4)]
            ktok_ = [ktok, mA[3].rearrange("p (t h d) -> p t h d", t=4, h=4)]
            vtok_ = [vtok, mB[0].rearrange("p (t n) -> p t n", t=4)]
            hT_ = [hT, mixT[:, 9:11, :].rearrange("p a n -> p (a n)").rearrange("p (k n) -> p k n", k=8)]
            RW3 = mB[3][:, 0:1024].bitcast(F32)
            RW1 = mB[3][:, 1024:2048].bitcast(F32)
            dec_ = [dec, A.alloc(32, F32).rearrange("p (h c) -> p h c", h=4)]
            Wsets = [(W1, W2, W3, W4, "a"), (W1b, W2b, W3b, W4b, "b")]
            FM = [1, 2, 3]
            _fm = [0]

            def fm_next():
                bk = FM[_fm[0] % 3]
                _fm[0] += 1
                return bk

            def fprojp(col0, p_):
                bk = fm_next()
                for k in range(8):
                    mm(bank(bk), win[:, k, col0:col0 + 128], hT_[p_][:, k, :], k == 0, k == 7, ["hT%d" % p_, "win"], ["ps%d" % bk])
                return bk

            def front(g):
                p_ = g % 2
                for t in range(4):
                    gt = 4 * g + t
                    xb, xk = xt[gt % 2], "xt%d" % (gt % 2)
                    rcol = rs1[:, gt:gt + 1]
                    dma(xb, xs[gt * 128:(gt + 1) * 128, :], w=[xk], grp="d" + xk, eng="sp" if gt % 2 == 0 else "pool")
                    act(junk, xb, AF.Square, [xk], ["junk", "rs1"], accum=rcol)
                    rsqrt_act(rcol, rcol, 1024, ["rs1"], ["rs1"])
                    ts("dve", hb[0], xb, rcol, None, ALU.mult, None, [xk, "rs1"], ["hb0"])
                    yield
                    pb = bankb(0)
                    for k in range(8):
                        tp(pb[:, k * 128:(k + 1) * 128], hb[0][:, k * 128:(k + 1) * 128], ["hb0"], ["ps0"])
                    cp("act", hT_[p_][:, :, t * 128:(t + 1) * 128], pb.rearrange("p (k n) -> p k n", k=8), ["ps0"], ["hT%d" % p_])
                    yield

            def f_chain(h, bk, p_):
                Wa, Wb, Wc, Wd, sfx = Wsets[h % 2]
                k1, k2, k3, k4 = "W1" + sfx, "W2" + sfx, "W3" + sfx, "W4" + sfx
                pk = "ps%d" % bk
                sigmoid_(Wa, bank(bk), [pk], k1)
                act(Wb, Wa, AF.Ln, [k1, "lb", "oml"], [k2], scale=oml[:, h:h + 1], bias=lb[:, h:h + 1])
                P.add("dve", lambda e, Wc=Wc, Wb=Wb, rmask=rmask: e.tensor_tensor_scan(out=Wc, data0=rmask, data1=Wb, initial=0.0,
                                                                                       op0=ALU.mult, op1=ALU.add),
                      reads=[k2, "vec"], writes=[k3])
                ts("dve", Wa, Wa, noml[:, h:h + 1], oml[:, h:h + 1], ALU.mult, ALU.add, [k1, "oml"], [k1])
                act(Wb, Wc, AF.Exp, [k3], [k2], scale=-1.0)
                act(Wd, Wc, AF.Exp, [k3], [k4])
                cp("dve", dec_[p_][:, h, :], Wd.rearrange("p (c t) -> p c t", t=64)[:, :, 63], [k4], ["dec%d" % p_])
                tt("dve", kT_[p_][:, h, :], Wa, Wb, ALU.mult, [k1, k2], ["kT%d" % p_])

            def kT_transpose(h, p_):
                pb = bankb(0)
                for t in range(4):
                    tp(pb[:, t * 128:(t + 1) * 128], kT_[p_][:, h, t * 128:(t + 1) * 128], ["kT%d" % p_], ["ps0"])
                cp("act", ktok_[p_][:, :, h, :], pb[:, 0:512].rearrange("p (t d) -> p t d", t=4), ["ps0"], ["ktok%d" % p_])

            def qg_chain(h, bq, bg_, p_):
                Wa, Wb, Wc, Wd, sfx = Wsets[h % 2]
                k1, k4 = "W1" + sfx, "W4" + sfx
                sigmoid_(Wa, bank(bq), ["ps%d" % bq], k1)
                tt("dve", Wa, bank(bq), Wa, ALU.mult, ["ps%d" % bq, k1], [k1])
                stt("dve", qT_[p_][:, h, :], Wa, 128 ** -0.5, Wd, ALU.mult, ALU.mult, [k1, k4], ["qT%d" % p_])
                sigmoid_(Wa, bank(bg_), ["ps%d" % bg_], k1)
                tt("dve", sgT_[p_][:, h, :], bank(bg_), Wa, ALU.mult, ["ps%d" % bg_, k1], ["sgT%d" % p_])

            def proj(g, own):
                p_ = g % 2
                hk = "hT%d" % p_
                gsl = slice(g * 512, (g + 1) * 512)
                g4 = slice(4 * g, 4 * g + 4)
                if not own:
                    bf = [fprojp(C_HF + h * 128, p_) for h in range(2)]
                    f_chain(0, bf[0], p_)
                    yield
                    bf.append(fprojp(C_HF + 2 * 128, p_))
                    f_chain(1, bf[1], p_)
                    yield
                    kT_transpose(0, p_)
                    bf.append(fprojp(C_HF + 3 * 128, p_))
                    f_chain(2, bf[2], p_)
                    yield
                    kT_transpose(1, p_)
                    f_chain(3, bf[3], p_)
                    yield
                    kT_transpose(2, p_)
                    yield
                    kT_transpose(3, p_)
                    yield
                else:
                    for h in range(4):
                        bf_ = fprojp(C_HF + h * 128, p_)
                        bq = fprojp(A_HQ + h * 128, p_)
                        f_chain(h, bf_, p_)
                        yield
                        bg_ = fprojp(A_HG + h * 128, p_)
                        qg_chain(h, bq, bg_, p_)
                        yield
                        kT_transpose(h, p_)
                        yield
                for t in range(4):
                    bk = fm_next()
                    for k in range(8):
                        mm(bank(bk), hT_[p_][:, k, t * 128:(t + 1) * 128], win[:, k, C_HI:C_HI + 512], k == 0, k == 7, [hk, "win"], ["ps%d" % bk])
                    cp("dve", vtok_[p_][:, t, :], bank(bk), ["ps%d" % bk], ["vtok%d" % p_])
                    yield
                bc = [fprojp(C_KV + m * 128, p_) for m in range(2)]
                for m in range(2):
                    bk = bc[m]
                    cp("act", ckvT[:, m, gsl], bank(bk), ["ps%d" % bk], ["ckvT"])
                    act(sqt[:, m, :], bank(bk), AF.Square, ["ps%d" % bk], ["sqt"])
                yield
                for t in range(4):
                    for m in range(2):
                        mm(bank(4, 1, 256 + t), sqt[:, m, t * 128:(t + 1) * 128], onesb[:, 0:1], m == 0, m == 1, ["sqt", "cbf"], ["ps4"])
                rsqrt_act(rskv[:, g4], bank(4, 4, 256), 256, ["ps4"], ["rskv"])
                tt("dve", rskv2[:, g4], rskv[:, g4], rskv[:, g4], ALU.mult, ["rskv"], ["rskv2"])
                yield
                for t in range(4):
                    for k in range(8):
                        mm(bank(4, 64, t * 64), hT_[p_][:, k, t * 128:(t + 1) * 128], win[:, k, C_KR:C_KR + 64], k == 0, k == 7,
                           [hk, "win"], ["ps4"])
                for t in range(4):
                    act(junk[:, 0:64], bank(4, 64, t * 64), AF.Square, ["ps4"], ["junk", "sskr"], accum=sskr[:, 4 * g + t:4 * g + t + 1])
                yield
                for t in range(4):
                    tt("dve", krg[:, t, :], bank(4, 64, t * 64), vec[:, V_GK + 128:V_GK + 192], ALU.mult, ["ps4", "vec"], ["krg"])
                yield
                cs_tables(g4)
                yield
                rope(kr[:, g4, 0:32], kr[:, g4, 32:64], krg[:, :, 0:32], krg[:, :, 32:64], cst, snt, ["krg", "cst", "snt"], ["kr"])
                yield

            def rec(g, own):
                p_ = g % 2
                kTp, qTp, sgTp, ktokp, vtokp, decp = kT_[p_], qT_[p_], sgT_[p_], ktok_[p_], vtok_[p_], dec_[p_]
                kk, qk, sgk, ktk, vtk, dk = ["%s%d" % (n_, p_) for n_ in ("kT", "qT", "sgT", "ktok", "vtok", "dec")]
                osl0 = (g - OG0) * 512
                for t in range(4):
                    tsl = slice(t * 128, (t + 1) * 128)
                    if own:
                        for h in range(4):
                            mm(bank(6, 128, h * 128), kTp[:, h, tsl], qTp[:, h, tsl], True, True, [kk, qk], ["ps6"])
                        tt("dve", Amb, bank(6).rearrange("p (h n) -> p h n", h=4), tri2.unsqueeze(1).to_broadcast([128, 4, 128]),
                           ALU.mult, ["ps6", "cbf"], ["Amb"])
                    for c2 in range(2):
                        rows = slice(c2 * 64, (c2 + 1) * 64)
                        csl = slice(t * 128 + c2 * 64, t * 128 + (c2 + 1) * 64)
                        ci = t * 2 + c2
                        if own:
                            for h in range(4):
                                mm(bank(7, 64, h * 128 + c2 * 64), vtokp[rows, t, h * 128:(h + 1) * 128], Amb[rows, h, rows], True, False,
                                   [vtk, "Amb"], ["ps7"])
                                mm(bank(7, 64, h * 128 + c2 * 64), Sb[:, h * 128:(h + 1) * 128], qTp[:, h, csl], False, True,
                                   ["Sb", qk], ["ps7"])
                        for h in range(4):
                            mm(bank(5, 128, h * 128), ktokp[rows, t, h, :], vtokp[rows, t, h * 128:(h + 1) * 128], True, True,
                               [ktk, vtk], ["ps5"])
                        tt("dve", RW3, S32, bank(5), ALU.add, ["S32", "ps5"], ["RW3"])
                        tt("dve", S32.rearrange("p (h n) -> p h n", h=4), RW3.rearrange("p (h n) -> p h n", h=4),
                           decp[:, :, ci:ci + 1].to_broadcast([128, 4, 128]), ALU.mult, ["RW3", dk], ["S32"])
                        if own or (g == OG0 - 1 and ci == 7):
                            cp("act", Sb, S32, ["S32"], ["Sb"])
                        yield
                    if own:
                        act(pm[0], bank(7), AF.Square, ["ps7"], ["pm0"])
                        mm(bank(6), onesb, pm[0], True, True, ["cbf", "pm0"], ["ps6"])
                        rsqrt_act(RW1, bank(6), 128, ["ps6"], ["RW1"])
                        tt("dve", RW1, RW1, bank(7), ALU.mult, ["RW1", "ps7"], ["RW1"])
                        for h in range(4):
                            stt("dve", mixT[:, 4 + h, osl0 + t * 128:osl0 + (t + 1) * 128], RW1[:, h * 128:(h + 1) * 128],
                                vec[:, V_GHGO + h:V_GHGO + h + 1], sgTp[:, h, tsl], ALU.mult, ALU.mult, ["RW1", "vec", sgk], ["mixT"])

            def interleave(gens):
                gens = [g_ for g_ in gens if g_ is not None]
                while gens:
                    for g_ in list(gens):
                        try:
                            next(g_)
                        except StopIteration:
                            gens.remove(g_)

            interleave([front(0)])
            interleave([proj(0, False), front(1)])
            for g in range(NG):
                if g == 1:
                    ck(3)
                if g == OG0 + 1:
                    ck(35)
                interleave([rec(g, g >= OG0),
                            proj(g + 1, g + 1 >= OG0) if g + 1 < NG else None,
                            front(g + 2) if g + 2 < NG else None])

            P.barrier()
            ck(4)
            A.off = p1_off
            qTn = A.alloc(4 * OWN).rearrange("p (h n) -> p h n", h=4)
            qTr = A.alloc(2 * OWN).rearrange("p (h n) -> p h n", h=2)
            p2_off = A.off
            NB = 896
            win = A.alloc(8 * NB).rearrange("p (k n) -> p k n", k=8)
            wuq = A.alloc(3 * 768).rearrange("p (k n) -> p k n", k=3)
            kmemT = A.alloc(4 * 256).rearrange("p (h n) -> p h n", h=4)
            vmem = A.alloc(2 * 512).rearrange("p (t n) -> p t n", t=2)
            xt = [A.alloc(1024, F32) for _ in range(2)]
            junk = A.alloc(1024)
            hb = [A.alloc(1024) for _ in range(2)]
            W1 = A.alloc(512, F32)
            W2 = A.alloc(512, F32)
            p1b_off = A.off
            stg = [A.alloc(1024, F32) for _ in range(4)]
            wmkv = A.alloc(8 * 1024).rearrange("p (k n) -> p k n", k=8)
            memT = A.alloc(8 * 256).rearrange("p (k n) -> p k n", k=8)
            kmtok = A.alloc(512).rearrange("p (h d) -> p h d", h=4)
            for k in range(8):
                load_w(win[:, k, 0:384], w_in_v[:, k, C_CQ:C_CQ + 384], 384, vec[:, V_GMIX + k:V_GMIX + k + 1], "win")
                load_w(win[:, k, 384:NB], w_in_v[:, k, C_MQ:C_MQ + 512], 512, vec[:, V_GMIX + k:V_GMIX + k + 1], "win")
            w_uq_v = w_uq.rearrange("(k p) n -> p k n", p=128)
            for k in range(3):
                load_w(wuq[:, k, :], w_uq_v[:, k, :], 768, vec[:, V_GQA + k:V_GQA + k + 1], "wuq")
            w_mkv_v = w_mkv.rearrange("(k p) n -> p k n", p=128)
            for k in range(8):
                load_w(wmkv[:, k, :], w_mkv_v[:, k, :], 1024, vec[:, V_GMEM + k:V_GMEM + k + 1], "wmkv")
            for mt in range(2):
                rms_load_transpose(memd[mt * 128:(mt + 1) * 128, :], xt[mt], "xt%d" % mt, hb[mt], "hb%d" % mt,
                                   scr[:, mt:mt + 1], "scr%d" % mt, memT, "memT", slice(mt * 128, (mt + 1) * 128), "sp" if mt == 0 else "pool")
            for mt in range(2):
                for half in range(2):
                    for k in range(8):
                        mm(bank(1 + half), memT[:, k, mt * 128:(mt + 1) * 128], wmkv[:, k, half * 512:(half + 1) * 512],
                           k == 0, k == 7, ["memT", "wmkv"], ["ps%d" % (1 + half)])
                kps = bank(1).rearrange("p (h d) -> p h d", h=4)
                act(W1, bank(1), AF.Square, ["ps1"], ["W1"])
                P.add("dve", lambda e: e.tensor_reduce(out=scr[:, 8:12], in_=W1.rearrange("p (h d) -> p h d", h=4), axis=AX.X, op=ALU.add),
                      reads=["W1"], writes=["scr8"])
                rsqrt_act(scr[:, 8:12], scr[:, 8:12], 128, ["scr8"], ["scr8"])
                W1v = W1.rearrange("p (h d) -> p h d", h=4)
                tt("dve", W1v, kps, scr[:, 8:12].unsqueeze(2).to_broadcast([128, 4, 128]), ALU.mult, ["ps1", "scr8"], ["W1"])
                tt("dve", kmtok, W1v, vec[:, V_GMK:V_GMK + 128].unsqueeze(1).to_broadcast([128, 4, 128]), ALU.mult,
                   ["W1", "vec"], ["kmtok"])
                pb = bankb(0)
                for h in range(4):
                    tp(pb[:, h * 128:(h + 1) * 128], kmtok[:, h, :], ["kmtok"], ["ps0"])
                cp("act", kmemT[:, :, mt * 128:(mt + 1) * 128], pb[:, 0:512].rearrange("p (h n) -> p h n", h=4), ["ps0"], ["kmemT"])
                cp("act", vmem[:, mt, :], bank(2), ["ps2"], ["vmem"])
            P.barrier()
            ck(5)
            A.off = p1b_off
            hT = A.alloc(8 * 512).rearrange("p (k n) -> p k n", k=8)
            sqt = A.alloc(3 * 512).rearrange("p (k n) -> p k n", k=3)
            cqT = A.alloc(3 * 512).rearrange("p (k n) -> p k n", k=3)
            rp = [A.alloc(4 * 32, F32).rearrange("p (t d) -> p t d", d=32) for _ in range(4)]
            cst = A.alloc(4 * 32, F32).rearrange("p (t d) -> p t d", d=32)
            snt = A.alloc(4 * 32, F32).rearrange("p (t d) -> p t d", d=32)
            tA = A.alloc(128, F32)
            tB = A.alloc(128, F32)
            tI = A.alloc(128, I32)
            qn = A.alloc(768, F32).rearrange("p (h d) -> p h d", h=4)
            qbn = A.alloc(4 * 128).rearrange("p (h d) -> p h d", h=4)
            qbr = A.alloc(4 * 64).rearrange("p (h d) -> p h d", h=4)
            mqT = A.alloc(4 * 512).rearrange("p (h n) -> p h n", h=4)
            pm = [A.alloc(512) for _ in range(2)]
            sqy = A.alloc(4 * 512).rearrange("p (h n) -> p h n", h=4)
            qn_ = [qn, A.alloc(768, F32).rearrange("p (h d) -> p h d", h=4)]
            qbn_ = [qbn, A.alloc(4 * 128).rearrange("p (h d) -> p h d", h=4)]
            qbr_ = [qbr, A.alloc(4 * 64).rearrange("p (h d) -> p h d", h=4)]
            mqb = A.alloc(4 * 128).rearrange("p (h d) -> p h d", h=4)

            def fb(g):
                for t in range(4):
                    gt = 4 * g + t
                    rms_load_transpose(xs[gt * 128:(gt + 1) * 128, :], xt[gt % 2], "xt%d" % (gt % 2), hb[gt % 2], "hb%d" % (gt % 2),
                                       scr[:, 2 + t:3 + t], "scr%d" % (2 + t), hT, "hT", slice(t * 128, (t + 1) * 128), "sp" if gt % 2 == 0 else "pool")
                    yield

            def qstream(g):
                g4 = slice(4 * g, 4 * g + 4)
                cs_tables(g4)
                yield
                for m in range(3):
                    b_ = 1 + (m % 2)
                    fproj(m * 128, b_)
                    cp("act", cqT[:, m, :], bank(b_), ["ps%d" % b_], ["cqT"])
                    act(sqt[:, m, :], bank(b_), AF.Square, ["ps%d" % b_], ["sqt"])
                yield
                for t in range(4):
                    for m in range(3):
                        mm(bank(4, 1, t), sqt[:, m, t * 128:(t + 1) * 128], onesb[:, 0:1], m == 0, m == 2, ["sqt", "cbf"], ["ps4"])
                ts("dve", scr[:, 16:20], bank(4, 4), EPS / 384.0, EPS * EPS, ALU.mult, ALU.add, ["ps4"], ["scr16"])
                yield
                for t in range(4):
                    q_ = t % 2
                    gt = 4 * g + t
                    tsl = slice(t * 128, (t + 1) * 128)
                    qb0 = 6 if q_ == 0 else 2
                    pk = ["ps%d" % qb0, "ps%d" % (qb0 + 1)]
                    qnq, qbnq, qbrq = qn_[q_], qbn_[q_], qbr_[q_]
                    nk, bnk, brk = "qn%d" % q_, "qbn%d" % q_, "qbr%d" % q_
                    sc0 = 20 + 4 * q_
                    sck = "scr%d" % sc0
                    scq = scr[:, sc0:sc0 + 4]
                    for (c0, n, off) in ((0, 512, 0), (512, 256, 512)):
                        for m in range(3):
                            mm(ps[:, qb0 * 512 + off:qb0 * 512 + off + n], cqT[:, m, tsl], wuq[:, m, c0:c0 + n], m == 0, m == 2,
                               ["cqT", "wuq"], [pk[off // 512]])
                    qps = ps[:, qb0 * 512:qb0 * 512 + 768]
                    qps3 = qps.rearrange("p (h d) -> p h d", h=4)
                    act(qnq.rearrange("p h d -> p (h d)"), qps, AF.Square, pk, [nk])
                    P.add("dve", lambda e, scq=scq, qnq=qnq: e.tensor_reduce(out=scq, in_=qnq, axis=AX.X, op=ALU.add), reads=[nk], writes=[sck])
                    ts("dve", scq, scq, 1.0 / 192, scr[:, 16 + t:17 + t], ALU.mult, ALU.add, [sck, "scr16"], [sck])
                    yield
                    act(scq, scq, AF.Ln, [sck], [sck])
                    act(scq, scq, AF.Exp, [sck], [sck], scale=-0.5)
                    tt("dve", qnq, qps3, scq.unsqueeze(2).to_broadcast([128, 4, 192]), ALU.mult, pk + [sck], [nk])
                    tt("dve", qnq, qnq, vec[:, V_GQ:V_GQ + 192].unsqueeze(1).to_broadcast([128, 4, 192]), ALU.mult, [nk, "vec"], [nk])
                    yield
                    cp("act", qbnq, qnq[:, :, 0:128], [nk], [bnk])
                    cs = cst[:, t, :].unsqueeze(1).to_broadcast([128, 4, 32])
                    sn = snt[:, t, :].unsqueeze(1).to_broadcast([128, 4, 32])
                    rope(qbrq[:, :, 0:32], qbrq[:, :, 32:64], qnq[:, :, 128:160], qnq[:, :, 160:192], cs, sn, [nk, "cst", "snt"], [brk])
                    yield
                    pb = bankb(0)
                    for h in range(4):
                        tp(pb[:, h * 128:(h + 1) * 128], qbnq[:, h, :], [bnk], ["ps0"])
                    qbrf = qbrq.rearrange("p h d -> p (h d)")
                    for pr in range(2):
                        tp(pb[:, 512 + pr * 128:512 + (pr + 1) * 128], qbrf[:, pr * 128:(pr + 1) * 128], [brk], ["ps0"])
                    osl_t = slice((g - OG0) * 512 + t * 128, (g - OG0) * 512 + (t + 1) * 128)
                    cp("act", qTn[:, :, osl_t], pb[:, 0:512].rearrange("p (h n) -> p h n", h=4), ["ps0"], ["qTn"])
                    cp("act", qTr[:, :, osl_t], pb[:, 512:768].rearrange("p (h n) -> p h n", h=2), ["ps0"], ["qTr"])
                    yield

            def mstream(g):
                for t in range(4):
                    tsl = slice(t * 128, (t + 1) * 128)
                    for k in range(8):
                        mm(bank(5), hT[:, k, tsl], win[:, k, 384:NB], k == 0, k == 7, ["hT", "win"], ["ps5"])
                    act(W1, bank(5), AF.Square, ["ps5"], ["W1"])
                    P.add("dve", lambda e: e.tensor_reduce(out=scr[:, 28:32], in_=W1.rearrange("p (h d) -> p h d", h=4), axis=AX.X, op=ALU.add),
                          reads=["W1"], writes=["scr28"])
                    rsqrt_act(scr[:, 28:32], scr[:, 28:32], 128, ["scr28"], ["scr28"])
                    yield
                    W1v = W1.rearrange("p (h d) -> p h d", h=4)
                    tt("dve", W1v, bank(5).rearrange("p (h d) -> p h d", h=4), scr[:, 28:32].unsqueeze(2).to_broadcast([128, 4, 128]),
                       ALU.mult, ["ps5", "scr28"], ["W1"])
                    tt("dve", mqb, W1v, vec[:, V_GMQ:V_GMQ + 128].unsqueeze(1).to_broadcast([128, 4, 128]), ALU.mult, ["W1", "vec"], ["mqb"])
                    yield
                    pb = bankb(5)
                    for h in range(4):
                        tp(pb[:, h * 128:(h + 1) * 128], mqb[:, h, :], ["mqb"], ["ps5"])
                    cp("act", mqT[:, :, tsl], pb[:, 0:512].rearrange("p (h n) -> p h n", h=4), ["ps5"], ["mqT"])
                    yield

            def ystream(g):
                osl = slice((g - OG0) * 512, (g - OG0 + 1) * 512)
                for h in range(4):
                    for mt in range(2):
                        mm(bank(1 + mt), kmemT[:, h, mt * 128:(mt + 1) * 128], mqT[:, h, :], True, True, ["kmemT", "mqT"], ["ps%d" % (1 + mt)])
                        act(pm[mt], bank(1 + mt), AF.Exp, ["ps%d" % (1 + mt)], ["pm%d" % mt], scale=128 ** -0.5)
                    yield
                    for mt in range(2):
                        mm(bank(6), vmem[:, mt, h * 128:(h + 1) * 128], pm[mt], mt == 0, mt == 1, ["vmem", "pm%d" % mt], ["ps6"])
                    for mt in range(2):
                        mm(bank(7), onesb, pm[mt], mt == 0, mt == 1, ["cbf", "pm%d" % mt], ["ps7"])
                    yield
                    P.add("dve", lambda e: e.reciprocal(out=W2, in_=bank(7)), reads=["ps7"], writes=["W2"])
                    tt("dve", W2, bank(6), W2, ALU.mult, ["ps6", "W2"], ["W2"])
                    cp("act", mixT[:, 8 + h, osl], W2, ["W2"], ["mixT"])
                    act(sqy[:, h, :], W2, AF.Square, ["W2"], ["sqy"])
                    yield
                for t in range(4):
                    for h in range(4):
                        mm(bank(4, 1, t), sqy[:, h, t * 128:(t + 1) * 128], onesb[:, 0:1], h == 0, h == 3, ["sqy", "cbf"], ["ps4"])
                o4 = slice(4 * (g - OG0), 4 * (g - OG0) + 4)
                cp("dve", ssmem[:, o4], bank(4, 4), ["ps4"], ["ssmem"])
                yield

            interleave([fb(OG0)])
            for g in range(OG0, NG):
                interleave([qstream(g), mstream(g)])
                interleave([ystream(g), fb(g + 1) if g + 1 < NG else None])

            P.barrier()
            ck(6)
            A.off = p2_off
            junk = A.alloc(1024)
            KTn = A.alloc(NS)
            KTr = A.alloc(NS)
            vt = A.alloc(NT * 128).rearrange("p (t d) -> p t d", d=128)
            kbn2 = [A.alloc(4 * 128).rearrange("p (t d) -> p t d", t=4) for _ in range(2)]
            kbr2 = [A.alloc(4 * 128).rearrange("p (t d) -> p t d", t=4) for _ in range(2)]
            pt = [A.alloc(512) for _ in range(4)]
            R1 = A.alloc(512, F32)
            Y1 = A.alloc(512, F32)
            sq2 = A.alloc(512)
            for p_ in range(2):
                P.add("dve", lambda e, p_=p_: e.memset(kbr2[p_].rearrange("p t d -> p (t d)"), 0.0), writes=["kbr%d" % p_])
            bgS = [A.alloc(1024, F32) for _ in range(2)]
            bgC = [A.alloc(1024) for _ in range(2)]

            def bg_convert():
                cnt = 0
                for a_, wsrc in enumerate((w_gate, w_up)):
                    wv = wsrc.rearrange("(k p) n -> p k n", p=128)
                    for k in range(8):
                        for c0 in range(0, DFF, 1024):
                            n = min(1024, DFF - c0)
                            i = cnt % 2
                            cnt += 1
                            dma(bgS[i][:, 0:n], wv[:, k, c0:c0 + n], w=["bgS%d" % i], grp="dbgs%d" % i, eng="sp")
                            ts("dve", bgC[i][:, 0:n], bgS[i][:, 0:n], vec[:, V_GFFN + k:V_GFFN + k + 1], None, ALU.mult, None,
                               ["bgS%d" % i, "vec"], ["bgC%d" % i])
                            f0, f1 = c0 // 128, (c0 + n) // 128
                            cc = a_ * 1024 + k * 128
                            dma(wgu_scr[f0:f1, :, cc:cc + 128].rearrange("f p n -> p f n"),
                                bgC[i][:, 0:n].rearrange("p (f n) -> p f n", n=128), r=["bgC%d" % i], w=["wgu_scr"],
                                grp="dbgo%d" % i, eng="pool")
                            yield

            bg = bg_convert()
            SC = 192 ** -0.5
            mixhm = mixT[:, 4:12, :].rearrange("p a n -> p (a n)")
            dma(mix_scr, mixhm, r=["mixT"], w=["mix_scr"], grp="dmsp")
            P.barrier()
            KTn_ = [KTn, mixT[:, 4:8, :].rearrange("p a n -> p (a n)")]
            vt_ = [vt, mixT[:, 8:12, :].rearrange("p a n -> p (a n)").rearrange("p (t d) -> p t d", d=128)]
            ksc = A.alloc(512, F32)
            P.add("dve", lambda e: e.memset(scr[:, 48:52], -0.5), writes=["scr48"])

            def kbuild(h, alone=False):
                ro = (h % 2) * 64
                s_ = h % 2
                KTs, vts = KTn_[s_], vt_[s_]
                kn_k, vt_k, kr_k = "KTn%d" % s_, "vt%d" % s_, "KTr%d" % s_
                for g in range(NG):
                    p_ = g % 2
                    b0, bT = 1, 0
                    kscb, kkey = ksc, "ksc"
                    if alone and p_ == 1:
                        b0, bT = 3, 5
                        kscb, kkey = Y1, "Y1"
                    pk = ["ps%d" % b0, "ps%d" % (b0 + 1)]
                    g4 = slice(4 * g, 4 * g + 4)
                    for ti in range(4):
                        tl = g * 4 + ti
                        for m in range(2):
                            mm(ps[:, b0 * 512 + ti * 256:b0 * 512 + (ti + 1) * 256], ckvT[:, m, tl * 128:(tl + 1) * 128],
                               wukv[:, m, h * 256:(h + 1) * 256], m == 0, m == 1, ["ckvT", "wukv"], [pk[ti // 2]])
                    kvv = ps[:, b0 * 512:b0 * 512 + 1024].rearrange("p (t c) -> p t c", c=256)
                    ksc3 = kscb.rearrange("p (t d) -> p t d", d=128)
                    c_ = 32 + p_ * 8
                    ssq_, c1_ = scr[:, c_:c_ + 4], scr[:, c_ + 4:c_ + 8]
                    sk = "scr%d" % c_
                    tt("dve", vts[:, g4, :], kvv[:, :, 128:256], rskv[:, g4].unsqueeze(2).to_broadcast([128, 4, 128]), ALU.mult,
                       pk + ["rskv"], [vt_k])
                    if alone:
                        act(ksc3, kvv[:, :, 0:128], AF.Square, pk, [kkey])
                        P.add("dve", lambda e, ssq_=ssq_, ksc3=ksc3: e.tensor_reduce(out=ssq_, in_=ksc3, axis=AX.X, op=ALU.add),
                              reads=[kkey], writes=[sk])
                        tt("dve", ssq_, ssq_, rskv2[:, g4], ALU.mult, [sk, "rskv2"], [sk])
                        tt("dve", ssq_, ssq_, sskr[:, g4], ALU.add, [sk, "sskr"], [sk])
                        yield
                        rsqrt_act(ssq_, ssq_, 192, [sk], [sk])
                        tt("dve", c1_, ssq_, rskv[:, g4], ALU.mult, [sk, "rskv"], [sk])
                        tt("dve", ksc3, kvv[:, :, 0:128], c1_.unsqueeze(2).to_broadcast([128, 4, 128]), ALU.mult, pk + [sk], [kkey])
                    else:
                        sqb = junk[:, 0:512].rearrange("p (t d) -> p t d", d=128)
                        cp("dve", ksc3, kvv[:, :, 0:128], pk, [kkey])
                        tt("dve", sqb, ksc3, ksc3, ALU.mult, [kkey], ["junk"])
                        P.add("dve", lambda e, ssq_=ssq_, sqb=sqb: e.tensor_reduce(out=ssq_, in_=sqb, axis=AX.X, op=ALU.add),
                              reads=["junk"], writes=[sk])
                        tt("dve", ssq_, ssq_, rskv2[:, g4], ALU.mult, [sk, "rskv2"], [sk])
                        tt("dve", ssq_, ssq_, sskr[:, g4], ALU.add, [sk, "sskr"], [sk])
                        ts("dve", ssq_, ssq_, 1.0 / 192, EPS, ALU.mult, ALU.add, [sk], [sk])
                        yield
                        tt("pool", ssq_, ssq_, scr[:, 48:52], ALU.pow, [sk, "scr48"], [sk])
                        tt("dve", c1_, ssq_, rskv[:, g4], ALU.mult, [sk, "rskv"], [sk])
                        tt("dve", ksc3, ksc3, c1_.unsqueeze(2).to_broadcast([128, 4, 128]), ALU.mult, [kkey, sk], [kkey])
                    tt("dve", kbn2[p_], ksc3, vec[:, V_GK:V_GK + 128].unsqueeze(1).to_broadcast([128, 4, 128]), ALU.mult,
                       [kkey, "vec"], ["kbn%d" % p_])
                    tt("dve", kbr2[p_][:, :, ro:ro + 64], kr[:, g4, :], ssq_.unsqueeze(2).to_broadcast([128, 4, 64]), ALU.mult,
                       ["kr", sk], ["kbr%d" % p_])
                    yield
                    pb = bankb(bT)
                    for ti in range(4):
                        tp(pb[:, ti * 128:(ti + 1) * 128], kbn2[p_][:, ti, :], ["kbn%d" % p_], ["ps%d" % bT])
                        tp(pb[:, 512 + ti * 128:512 + (ti + 1) * 128], kbr2[p_][:, ti, :], ["kbr%d" % p_], ["ps%d" % bT])
                    ev = "act" if alone else "dve"
                    cp(ev, KTs[:, g * 512:(g + 1) * 512], pb[:, 0:512], ["ps%d" % bT], [kn_k])
                    cp(ev, KTr[ro:ro + 64, g * 512:(g + 1) * 512], pb[ro:ro + 64, 512:1024], ["ps%d" % bT], [kr_k])
                    if g % 4 == 3:
                        next(bg, None)
                    yield

            for _ in kbuild(0, alone=True):
                pass
            for h in range(4):
                ro = (h % 2) * 64
                KTn, vt = KTn_[h % 2], vt_[h % 2]
                kn_k, vt_k, kr_k = "KTn%d" % (h % 2), "vt%d" % (h % 2), "KTr%d" % (h % 2)
                kb = kbuild(h + 1) if h < 3 else iter(())
                tile_ctr = [0]
                for Q in range(4):
                    qsl0 = Q * 512
                    nkt = OT0 + 4 * Q + 4
                    def tile_geo(kt):
                        j = kt - (OT0 + 4 * Q)
                        q0 = 0 if j < 0 else 128 * j
                        return j, q0, 512 - q0

                    def emit_S(kt):
                        j, q0, n = tile_geo(kt)
                        sb_ = 3 + (kt % 2)
                        ksl = slice(kt * 128, (kt + 1) * 128)
                        qs = slice(qsl0 + q0, qsl0 + 512)
                        mm(bank(sb_, n, q0), KTn[:, ksl], qTn[:, h, qs], True, False, [kn_k, "qTn"], ["ps%d" % sb_])
                        mm(bank(sb_, n, q0), KTr[ro:ro + 64, ksl], qTr[ro:ro + 64, h // 2, qs], False, True, [kr_k, "qTr"], ["ps%d" % sb_])

                    emit_S(0)
                    for kt in range(nkt):
                        if kt % 16 == 8:
                            next(bg, None)
                        tile_ctr[0] += 1
                        if tile_ctr[0] % 4 == 2:
                            next(kb, None)
                        if kt + 1 < nkt:
                            emit_S(kt + 1)
                        j, q0, n = tile_geo(kt)
                        sb_ = 3 + (kt % 2)
                        pti = kt % 4
                        act(pt[pti][:, q0:512], bank(sb_, n, q0), AF.Exp, ["ps%d" % sb_], ["pt%d" % pti], scale=SC)
                        if j >= 0:
                            tt("dve", pt[pti][:, q0:q0 + 128], pt[pti][:, q0:q0 + 128], tri, ALU.mult, ["pt%d" % pti, "cbf"], ["pt%d" % pti])
                        reg = kt // 16
                        von = onesb if reg >= 3 else vones[:, reg * 128:(reg + 1) * 128]
                        mm(bank(5, n, q0), vt[:, kt, :], pt[pti][:, q0:512], kt == 0, kt == nkt - 1, [vt_k, "pt%d" % pti], ["ps5"])
                        mm(bank(6, n, q0), von, pt[pti][:, q0:512], kt == 0, kt == nkt - 1, ["vones", "cbf", "pt%d" % pti], ["ps6"])
                    P.add("dve", lambda e: e.reciprocal(out=R1, in_=bank(6)), reads=["ps6"], writes=["R1"])
                    tt("dve", Y1, bank(5), R1, ALU.mult, ["ps5", "R1"], ["Y1"])
                    cp("act", mixT[:, h, qsl0:qsl0 + 512], Y1, ["Y1"], ["mixT"])
                    act(sq2, Y1, AF.Square, ["Y1"], ["sq2"])
                    for t in range(4):
                        mm(bank(7, 1, t), sq2[:, t * 128:(t + 1) * 128], onesb[:, 0:1], True, True, ["sq2", "cbf"], ["ps7"])
                    o4 = slice(4 * Q, 4 * Q + 4)
                    tt("dve", ssmla[:, o4], ssmla[:, o4], bank(7, 4), ALU.add, ["ssmla", "ps7"], ["ssmla"])
                for _ in kb:
                    pass

            for _ in bg:
                pass
            P.barrier()
            ck(7)
            dma(mixhm, mix_scr, r=["mix_scr"], w=["mixT"], grp="dmsp")
            A.off = base_off
            wout = A.alloc(12 * 1024).rearrange("p (k n) -> p k n", k=12)
            wdn = A.alloc(NF * 1024).rearrange("p (k n) -> p k n", k=NF)
            p3_off = A.off
            stg4 = [A.alloc(1024, F32) for _ in range(6)]
            rsqrt_act(rsmla, ssmla, 512, ["ssmla"], ["rsmla"])
            rsqrt_act(rsmem, ssmem, 512, ["ssmem"], ["rsmem"])
            w_out_v = w_out.rearrange("(k p) n -> p k n", p=128)
            w_dn_v = w_down.rearrange("(k p) n -> p k n", p=128)
            jobs = [(wout[:, k, :], w_out_v[:, k, :], None if 4 <= k < 8 else vec[:, V_GOUT + k:V_GOUT + k + 1], "wout") for k in range(12)]
            jobs += [(wdn[:, k, :], w_dn_v[:, k, :], None, "wdn") for k in range(NF)]
            for ji, (dst, src, sc, dkey) in enumerate(jobs):
                i = ji % 6
                dma(stg4[i], src, w=["stg4_%d" % i], grp="dstg4_%d" % i, eng=("sp", "pool", "act")[ji % 3])
                if ji % 2 == 0:
                    if sc is None:
                        cp("dve", dst, stg4[i], ["stg4_%d" % i], [dkey])
                    else:
                        ts("dve", dst, stg4[i], sc, None, ALU.mult, None, ["stg4_%d" % i, "vec"], [dkey])
                else:
                    act(dst, stg4[i], AF.Copy, ["stg4_%d" % i, "vec"], [dkey], scale=(1.0 if sc is None else sc))
            P.barrier()
            A.off = p3_off
            xo = [A.alloc(1024, F32) for _ in range(4)]
            h2b = [A.alloc(1024)] * 2
            h2T = A.alloc(8 * 512).rearrange("p (k n) -> p k n", k=8)
            actT = A.alloc(NF * 512).rearrange("p (k n) -> p k n", k=NF)
            wgu = [A.alloc(2048).rearrange("p (a k n) -> p a k n", a=2, k=8) for _ in range(3)]
            G1 = A.alloc(512, F32)
            G2 = A.alloc(512, F32)
            yo = [A.alloc(512, F32) for _ in range(2)]
            junk3 = A.alloc(1024)
            ck(8)
            for Gq in range(4):
                for t in range(4):
                    ot = Gq * 4 + t
                    tok = slice(ot * 128, (ot + 1) * 128)
                    dma(xo[t], xs[OWN0 + ot * 128:OWN0 + (ot + 1) * 128, :], w=["xo%d" % t], grp="dxo%d" % t, eng="sp" if t % 2 == 0 else "pool")
                    pa = ps[:, 1 * 512:3 * 512]
                    pbk = ps[:, 3 * 512:5 * 512]
                    pc = ps[:, 5 * 512:7 * 512]
                    for (k0, pk, keys) in ((0, 1, ["ps1", "ps2"]), (4, 3, ["ps3", "ps4"]), (8, 5, ["ps5", "ps6"])):
                        for nh in range(2):
                            for k in range(4):
                                mm(bank(pk + nh), mixT[:, k0 + k, tok], wout[:, k0 + k, nh * 512:(nh + 1) * 512], k == 0, k == 3,
                                   ["mixT", "wout"], [keys[nh]])
                    stt("dve", xo[t], pa, rsmla[:, ot:ot + 1], xo[t], ALU.mult, ALU.add, ["ps1", "ps2", "rsmla", "xo%d" % t], ["xo%d" % t])
                    tt("dve", xo[t], xo[t], pbk, ALU.add, ["xo%d" % t, "ps3", "ps4"], ["xo%d" % t])
                    stt("dve", xo[t], pc, rsmem[:, ot:ot + 1], xo[t], ALU.mult, ALU.add, ["ps5", "ps6", "rsmem", "xo%d" % t], ["xo%d" % t])
                    c_ = 40 + t
                    act(junk3, xo[t], AF.Square, ["xo%d" % t], ["junk3", "scr%d" % c_], accum=scr[:, c_:c_ + 1])
                    rsqrt_act(scr[:, c_:c_ + 1], scr[:, c_:c_ + 1], 1024, ["scr%d" % c_], ["scr%d" % c_])
                    ts("dve", h2b[t % 2], xo[t], scr[:, c_:c_ + 1], None, ALU.mult, None, ["xo%d" % t, "scr%d" % c_], ["h2b"])
                    pb = bankb(0)
                    for k in range(8):
                        tp(pb[:, k * 128:(k + 1) * 128], h2b[t % 2][:, k * 128:(k + 1) * 128], ["h2b"], ["ps0"])
                    cp("act", h2T[:, :, t * 128:(t + 1) * 128], pb.rearrange("p (k n) -> p k n", k=8), ["ps0"], ["h2T"])
                for f in range(NF):
                    wi = f % 3
                    dma(wgu[wi].rearrange("p a k n -> p (a k n)"), wgu_scr[f, :, :], r=["wgu_scr"], w=["wgu%d" % wi], grp="dwgu%d" % wi,
                        eng="pool" if wi == 1 else "sp")
                    bg = 1 + 2 * (f % 2)
                    for a_ in range(2):
                        for k in range(8):
                            mm(bank(bg + a_), wgu[wi][:, a_, k, :], h2T[:, k, :], k == 0, k == 7, ["wgu%d" % wi, "h2T"], ["ps%d" % (bg + a_)])
                    Gb = G1 if f % 2 == 0 else G2
                    gk = "G%d" % (f % 2)
                    act(Gb, bank(bg), AF.Silu, ["ps%d" % bg], [gk])
                    tt("dve", actT[:, f, :], bank(bg + 1), Gb, ALU.mult, ["ps%d" % (bg + 1), gk], ["actT"])
                for t in range(4):
                    ot = Gq * 4 + t
                    for nh in range(2):
                        bi = 5 + nh
                        for f in range(NF):
                            mm(bank(bi), actT[:, f, t * 128:(t + 1) * 128], wdn[:, f, nh * 512:(nh + 1) * 512], f == 0, f == NF - 1,
                               ["actT", "wdn"], ["ps%d" % bi])
                        tt("dve", yo[nh], bank(bi), xo[t][:, nh * 512:(nh + 1) * 512], ALU.add, ["ps%d" % bi, "xo%d" % t], ["yo%d" % nh])
                        dma(yd[ot * 128:(ot + 1) * 128, nh * 512:(nh + 1) * 512], yo[nh], r=["yo%d" % nh], grp="dyo%d" % nh)
        except _Stop:
            pass
        if debug:
            P.barrier()
            dma(dbg, mixT.rearrange("p k n -> p (k n)"), r=["mixT"], grp="ddbg")
        P.barrier()
        P.run(nc, st)
    return nc


def _prep(inputs):
    x = np.asarray(inputs["x"], np.float32)
    mem = np.asarray(inputs["mem"], np.float32)
    pos = np.asarray(inputs["positions"], np.int32)
    g = lambda k: np.asarray(inputs[k], np.float32)[0]
    perm = np.concatenate([np.arange(384, 640), np.arange(640, 704), np.arange(1216, 1728), np.arange(1728, 2240),
                           np.arange(0, 384), np.arange(704, 1216), np.arange(2240, 2752), np.arange(2752, 3264)])
    w_in = np.ascontiguousarray(g("w_in")[:, perm])
    colmaj = lambda v, k: np.ascontiguousarray(v.reshape(k, 128).T)
    rep = lambda v: np.broadcast_to(v[None, :], (128, v.shape[0]))
    vec = np.zeros((128, NV), np.float32)
    vec[:, V_GMIX:V_GMIX + 8] = colmaj(g("norm_mix"), 8)
    vec[:, V_GMEM:V_GMEM + 8] = colmaj(g("norm_mem"), 8)
    vec[:, V_GFFN:V_GFFN + 8] = colmaj(g("norm_ffn"), 8)
    vec[:, V_GQA:V_GQA + 3] = colmaj(g("q_a_norm"), 3)
    vec[:, V_GKVA:V_GKVA + 2] = colmaj(g("kv_a_norm"), 2)
    vec[:, V_GOUT:V_GOUT + 4] = colmaj(g("mla_out_norm"), 4)
    vec[:, V_GOUT + 8:V_GOUT + 12] = colmaj(g("mem_out_norm"), 4)
    vec[:, V_GHGO:V_GHGO + 4] = colmaj(g("hg_out_norm"), 4)
    lbl = np.asarray(inputs["hg_lb_logits"], np.float32)
    vec[:, V_LBL:V_LBL + 4] = colmaj(lbl[0], 4)
    vec[:, V_LBL + 4:V_LBL + 8] = colmaj(lbl[1], 4)
    vec[:, V_GQ:V_GQ + 192] = rep(g("mla_q_norm"))
    vec[:, V_GK:V_GK + 192] = rep(g("mla_k_norm"))
    vec[:, V_GMQ:V_GMQ + 128] = rep(g("mem_q_norm"))
    vec[:, V_GMK:V_GMK + 128] = rep(g("mem_k_norm"))
    rm = np.ones(512, np.float32)
    rm[::64] = 0.0
    vec[:, V_RMASK:V_RMASK + 512] = rm[None, :]
    half = 32
    invf = (10000.0 ** (-np.arange(half, dtype=np.float64) / half)) / (2.0 * np.pi)
    vec[:, V_INVF:V_INVF + 32] = invf.astype(np.float32)[None, :]
    cb = np.zeros((128, 512), np.float32)
    cb[:, 0:128] = np.eye(128)
    cb[:, 128:256] = 1.0
    kk = np.arange(128)
    cb[:, 256:384] = (kk[None, :] >= kk[:, None])
    cb[:, 384:512] = (kk[None, :] >= kk[:, None]) & ((kk[None, :] // 64) == (kk[:, None] // 64))
    cbf = cb.astype(ml_dtypes.bfloat16)
    shared = dict(w_in=w_in, w_uq=g("w_uq"), w_ukv=g("w_ukv"), w_mkv=g("w_mem_kv"), w_out=g("w_out"),
                  w_gate=g("w_gate"), w_up=g("w_up"), w_down=g("w_down"), cbf=cbf)
    maps = []
    for c in range(8):
        b, j = c // 4, c % 4
        n = OWN * (j + 1)
        xsl = np.zeros((NS, 1024), np.float32)
        xsl[NS - n:] = x[b, :n]
        ps_ = np.zeros((NS,), np.int32)
        ps_[NS - n:] = pos[b, :n]
        v = vec.copy()
        for r_ in range(3):
            v[:, V_VFLAG + r_] = 1.0 if (r_ + 1) * OWN > NS - n else 0.0
        m = dict(shared)
        m.update(xs=xsl, pos=np.ascontiguousarray(ps_.reshape(NT, 128).T), mem=np.ascontiguousarray(mem[b]), vec=v)
        maps.append(m)
    return maps


_NC = {}


def kernel(**inputs):
    maps = _prep(inputs)
    if "nc" not in _NC:
        _NC["nc"] = build(False)
    res = run_bass_kernel_spmd(_NC["nc"], maps, core_ids=list(range(8)))
    out = np.zeros((2, 8192, 1024), np.float32)
    for c in range(8):
        b, j = c // 4, c % 4
        out[b, j * OWN:(j + 1) * OWN] = res.results[c]["y"]
    return out
```
